# Optimizing a Trainium2 kernel written in Bass

```python
import jax, jax.numpy as jnp
from jax import lax
import numpy as np

D_MODEL = 1024
BATCH = 32
SEQ = 2048
DEPTH = 2

CHUNK = 64
HEAD_DIM = 64
POOL_WIDTH = D_MODEL // 4
POOL_WINDOWS = (2, 4, 8, 16)
POOL_GROUPS = len(POOL_WINDOWS)
POOL_GROUP_DIM = POOL_WIDTH // POOL_GROUPS
SGU_WIDTH = D_MODEL // 4
SGU_HEADS = SGU_WIDTH // HEAD_DIM
SGU_BLOCK = 2 * CHUNK
SB_WIDTH = D_MODEL - POOL_WIDTH - SGU_WIDTH
SB_HEADS = SB_WIDTH // HEAD_DIM
ATTN_BLOCK = 2 * CHUNK
IN_WIDTHS = (POOL_WIDTH, POOL_WIDTH,
             SGU_WIDTH, SGU_WIDTH, SGU_WIDTH,
             SB_WIDTH, SB_WIDTH, SB_WIDTH, SB_WIDTH)
IN_WIDTH = sum(IN_WIDTHS)
DN_ALPHA = (2 * DEPTH) ** 0.25
DN_BETA = (8 * DEPTH) ** -0.25
LN_EPS = 1e-5

kernel_name = "hybrid_pool_sgu_stickbreak_deepnorm_adaln"


def _layer_norm(x, g, b):
    xf = x.astype(jnp.float32)
    mu = jnp.mean(xf, axis=-1, keepdims=True)
    var = jnp.mean(jnp.square(xf - mu), axis=-1, keepdims=True)
    y = (xf - mu) * lax.rsqrt(var + LN_EPS)
    return (y * g.astype(jnp.float32) + b.astype(jnp.float32)).astype(x.dtype)


def _pool_mixer(a, w, scale):
    B, S, _ = a.shape
    af = a.astype(jnp.float32)
    cs = jnp.cumsum(af, axis=1)
    pos = jnp.arange(S)
    means = []
    for g, win in enumerate(POOL_WINDOWS):
        csg = cs[..., g * POOL_GROUP_DIM:(g + 1) * POOL_GROUP_DIM]
        lag = jnp.pad(csg, ((0, 0), (win, 0), (0, 0)))[:, :S]
        cnt = jnp.minimum(pos + 1, win).astype(jnp.float32)[None, :, None]
        means.append((csg - lag) / cnt)
    mean = jnp.stack(means, axis=2)
    d = (mean - af.reshape(B, S, POOL_GROUPS, POOL_GROUP_DIM)).astype(a.dtype)
    y = jnp.einsum('bsgc,gcd->bsgd', d, w).reshape(B, S, POOL_WIDTH)
    return y * scale


def _spatial_gating(u, v, ln_g, ln_b, w_s, b_s):
    B, S, _ = u.shape
    v = _layer_norm(v, ln_g, ln_b)
    t = jnp.arange(SGU_BLOCK)
    mask = (t[None, :] // CHUNK) <= (t[:, None] // CHUNK)
    w = jnp.where(mask[None], w_s, 0.0)
    vb = v.reshape(B, S // SGU_BLOCK, SGU_BLOCK, SGU_HEADS, HEAD_DIM)
    mixed = jnp.einsum('hts,bnshd->bnthd', w, vb) + b_s.T[None, None, :, :, None]
    return u * mixed.reshape(B, S, SGU_WIDTH)


def _stick_breaking(q, k, v):
    B, S, _ = q.shape
    q = q.reshape(B, S, SB_HEADS, HEAD_DIM).transpose(0, 2, 1, 3)
    k = k.reshape(B, S, SB_HEADS, HEAD_DIM).transpose(0, 2, 1, 3)
    v = v.reshape(B, S, SB_HEADS, HEAD_DIM).transpose(0, 2, 1, 3)
    inv_sqrt_d = HEAD_DIM ** -0.5
    outs = []
    for i in range(S // ATTN_BLOCK):
        start = i * ATTN_BLOCK
        end = start + ATTN_BLOCK
        qb = q[:, :, start:end]
        kb = k[:, :, :end]
        vb = v[:, :, :end]
        z = jnp.einsum('bhtd,bhsd->bhts', qb, kb).astype(jnp.float32) * inv_sqrt_d
        tpos = start + jnp.arange(ATTN_BLOCK)
        spos = jnp.arange(end)
        mask = spos[None, :] < tpos[:, None]
        log_beta = jax.nn.log_sigmoid(z)
        log_1m_beta = jnp.where(mask, log_beta - z, 0.0)
        later = lax.cumsum(log_1m_beta, axis=3, reverse=True) - log_1m_beta
        a = jnp.where(mask, jnp.exp(log_beta + later), 0.0)
        outs.append(jnp.einsum('bhts,bhsd->bhtd', a.astype(v.dtype), vb))
    o = jnp.concatenate(outs, axis=2)
    return o.transpose(0, 2, 1, 3).reshape(B, S, SB_WIDTH)


def setup_inputs(seed: int = 0) -> dict:
    key = jax.random.key(seed)
    ks = jax.random.split(key, 16)
    f32 = jnp.float32
    nrm = lambda k, shape, s: jax.random.normal(k, shape, f32) * s
    x = jax.random.normal(ks[0], (BATCH, SEQ, D_MODEL), f32)
    c = jax.random.normal(ks[1], (BATCH, D_MODEL), f32)
    w_in = nrm(ks[2], (DEPTH, D_MODEL, IN_WIDTH), D_MODEL ** -0.5)
    pool_w = nrm(ks[3], (DEPTH, POOL_GROUPS, POOL_GROUP_DIM, POOL_GROUP_DIM), POOL_GROUP_DIM ** -0.5)
    pool_scale = 1.0 + nrm(ks[4], (DEPTH, POOL_WIDTH), 0.05)
    sgu_ln_g = 1.0 + nrm(ks[5], (DEPTH, SGU_WIDTH), 0.02)
    sgu_ln_b = nrm(ks[6], (DEPTH, SGU_WIDTH), 0.02)
    sgu_w = nrm(ks[7], (DEPTH, SGU_HEADS, SGU_BLOCK, SGU_BLOCK), SGU_BLOCK ** -0.5)
    sgu_b = 1.0 + nrm(ks[8], (DEPTH, SGU_HEADS, SGU_BLOCK), 0.02)
    w_out = nrm(ks[9], (DEPTH, D_MODEL, D_MODEL), DN_BETA * D_MODEL ** -0.5)
    ada_w = nrm(ks[10], (DEPTH, D_MODEL, 3 * D_MODEL), D_MODEL ** -0.5)
    ada_b = nrm(ks[11], (DEPTH, 3 * D_MODEL), 0.02)
    ln_g = 1.0 + nrm(ks[12], (DEPTH, D_MODEL), 0.02)
    ln_b = nrm(ks[13], (DEPTH, D_MODEL), 0.02)
    return {"x": x, "c": c, "w_in": w_in, "pool_w": pool_w, "pool_scale": pool_scale,
            "sgu_ln_g": sgu_ln_g, "sgu_ln_b": sgu_ln_b, "sgu_w": sgu_w, "sgu_b": sgu_b,
            "w_out": w_out, "ada_w": ada_w, "ada_b": ada_b, "ln_g": ln_g, "ln_b": ln_b}


def reference(x, c, w_in, pool_w, pool_scale, sgu_ln_g, sgu_ln_b, sgu_w, sgu_b,
              w_out, ada_w, ada_b, ln_g, ln_b):
    splits = list(np.cumsum(IN_WIDTHS)[:-1])
    for l in range(DEPTH):
        mod = jax.nn.silu(c) @ ada_w[l] + ada_b[l]
        shift, scale, gate = jnp.split(mod, 3, axis=-1)
        h = x * (1.0 + scale[:, None, :]) + shift[:, None, :]
        p = h @ w_in[l]
        a, g_a, u, v_sg, g_b, q, k, v_sb, g_c = jnp.split(p, splits, axis=-1)
        y_a = _pool_mixer(a, pool_w[l], pool_scale[l]) * jax.nn.silu(g_a)
        y_b = _spatial_gating(u, v_sg, sgu_ln_g[l], sgu_ln_b[l], sgu_w[l], sgu_b[l]) * jax.nn.silu(g_b)
        y_c = _stick_breaking(q, k, v_sb) * jax.nn.silu(g_c)
        y = jnp.concatenate([y_a, y_b, y_c], axis=-1) @ w_out[l]
        x = _layer_norm(DN_ALPHA * x + gate[:, None, :] * y, ln_g[l], ln_b[l])
    return x
```

```python
import contextlib
import numpy as np
import ml_dtypes
import concourse.bass as bass
import concourse.mybir as mybir
from concourse.bass_utils import run_bass_kernel_spmd

F32 = mybir.dt.float32
BF16 = mybir.dt.bfloat16
AF = mybir.ActivationFunctionType
ALU = mybir.AluOpType

NCORES = 8
NSEQ = 4
S = 2048
D = 1024
NT = 16
KC = 8
DEPTH = 2
LN_EPS = 1e-5
DN_ALPHA = (2 * DEPTH) ** 0.25
WINS = (2, 4, 8, 16)
PAD = 16

ARENA_BYTES = 64 * 1024


class Op:
    __slots__ = ("eng", "fn", "deps", "kind", "signal", "seq", "sem", "val", "idx")


class Prog:
    def __init__(self):
        self.ops = []
        self.res_w = {}
        self.res_r = {}
        self.alias = {}
        self.by_space = {}

    def set_alias(self, name, space, ranges):
        if name in self.alias:
            assert self.alias[name] == (space, ranges), name
            return
        self.alias[name] = (space, ranges)
        self.by_space.setdefault(space, []).append(name)

    def _overlaps(self, name):
        a = self.alias.get(name)
        if a is None:
            return ()
        space, rs = a
        out = []
        for m in self.by_space[space]:
            if m == name:
                continue
            for (lo, hi) in self.alias[m][1]:
                if any(lo < h2 and l2 < hi for (l2, h2) in rs):
                    out.append(m)
                    break
        return out

    def _add(self, eng, fn, reads, writes, kind):
        op = Op()
        op.eng, op.fn, op.kind = eng, fn, kind
        op.signal = False
        op.seq = op.sem = op.val = None
        op.idx = len(self.ops)
        deps = set()
        for r in reads:
            w = self.res_w.get(r)
            if w is not None:
                deps.add(w)
        for r in writes:
            w = self.res_w.get(r)
            if w is not None:
                deps.add(w)
            for rd in self.res_r.get(r, ()):
                deps.add(rd)
        for r in reads:
            self.res_r.setdefault(r, []).append(op)
        for r in writes:
            self.res_w[r] = op
            self.res_r[r] = []
        for r in list(reads) + list(writes):
            for m in self._overlaps(r):
                w = self.res_w.get(m)
                if w is not None:
                    deps.add(w)
                for rd in self.res_r.get(m, ()):
                    deps.add(rd)
        deps.discard(op)
        op.deps = deps
        self.ops.append(op)
        return op

    def add(self, eng, fn, reads=(), writes=()):
        return self._add(eng, fn, reads, writes, "c")

    def dma(self, queue, fn, reads=(), writes=()):
        return self._add(queue, fn, reads, writes, "d")

    def finalize(self):
        for op in self.ops:
            nd = set()
            for d in op.deps:
                if d.kind == "c" and op.kind == "c" and d.eng == "pe" and op.eng == "pe":
                    continue
                nd.add(d)
            op.deps = nd
            for d in nd:
                d.signal = True


def emit(nc, prog, block, sems, dma_sems):
    prog.finalize()
    seqc = {e: 0 for e in sems}
    dcount = {q: 0 for q in dma_sems}
    dprev = {}
    for op in prog.ops:
        if op.kind == "c":
            if op.signal:
                seqc[op.eng] += 1
                op.seq = seqc[op.eng]
        else:
            ring = dma_sems[op.eng]
            i = dcount[op.eng]
            dcount[op.eng] += 1
            op.sem = ring[i % len(ring)]
            op.val = 16 * (i // len(ring) + 1)
    lists = {e: [] for e in sems}
    for op in prog.ops:
        lists[op.eng].append(op)

    def body(engname):
        def f(e):
            waited = {}

            def wait(sem, val):
                k = id(sem)
                if waited.get(k, 0) >= val:
                    return
                waited[k] = val
                e.wait_ge(sem, val)

            for op in lists[engname]:
                for d in sorted(op.deps, key=lambda o: o.idx):
                    if d.kind == "c":
                        wait(sems[d.eng], d.seq)
                    else:
                        wait(d.sem, d.val)
                if op.kind == "d":
                    if op.val > 16:
                        wait(op.sem, op.val - 16)
                    inst = op.fn(e)
                    inst.then_inc(op.sem, 16)
                else:
                    if op.fn is None:
                        continue
                    inst = op.fn(e)
                    if op.signal:
                        inst.then_inc(sems[op.eng], 1)
        return f

    block.sync(body("sp"))
    block.gpsimd(body("pool"))
    block.tensor(body("pe"))
    block.vector(body("dve"))
    block.scalar(body("act"))


def build_nc(nlayers=DEPTH, layer0=0, nseq=NSEQ):
    nc = bass.Bass("TRN2", target_bir_lowering=False)
    dr = lambda name, shape, dt=F32: nc.dram_tensor(name, list(shape), dt, kind="ExternalInput")
    x_d = dr("x", [nseq, S, D])
    cT_d = dr("cT", [128, KC * 4])
    win_d = dr("w_in_b", [DEPTH, 26, 128, 1024])
    wout_d = dr("w_out_b", [DEPTH, 128, 8192])
    ada_d = dr("ada_w_b", [DEPTH, 6, 128, 4096])
    adab_d = dr("ada_b", [DEPTH, 3072])
    pw_d = dr("pool_w_b", [DEPTH, 128, 128])
    psc_d = dr("pool_sc", [DEPTH, 128, 2])
    sgg_d = dr("sgu_g", [DEPTH, 256])
    sgb_d = dr("sgu_bln", [DEPTH, 256])
    swt_d = dr("sgu_wT", [DEPTH, 128, 512])
    sbias_d = dr("sgu_bias", [DEPTH, 4, 128])
    lng_d = dr("ln_g", [DEPTH, 1024])
    lnb_d = dr("ln_b", [DEPTH, 1024])
    ident_d = dr("ident", [128, 128])
    cb_d = dr("cst_bf", [128, 640], BF16)
    invc_d = dr("invc", [128, 2 * PAD])
    invw_d = dr("invw", [128, 2])
    out_d = nc.dram_tensor("out", [nseq, S, D], F32, kind="ExternalOutput")
    modd = nc.dram_tensor("modd", [DEPTH, 4, 3072], F32, kind="Internal")

    def dap(t, offset, pat):
        return bass.AP(t, offset, [list(p) for p in pat])

    P = Prog()

    with contextlib.ExitStack() as es:
        sb = lambda name, shape, dt: es.enter_context(nc.sbuf_tensor(name, shape, dt))
        XS = sb("XS", [128, NT * D], F32)
        HT = sb("HT", [128, KC * S], BF16)
        YT = sb("YT", [128, KC * S], BF16)
        ARb = sb("arena", [128, ARENA_BYTES // 2], BF16)
        IDENT = sb("IDENT", [128, 128], F32)
        CB = sb("CB", [128, 640], BF16)
        ZER = sb("ZER", [128, 512], BF16)
        MODT = sb("MODT", [128, DEPTH * 4 * 16], F32)
        PW = sb("PW", [128, DEPTH * 128], BF16)
        PSC = sb("PSC", [128, DEPTH * 2], F32)
        SWT = sb("SWT", [128, DEPTH * 512], BF16)
        INVC = sb("INVC", [128, 2 * PAD], F32)
        INVW = sb("INVW", [128, 2], F32)
        STAT = sb("STAT", [128, 256], F32)
        PS = es.enter_context(nc.psum_tensor("PS", [128, 4096], F32))
        sem = lambda name: es.enter_context(nc.semaphore(name))
        sems = {"pe": sem("s_pe"), "act": sem("s_act"), "dve": sem("s_dve"),
                "pool": sem("s_pool"), "sp": sem("s_sp")}
        ring_sp = [sem(f"dsp{i}") for i in range(20)]
        ring_pool = [sem(f"dpl{i}") for i in range(8)]
        AR32 = ARb.bitcast(F32)

        def cv(name, off, n, dt):
            size = 4 if dt == F32 else 2
            assert off % size == 0 and off + size * n <= ARENA_BYTES, (name, off, n)
            P.set_alias(name, "arena", [(off, off + size * n)])
            if dt == F32:
                return AR32[:, off // 4: off // 4 + n]
            return ARb[:, off // 2: off // 2 + n]

        XSv = XS[:].rearrange("p (t f) -> p t f", t=NT)
        HTv = HT[:].rearrange("p (k t) -> p k t", k=KC)
        YTv = YT[:].rearrange("p (k t) -> p k t", k=KC)
        PSv = PS[:].rearrange("p (b n) -> p b n", b=8)
        UNEG = CB[:, 0:128]
        LNEG = CB[:, 128:256]
        MNEG = CB[:, 256:384]
        IDB = CB[:, 384:512]
        ONESB = CB[:, 512:640]
        for kc in range(KC):
            for g4 in range(4):
                o = (kc * S + g4 * 512) * 2
                P.set_alias(("HT", kc, g4), "HT", [(o, o + 1024)])
                P.set_alias(("YT", kc, g4), "YT", [(o, o + 1024)])
        for q in range(4):
            P.set_alias(("WOUT", q), "HT", [(q * 4096, (q + 1) * 4096)])
        P.set_alias("DT", "YT", [(7 * S * 2, 8 * S * 2)])
        for t in range(NT):
            P.set_alias(("VLN", t), "YT", [((5 + hf) * S * 2 + t * 256, (5 + hf) * S * 2 + (t + 1) * 256)
                                           for hf in range(2)])
        HT_all = [("HT", kc, g4) for kc in range(KC) for g4 in range(4)]
        WOUT_all = [("WOUT", q) for q in range(4)]
        YT_all = [("YT", chn, g4) for chn in range(8) for g4 in range(4)]

        bank_ctr = [0]

        def next_bank():
            b = bank_ctr[0] % 8
            bank_ctr[0] += 1
            return b

        WR = [cv(("W", i), i * 2048, 1024, BF16) for i in range(4)]
        A0 = 4 * 2048
        wring_ctr = [0]

        def load_wblock(l, blk):
            slot = wring_ctr[0] % 4
            wring_ctr[0] += 1
            src = dap(win_d, ((l * 26 + blk) * 128) * 1024, [[1024, 128], [1, 1024]])
            P.dma("pool", lambda e, slot=slot, src=src: e.dma_start(out=WR[slot], in_=src),
                  writes=[("W", slot)])
            return slot

        P.dma("sp", lambda e: e.dma_start(out=IDENT[:], in_=ident_d.ap()), writes=["IDENT"])
        P.dma("sp", lambda e: e.dma_start(out=CB[:], in_=cb_d.ap()), writes=["CB"])
        P.dma("sp", lambda e: e.dma_start(out=INVC[:], in_=invc_d.ap()), writes=["INVC"])
        P.dma("sp", lambda e: e.dma_start(out=INVW[:], in_=invw_d.ap()), writes=["INVW"])
        P.dma("sp", lambda e: e.dma_start(out=PSC[:].rearrange("p (l c) -> p l c", l=DEPTH),
                                          in_=psc_d.ap().rearrange("l p c -> p l c")),
              writes=["PSC"])
        P.dma("pool", lambda e: e.dma_start(out=PW[:].rearrange("p (l c) -> p l c", l=DEPTH),
                                            in_=pw_d.ap().rearrange("l p c -> p l c")),
              writes=["PW"])
        P.dma("pool", lambda e: e.dma_start(out=SWT[:].rearrange("p (l c) -> p l c", l=DEPTH),
                                            in_=swt_d.ap().rearrange("l p c -> p l c")),
              writes=["SWT"])
        P.add("dve", lambda e: e.memset(ZER[:], 0.0), writes=["ZER"])
        SWTv = SWT[:].rearrange("p (a t) -> p a t", t=128)
        P.add("dve", lambda e: e.memset(SWTv[64:128, :, 0:64], 0.0), reads=["SWT"], writes=["SWT"])

        SC = cv("SC", A0, 32, F32)
        ADAB = cv("ADAB", A0 + 128, 3072, F32)
        MR = [cv(("MR", i), A0 + 128 + 12288 + i * 2048, 512, F32) for i in range(2)]
        ADA_OFF = A0 + 128 + 12288 + 4096
        ADA = [cv(("ADA", i), ADA_OFF + i * 8192, 2048, F32) for i in range(4)]

        P.dma("sp", lambda e: e.dma_start(out=SC, in_=cT_d.ap()), writes=["SC"])
        P.add("act", lambda e: e.activation(out=SC, in_=SC, func=AF.Silu), reads=["SC"], writes=["SC"])
        adactr = 0
        for l in range(DEPTH):
            P.dma("sp", lambda e, l=l: e.dma_start(
                out=ADAB[0:4, :], in_=dap(adab_d, l * 3072, [[0, 4], [1, 3072]])),
                writes=["ADAB"])
            for cbk in range(6):
                bank = next_bank()
                for half in range(2):
                    buf = adactr % 4
                    adactr += 1
                    src = dap(ada_d, ((l * 6 + cbk) * 128) * 4096 + half * 2048,
                              [[4096, 128], [1, 2048]])
                    P.dma("sp", lambda e, buf=buf, src=src: e.dma_start(out=ADA[buf], in_=src),
                          writes=[("ADA", buf)])

                    def mm(e, buf=buf, half=half, bank=bank):
                        inst = None
                        for k4 in range(4):
                            kc = half * 4 + k4
                            inst = e.matmul(PSv[0:4, bank, :], lhsT=SC[:, kc * 4:(kc + 1) * 4],
                                            rhs=ADA[buf][:, k4 * 512:(k4 + 1) * 512],
                                            start=(kc == 0), stop=(kc == 7))
                        return inst
                    P.add("pe", mm, reads=[("ADA", buf), "SC"], writes=[("ps", bank)])
                mr = MR[cbk % 2]
                P.add("dve", lambda e, mr=mr, bank=bank, cbk=cbk: e.tensor_tensor(
                    out=mr[0:4, :], in0=PSv[0:4, bank, :], in1=ADAB[0:4, cbk * 512:(cbk + 1) * 512],
                    op=ALU.add), reads=[("ps", bank), "ADAB"], writes=[("MR", cbk % 2)])
                P.dma("sp", lambda e, mr=mr, l=l, cbk=cbk: e.dma_start(
                    out=dap(modd, l * 4 * 3072 + cbk * 512, [[3072, 4], [1, 512]]), in_=mr[0:4, :]),
                    reads=[("MR", cbk % 2)], writes=[("modd", l, cbk)])
        for l in range(DEPTH):
            for b in range(4):
                P.dma("sp", lambda e, l=l, b=b: e.dma_start(
                    out=MODT[:, (l * 4 + b) * 16:(l * 4 + b + 1) * 16],
                    in_=dap(modd, (l * 4 + b) * 3072, [[1, 128], [128, 16]]),
                    allow_slow_non_contiguous=True),
                    reads=[("modd", l, cbk) for cbk in range(4)], writes=[("MODTld", l, b)])
        MODTv = MODT[:].rearrange("p (a j) -> p a j", j=16)
        P.add("dve", lambda e: e.tensor_scalar(out=MODTv[:, :, 8:16], in0=MODTv[:, :, 8:16],
                                               scalar1=1.0, scalar2=None, op0=ALU.add),
              reads=[("MODTld", l, b) for l in range(DEPTH) for b in range(4)], writes=["MODT"])

        out_dmas = []
        evac_ctr = [0]

        for si in range(nseq):
            for tt in range(NT):
                P.dma("sp", lambda e, si=si, tt=tt: e.dma_start(
                    out=XSv[:, tt, :], in_=dap(x_d, (si * S + tt * 128) * D, [[D, 128], [1, D]])),
                    writes=[("XS", tt)])
            for li in range(nlayers):
                l = layer0 + li
                mcol = (l * 4 + si) * 16
                blk_order = [0, 2, 1, 3, 6, 7, 8, 4, 9, 5]
                for hp in range(4):
                    if hp == 0:
                        blk_order += [22 + hp, 10 + hp, 14 + hp, 18 + hp]
                    else:
                        blk_order += [10 + hp, 14 + hp, 18 + hp, 22 + hp]
                slots = {}
                nload = [0]

                def prefetch(upto, l=l, slots=slots, nload=nload, blk_order=blk_order):
                    while nload[0] < min(upto, len(blk_order)):
                        slots[nload[0]] = load_wblock(l, blk_order[nload[0]])
                        nload[0] += 1
                prefetch(2)

                for g4 in range(4):
                    for kc in range(KC):
                        bank = next_bank()

                        def tr(e, g4=g4, kc=kc, bank=bank):
                            inst = None
                            for j in range(4):
                                inst = e.transpose(out=PSv[:, bank, j * 128:(j + 1) * 128],
                                                   in_=XSv[:, g4 * 4 + j, kc * 128:(kc + 1) * 128],
                                                   identity=IDENT[:])
                            return inst
                        P.add("pe", tr, reads=[("XS", g4 * 4 + j) for j in range(4)] + ["IDENT"],
                              writes=[("ps", bank)])
                        dst = HTv[:, kc, g4 * 512:(g4 + 1) * 512]
                        sc_ap = MODT[:, mcol + 8 + kc: mcol + 9 + kc]
                        sh_ap = MODT[:, mcol + kc: mcol + kc + 1]
                        if evac_ctr[0] % 2 == 0:
                            P.add("dve", lambda e, dst=dst, bank=bank, sc_ap=sc_ap, sh_ap=sh_ap:
                                  e.tensor_scalar(out=dst, in0=PSv[:, bank, :], scalar1=sc_ap,
                                                  scalar2=sh_ap, op0=ALU.mult, op1=ALU.add),
                                  reads=[("ps", bank), "MODT"], writes=[("HT", kc, g4)])
                        else:
                            P.add("act", lambda e, dst=dst, bank=bank, sc_ap=sc_ap, sh_ap=sh_ap:
                                  e.activation(out=dst, in_=PSv[:, bank, :], func=AF.Identity,
                                               scale=sc_ap, bias=sh_ap),
                                  reads=[("ps", bank), "MODT"], writes=[("HT", kc, g4)])
                        evac_ctr[0] += 1

                def proj_fm(widx, evac, slots=slots, prefetch=prefetch):
                    slot = slots[widx]
                    prefetch(widx + 3)
                    for g4 in range(4):
                        bank = next_bank()

                        def mm(e, slot=slot, g4=g4, bank=bank):
                            inst = None
                            for kc in range(KC):
                                inst = e.matmul(PSv[:, bank, :], lhsT=WR[slot][:, kc * 128:(kc + 1) * 128],
                                                rhs=HTv[:, kc, g4 * 512:(g4 + 1) * 512],
                                                start=(kc == 0), stop=(kc == KC - 1))
                            return inst
                        P.add("pe", mm, reads=[("W", slot)] + [("HT", kc, g4) for kc in range(KC)],
                              writes=[("ps", bank)])
                        evac(bank, g4)

                def proj_tm(widx, evac, group=4, slots=slots, prefetch=prefetch):
                    slot = slots[widx]
                    prefetch(widx + 3)
                    for t0 in range(0, NT, group):
                        bank = next_bank()

                        def mm(e, slot=slot, t0=t0, bank=bank):
                            inst = None
                            for j in range(group):
                                tt = t0 + j
                                for kc in range(KC):
                                    inst = e.matmul(PSv[:, bank, j * 128:(j + 1) * 128],
                                                    lhsT=HTv[:, kc, tt * 128:(tt + 1) * 128],
                                                    rhs=WR[slot][:, kc * 128:(kc + 1) * 128],
                                                    start=(kc == 0), stop=(kc == KC - 1))
                            return inst
                        P.add("pe", mm, reads=[("W", slot)] + HT_all, writes=[("ps", bank)])
                        evac(bank, t0, group)

                W2 = S + PAD
                AT = cv("AT", A0, W2, F32)
                B1 = cv("B1", A0 + 4 * W2, W2, F32)
                B2 = cv("B2", A0 + 8 * W2, W2, F32)
                SGA = cv("SGA", A0 + 12 * W2, S, F32)
                DT = YTv[:, 7, :]
                for bufname, buf in (("AT", AT), ("B1", B1), ("B2", B2)):
                    P.add("pool", lambda e, buf=buf: e.memset(buf[:, 0:PAD], 0.0), writes=[bufname])
                widx = 0
                sh = lambda buf, k: buf[:, PAD - k: PAD - k + S]
                for ch in range(2):
                    def ev_a(bank, g4):
                        P.add("act", lambda e, bank=bank, g4=g4: e.activation(
                            out=AT[:, PAD + g4 * 512: PAD + (g4 + 1) * 512], in_=PSv[:, bank, :],
                            func=AF.Identity), reads=[("ps", bank)], writes=["AT"])
                    proj_fm(widx, ev_a)
                    widx += 1

                    def ev_g(bank, g4):
                        P.add("act", lambda e, bank=bank, g4=g4: e.activation(
                            out=SGA[:, g4 * 512:(g4 + 1) * 512], in_=PSv[:, bank, :],
                            func=AF.Silu), reads=[("ps", bank)], writes=["SGA"])
                    proj_fm(widx, ev_g)
                    widx += 1
                    P.add("pool", lambda e: e.tensor_tensor(out=B1[:, PAD:], in0=AT[:, PAD:], in1=sh(AT, 1),
                                                            op=ALU.add), reads=["AT"], writes=["B1"])
                    if ch == 0:
                        P.add("pool", lambda e: e.tensor_tensor(out=B2[64:128, PAD:], in0=B1[64:128, PAD:],
                                                                in1=sh(B1, 2)[64:128], op=ALU.add),
                              reads=["B1"], writes=["B2"])
                    else:
                        P.add("pool", lambda e: e.tensor_tensor(out=B2[:, PAD:], in0=B1[:, PAD:], in1=sh(B1, 2),
                                                                op=ALU.add), reads=["B1"], writes=["B2"])
                        P.add("pool", lambda e: e.tensor_tensor(out=B1[:, PAD:], in0=B2[:, PAD:], in1=sh(B2, 4),
                                                                op=ALU.add), reads=["B2"], writes=["B1"])
                        P.add("pool", lambda e: e.tensor_tensor(out=B2[64:128, PAD:], in0=B1[64:128, PAD:],
                                                                in1=sh(B1, 8)[64:128], op=ALU.add),
                              reads=["B1"], writes=["B2"])
                    for (p0, p1, src, sname) in ((0, 64, B1, "B1"), (64, 128, B2, "B2")):
                        P.add("dve", lambda e, p0=p0, p1=p1, src=src, ch=ch: e.scalar_tensor_tensor(
                            out=DT[p0:p1, :], in0=src[p0:p1, PAD:], scalar=INVW[p0:p1, ch:ch + 1],
                            in1=AT[p0:p1, PAD:], op0=ALU.mult, op1=ALU.subtract),
                            reads=[sname, "AT", "INVW"], writes=["DT"])
                        P.add("dve", lambda e, p0=p0, p1=p1, src=src, ch=ch: e.tensor_tensor(
                            out=src[p0:p1, PAD:2 * PAD], in0=src[p0:p1, PAD:2 * PAD],
                            in1=INVC[p0:p1, ch * PAD:(ch + 1) * PAD], op=ALU.mult),
                            reads=["DT", "INVC"], writes=[sname])
                        P.add("dve", lambda e, p0=p0, p1=p1, src=src: e.tensor_tensor(
                            out=DT[p0:p1, 0:PAD], in0=src[p0:p1, PAD:2 * PAD], in1=AT[p0:p1, PAD:2 * PAD],
                            op=ALU.subtract), reads=[sname, "AT"], writes=["DT"])
                    for g4 in range(4):
                        for (p0, p1) in ((0, 64), (64, 128)):
                            bank = next_bank()
                            P.add("pe", lambda e, p0=p0, p1=p1, g4=g4, bank=bank, ch=ch, l=l: e.matmul(
                                PSv[p0:p1, bank, :],
                                lhsT=PW[p0:p1, l * 128 + ch * 64: l * 128 + (ch + 1) * 64],
                                rhs=DT[p0:p1, g4 * 512:(g4 + 1) * 512], start=True, stop=True),
                                reads=["DT", "PW"], writes=[("ps", bank)])
                            P.add("dve", lambda e, p0=p0, p1=p1, g4=g4, bank=bank, ch=ch, l=l:
                                  e.scalar_tensor_tensor(
                                      out=YTv[p0:p1, ch, g4 * 512:(g4 + 1) * 512], in0=PSv[p0:p1, bank, :],
                                      scalar=PSC[p0:p1, l * 2 + ch: l * 2 + ch + 1],
                                      in1=SGA[p0:p1, g4 * 512:(g4 + 1) * 512], op0=ALU.mult, op1=ALU.mult),
                                  reads=[("ps", bank), "PSC", "SGA"], writes=[("YT", ch, g4)])

                VLN = [YTv[:, 5, :].rearrange("p (t f) -> p t f", t=NT),
                       YTv[:, 6, :].rearrange("p (t f) -> p t f", t=NT)]
                UT = cv("UT", A0, S, F32)
                SGB = cv("SGB", A0 + 4 * S, S, F32)
                for gi in range(4):
                    P.set_alias(("UT", gi), "arena", [(A0 + gi * 2048, A0 + (gi + 1) * 2048)])
                    P.set_alias(("SGB", gi), "arena", [(A0 + 4 * S + gi * 2048, A0 + 4 * S + (gi + 1) * 2048)])
                T1 = [cv(("T1", i), A0 + 8 * S + i * 2048, 512, F32) for i in range(2)]
                BSB = cv("BSB", A0 + 8 * S + 8192, 256, F32)
                GCOL = STAT[:, 208:210]
                BCOL = STAT[:, 210:212]
                P.dma("sp", lambda e, l=l: e.dma_start(out=GCOL, in_=dap(sgg_d, l * 256, [[1, 128], [128, 2]]),
                                                      allow_slow_non_contiguous=True), writes=["GCOL"])
                P.dma("sp", lambda e, l=l: e.dma_start(out=BCOL, in_=dap(sgb_d, l * 256, [[1, 128], [128, 2]]),
                                                      allow_slow_non_contiguous=True), writes=["BCOL"])
                for h in range(4):
                    P.dma("sp", lambda e, h=h, l=l: e.dma_start(
                        out=BSB[(h % 2) * 64:(h % 2 + 1) * 64, (h // 2) * 128:(h // 2 + 1) * 128],
                        in_=dap(sbias_d, (l * 4 + h) * 128, [[0, 64], [1, 128]])),
                        writes=["BSB"] if h == 0 else [("BSBx", h)])
                BSB_all = ["BSB"] + [("BSBx", h) for h in range(1, 4)]
                for pr_ in range(2):
                    P.set_alias(("BSB2", pr_), "arena", [(A0 + 8 * S + 8192, A0 + 8 * S + 8192 + 1024)])
                for half in range(2):
                    slot = slots[widx]
                    prefetch(widx + 3)
                    widx += 1
                    for tt in range(NT):
                        bank = tt // 2
                        c0 = (tt % 2) * 256 + half * 128

                        def mm(e, slot=slot, tt=tt, bank=bank, c0=c0):
                            inst = None
                            for kc in range(KC):
                                inst = e.matmul(PSv[:, bank, c0:c0 + 128],
                                                lhsT=HTv[:, kc, tt * 128:(tt + 1) * 128],
                                                rhs=WR[slot][:, kc * 128:(kc + 1) * 128],
                                                start=(kc == 0), stop=(kc == KC - 1))
                            return inst
                        P.add("pe", mm, reads=[("W", slot)] + HT_all, writes=[("ps", bank)])
                bank_ctr[0] = 0
                BNS = STAT[:, 0:96].rearrange("p (t s) -> p t s", t=NT)
                MV = STAT[:, 96:128].rearrange("p (t s) -> p t s", t=NT)
                RSTD = STAT[:, 128:144]
                for tt in range(NT):
                    bank = tt // 2
                    c0 = (tt % 2) * 256
                    P.add("dve", lambda e, tt=tt, bank=bank, c0=c0: e.bn_stats(
                        out=BNS[:, tt, :], in_=PSv[:, bank, c0:c0 + 256]),
                        reads=[("ps", bank)], writes=[("BNS", tt)])
                    P.add("dve", lambda e, tt=tt: e.bn_aggr(out=MV[:, tt, :], in_=BNS[:, tt, :]),
                          reads=[("BNS", tt)], writes=["MV"])
                P.add("act", lambda e: e.activation(out=RSTD, in_=MV[:, :, 1], func=AF.Ln, bias=LN_EPS),
                      reads=["MV"], writes=["RSTD"])
                P.add("act", lambda e: e.activation(out=RSTD, in_=RSTD, func=AF.Exp, scale=-0.5),
                      reads=["RSTD"], writes=["RSTD"])
                for tt in range(NT):
                    bank = tt // 2
                    c0 = (tt % 2) * 256
                    P.add("dve", lambda e, tt=tt, bank=bank, c0=c0: e.tensor_scalar(
                        out=YTv[:, 5:7, tt * 128:(tt + 1) * 128],
                        in0=PSv[:, bank, c0:c0 + 256].rearrange("p (h f) -> p h f", h=2),
                        scalar1=MV[:, tt, 0:1], scalar2=RSTD[:, tt:tt + 1], op0=ALU.subtract, op1=ALU.mult),
                        reads=[("ps", bank), "MV", "RSTD"], writes=[("VLN", tt)])
                SWTl = SWT[:, l * 512:(l + 1) * 512].rearrange("p (h t) -> p h t", h=4)
                for pr in range(2):
                    bank = next_bank()

                    def rwmm(e, bank=bank, pr=pr, SWTl=SWTl):
                        inst = None
                        for hh in range(2):
                            inst = e.matmul(PSv[hh * 64:(hh + 1) * 64, bank, 0:128], lhsT=ONESB[:, 0:64],
                                            rhs=SWTl[:, pr * 2 + hh, :], start=True, stop=True)
                        return inst
                    P.add("pe", rwmm, reads=["SWT", "CB"], writes=[("ps", bank)])
                    P.add("dve", lambda e, bank=bank, pr=pr: e.scalar_tensor_tensor(
                        out=BSB[:, pr * 128:(pr + 1) * 128], in0=PSv[:, bank, 0:128], scalar=BCOL[:, pr:pr + 1],
                        in1=BSB[:, pr * 128:(pr + 1) * 128], op0=ALU.mult, op1=ALU.add),
                        reads=[("ps", bank), "BCOL"] + BSB_all, writes=[("BSB2", pr)])

                    def ev_gb(bank, g4):
                        P.add("act", lambda e, bank=bank, g4=g4: e.activation(
                            out=SGB[:, g4 * 512:(g4 + 1) * 512], in_=PSv[:, bank, :], func=AF.Silu),
                            reads=[("ps", bank)], writes=[("SGB", g4)])
                    proj_fm(widx, ev_gb)
                    widx += 1

                    def ev_u(bank, g4):
                        P.add("dve", lambda e, bank=bank, g4=g4: e.tensor_tensor(
                            out=UT[:, g4 * 512:(g4 + 1) * 512], in0=PSv[:, bank, :],
                            in1=SGB[:, g4 * 512:(g4 + 1) * 512], op=ALU.mult),
                            reads=[("ps", bank), ("SGB", g4)], writes=[("UT", g4)])
                    proj_fm(widx, ev_u)
                    widx += 1
                    for g4 in range(4):
                        bank = next_bank()

                        def mm(e, g4=g4, bank=bank, pr=pr, SWTl=SWTl):
                            inst = None
                            for j in range(4):
                                tt = g4 * 4 + j
                                for hh in range(2):
                                    inst = e.matmul(PSv[hh * 64:(hh + 1) * 64, bank, j * 128:(j + 1) * 128],
                                                    lhsT=VLN[pr][:, tt, hh * 64:(hh + 1) * 64],
                                                    rhs=SWTl[:, pr * 2 + hh, :], start=True, stop=True)
                            return inst
                        P.add("pe", mm, reads=[("VLN", g4 * 4 + j) for j in range(4)] + ["SWT"],
                              writes=[("ps", bank)])
                        t1 = T1[g4 % 2]
                        bs_b = bass.AP(BSB.tensor, BSB.offset + pr * 128,
                                       [[int(v) for v in BSB.ap[0]], [0, 4], [1, 128]])
                        P.add("dve", lambda e, t1=t1, bank=bank, bs_b=bs_b, pr=pr: e.scalar_tensor_tensor(
                            out=t1.rearrange("p (j t) -> p j t", j=4),
                            in0=PSv[:, bank, :].rearrange("p (j t) -> p j t", j=4), scalar=GCOL[:, pr:pr + 1],
                            in1=bs_b, op0=ALU.mult, op1=ALU.add),
                            reads=[("ps", bank), ("BSB2", pr), "GCOL"], writes=[("T1", g4 % 2)])
                        P.add("pool", lambda e, t1=t1, g4=g4, pr=pr: e.tensor_tensor(
                            out=YTv[:, 2 + pr, g4 * 512:(g4 + 1) * 512], in0=t1,
                            in1=UT[:, g4 * 512:(g4 + 1) * 512], op=ALU.mult),
                            reads=[("T1", g4 % 2), ("UT", g4)], writes=[("YT", 2 + pr, g4)])

                QTs = [cv(("QTb", i), A0 + i * 12288, S, BF16) for i in range(2)]
                KTs = [cv(("KTb", i), A0 + i * 12288 + 4096, S, BF16) for i in range(2)]
                VPs = [cv(("VPb", i), A0 + i * 12288 + 8192, S, BF16).rearrange("p (t f) -> p t f", t=NT)
                       for i in range(2)]
                GTs = [cv(("GTb", 0), A0 + 24576, S, BF16), cv(("GTb", 1), A0 + 53248, S, BF16)]
                for gi in range(4):
                    P.set_alias(("GT", 0, gi), "arena", [(A0 + 24576 + gi * 1024, A0 + 24576 + (gi + 1) * 1024)])
                    P.set_alias(("GT", 1, gi), "arena", [(A0 + 53248 + gi * 1024, A0 + 53248 + (gi + 1) * 1024)])
                    for par in range(2):
                        for nm, off in (("QT", 0), ("KT", 4096), ("VP", 8192)):
                            o = A0 + par * 12288 + off + gi * 1024
                            P.set_alias((nm, par, gi), "arena", [(o, o + 1024)])
                EB = [cv(("E", i), A0 + 28672 + i * 4096, 1024, F32).rearrange("p (h n) -> p h n", h=2)
                      for i in range(3)]
                SPB = [cv(("SP", i), A0 + 40960 + i * 2048, 1024, BF16).rearrange("p (h n) -> p h n", h=2)
                       for i in range(2)]
                PB = cv("PB", A0 + 45056, 1024, F32).rearrange("p (h n) -> p h n", h=2)
                AB = [cv(("A", i), A0 + 49152 + i * 2048, 1024, BF16).rearrange("p (h n) -> p h n", h=2)
                      for i in range(2)]
                HT32 = HT.bitcast(F32)
                GBCh = HT32[:, 4096:4096 + D]
                P.set_alias("GBCh", "HT", [(16384, 16384 + 4 * D)])
                WOv = HT[:, 0:KC * D].rearrange("p (k f) -> p k f", k=KC)

                def piece_fm(slot, g4, bank, evac):
                    def run():
                        def mm(e):
                            inst = None
                            for kc in range(KC):
                                inst = e.matmul(PSv[:, bank, :], lhsT=WR[slot][:, kc * 128:(kc + 1) * 128],
                                                rhs=HTv[:, kc, g4 * 512:(g4 + 1) * 512],
                                                start=(kc == 0), stop=(kc == KC - 1))
                            return inst
                        P.add("pe", mm, reads=[("W", slot)] + [("HT", kc, g4) for kc in range(KC)],
                              writes=[("ps", bank)])
                        evac(bank, g4)
                    return run

                def piece_v(slot, g4, j, bank, par):
                    def run():
                        tt = g4 * 4 + j

                        def mm(e):
                            inst = None
                            for kc in range(KC):
                                inst = e.matmul(PSv[:, bank, j * 128:(j + 1) * 128],
                                                lhsT=HTv[:, kc, tt * 128:(tt + 1) * 128],
                                                rhs=WR[slot][:, kc * 128:(kc + 1) * 128],
                                                start=(kc == 0), stop=(kc == KC - 1))
                            return inst
                        P.add("pe", mm, reads=[("W", slot)] + [("HT", kc, g4) for kc in range(KC)],
                              writes=[("ps", bank)])
                        if j == 3:
                            P.add("dve", lambda e: e.tensor_copy(
                                out=VPs[par][:, g4 * 4:g4 * 4 + 4, :],
                                in_=PSv[:, bank, :].rearrange("p (j f) -> p j f", j=4)),
                                reads=[("ps", bank)], writes=[("VP", par, g4)])
                    return run

                def mk_ev_q(par):
                    def ev_q(bank, g4):
                        P.add("dve", lambda e: e.tensor_scalar(
                            out=QTs[par][:, g4 * 512:(g4 + 1) * 512], in0=PSv[:, bank, :], scalar1=0.125,
                            scalar2=None, op0=ALU.mult), reads=[("ps", bank)], writes=[("QT", par, g4)])
                    return ev_q

                def mk_ev_k(par):
                    def ev_k(bank, g4):
                        P.add("dve", lambda e: e.tensor_copy(
                            out=KTs[par][:, g4 * 512:(g4 + 1) * 512], in_=PSv[:, bank, :]),
                            reads=[("ps", bank)], writes=[("KT", par, g4)])
                    return ev_k

                def mk_ev_g(par):
                    def ev_g(bank, g4):
                        P.add("dve", lambda e: e.tensor_copy(
                            out=GTs[par][:, g4 * 512:(g4 + 1) * 512], in_=PSv[:, bank, :]),
                            reads=[("ps", bank)], writes=[("GT", par, g4)])
                    return ev_g

                def silu_gate(par):
                    P.add("act", lambda e: e.activation(out=GTs[par], in_=GTs[par], func=AF.Silu),
                          reads=[("GT", par, gi) for gi in range(4)], writes=[("GT", par, gi) for gi in range(4)])

                xb = [0, 1, 2, 3, 7]
                xctr = [0]

                def xbank():
                    b = xb[xctr[0] % len(xb)]
                    xctr[0] += 1
                    return b

                prefetch(widx + 4)
                for g4 in range(4):
                    piece_fm(slots[widx], g4, xbank(), mk_ev_g(0))()
                for g4 in range(4):
                    piece_fm(slots[widx + 1], g4, xbank(), mk_ev_q(0))()
                    piece_fm(slots[widx + 2], g4, xbank(), mk_ev_k(0))()
                    bv = xbank()
                    for j in range(4):
                        piece_v(slots[widx + 3], g4, j, bv, 0)()
                widx += 4
                steps = []
                for hp in range(4):
                    for c in range(4):
                        for kb in range(4 * c + 3, -1, -1):
                            r = kb - 4 * c
                            steps.append((hp, c, kb, max(r, 0) * 128, r >= 0, kb == 4 * c + 3, kb == 0))
                n = len(steps)
                NS = n // 4
                ZB = [(0, 1), (2, 3)]
                XB = (4, 5)
                OB6 = 6
                pending = []
                pstate = {"npieces": 1, "colacc": 0.0}

                def pair_begin(hp):
                    nonlocal_widx = widx_box[0]
                    par = hp % 2
                    if hp < 3:
                        prefetch(nonlocal_widx + 4)
                        npar = 1 - par
                        for g4 in range(4):
                            pending.append(piece_fm(slots[nonlocal_widx + 3], g4, 7, mk_ev_g(npar)))
                            pending.append(piece_fm(slots[nonlocal_widx], g4, 7, mk_ev_q(npar)))
                            pending.append(piece_fm(slots[nonlocal_widx + 1], g4, 7, mk_ev_k(npar)))
                            for j in range(4):
                                pending.append(piece_v(slots[nonlocal_widx + 2], g4, j, 7, npar))
                        widx_box[0] += 4
                    else:
                        for q in range(4):
                            P.dma("pool", lambda e, q=q, l=l: e.dma_start(
                                out=HT[:, q * 2048:(q + 1) * 2048],
                                in_=dap(wout_d, l * 128 * 8192 + q * 2048, [[8192, 128], [1, 2048]])),
                                writes=[("WOUT", q)])
                        P.dma("sp", lambda e, l=l, si=si: e.dma_start(
                            out=GBCh, in_=dap(modd, (l * 4 + si) * 3072 + 2048, [[0, 128], [1, D]])),
                            reads=[("modd", l, 4), ("modd", l, 5)], writes=["GBCh"])
                    pstate["npieces"] = max(len(pending), 1)
                    pstate["colacc"] = 0.0

                widx_box = [widx]

                def qk(i):
                    hp, c, kb, c0, diag, first, last = steps[i]
                    par = hp % 2
                    zb = ZB[i % 2]
                    QTl, KTl = QTs[par], KTs[par]

                    def f(e):
                        inst = None
                        for h in range(2):
                            inst = e.matmul(PSv[:, zb[h], c0:512],
                                            lhsT=KTl[h * 64:(h + 1) * 64, kb * 128:(kb + 1) * 128],
                                            rhs=QTl[h * 64:(h + 1) * 64, c * 512 + c0:(c + 1) * 512],
                                            start=True, stop=not diag)
                        if diag:
                            for h in range(2):
                                inst = e.matmul(PSv[:, zb[h], c0:c0 + 128], lhsT=IDB, rhs=MNEG,
                                                start=False, stop=True)
                        return inst
                    P.add("pe", f, reads=[("QT", par, c), ("KT", par, kb // 4), "CB"],
                          writes=[("ps", zb[0]), ("ps", zb[1])])

                def exp_ln(i):
                    hp, c, kb, c0, diag, first, last = steps[i]
                    zb = ZB[i % 2]
                    eb, sb_ = EB[i % 3], SPB[i % 2]
                    P.add("act", lambda e: e.activation(out=eb[:, :, c0:512],
                                                        in_=PSv[:, zb[0]:zb[0] + 2, c0:512], func=AF.Exp),
                          reads=[("ps", zb[0]), ("ps", zb[1])], writes=[("E", i % 3)])
                    P.add("act", lambda e: e.activation(out=sb_[:, :, c0:512], in_=eb[:, :, c0:512],
                                                        func=AF.Ln, bias=1.0),
                          reads=[("E", i % 3)], writes=[("SP", i % 2)])

                def umm(i):
                    hp, c, kb, c0, diag, first, last = steps[i]
                    sb_ = SPB[i % 2]

                    def f(e):
                        inst = None
                        if first:
                            for h in range(2):
                                inst = e.matmul(PSv[:, XB[h], :], lhsT=ZER[:, 0:128], rhs=ZER[:, :],
                                                start=True, stop=True)
                        for h in range(2):
                            inst = e.matmul(PSv[:, XB[h], c0:512], lhsT=UNEG, rhs=sb_[:, h, c0:512],
                                            start=False, stop=True, skip_group_check=True)
                        return inst
                    P.add("pe", f, reads=[("SP", i % 2), "CB", "ZER"], writes=[("ps", XB[0]), ("ps", XB[1])])

                def expx(i):
                    hp, c, kb, c0, diag, first, last = steps[i]
                    P.add("act", lambda e: e.activation(out=PB[:, :, c0:512],
                                                        in_=PSv[:, XB[0]:XB[0] + 2, c0:512], func=AF.Exp),
                          reads=[("ps", XB[0]), ("ps", XB[1])], writes=["PB"])

                def lmm(i):
                    hp, c, kb, c0, diag, first, last = steps[i]
                    if last:
                        return
                    sb_ = SPB[i % 2]

                    def f(e):
                        inst = None
                        for h in range(2):
                            inst = e.matmul(PSv[:, XB[h], c0:512], lhsT=LNEG, rhs=sb_[:, h, c0:512],
                                            start=False, stop=True, skip_group_check=True)
                        return inst
                    P.add("pe", f, reads=[("SP", i % 2), "CB"], writes=[("ps", XB[0]), ("ps", XB[1])])

                def mul(i):
                    hp, c, kb, c0, diag, first, last = steps[i]
                    eb, ab = EB[i % 3], AB[i % 2]
                    P.add("dve", lambda e: e.tensor_tensor(out=ab[:, :, c0:512], in0=eb[:, :, c0:512],
                                                           in1=PB[:, :, c0:512], op=ALU.mult),
                          reads=[("E", i % 3), "PB"], writes=[("A", i % 2)])

                def av(i):
                    hp, c, kb, c0, diag, first, last = steps[i]
                    par = hp % 2
                    ab = AB[i % 2]
                    ob = OB6
                    VPl, GTl = VPs[par], GTs[par]

                    def f(e):
                        inst = None
                        if first:
                            inst = e.matmul(PSv[:, ob, :], lhsT=ZER[:, 0:128], rhs=ZER[:, :],
                                            start=True, stop=True)
                        for h in range(2):
                            inst = e.matmul(PSv[h * 64:(h + 1) * 64, ob, c0:512],
                                            lhsT=VPl[:, kb, h * 64:(h + 1) * 64], rhs=ab[:, h, c0:512],
                                            start=False, stop=True, skip_group_check=True)
                        return inst
                    P.add("pe", f, reads=[("A", i % 2), ("VP", par, kb // 4), "ZER"], writes=[("ps", ob)])
                    if last:
                        P.add("dve", lambda e: e.tensor_tensor(
                            out=YTv[:, 4 + hp, c * 512:(c + 1) * 512], in0=PSv[:, ob, :],
                            in1=GTl[:, c * 512:(c + 1) * 512], op=ALU.mult),
                            reads=[("ps", ob), ("GT", par, c)], writes=[("YT", 4 + hp, c)])

                silu_gate(0)
                pair_begin(0)
                qk(0)
                exp_ln(0)
                umm(0)
                qk(1)
                for i in range(n):
                    hpc = steps[i][0]
                    lstep = i - hpc * NS
                    if lstep == 0 and hpc > 0:
                        assert not pending
                        pair_begin(hpc)
                    if i + 1 < n:
                        exp_ln(i + 1)
                    if i + 2 < n:
                        qk(i + 2)
                    pstate["colacc"] += (512 - steps[i][3])
                    thr = 14500.0 / pstate["npieces"]
                    while pending and pstate["colacc"] >= thr:
                        pstate["colacc"] -= thr
                        pending.pop(0)()
                    if lstep == 34:
                        while pending:
                            pending.pop(0)()
                    if lstep == 35 and hpc < 3:
                        silu_gate((hpc + 1) % 2)
                    expx(i)
                    mul(i)
                    lmm(i)
                    if i + 1 < n:
                        umm(i + 1)
                    if i >= 1:
                        av(i - 1)
                    if hpc == 3 and lstep == 19:
                        P.add("dve", lambda e: e.tensor_scalar(out=GBCh, in0=GBCh, scalar1=float(1.0 / DN_ALPHA),
                                                               scalar2=None, op0=ALU.mult),
                              reads=["GBCh"], writes=["GBCh"])
                    if hpc == 3 and 20 <= lstep < 28:
                        kcw = lstep - 20
                        P.add("dve", lambda e, kcw=kcw: e.tensor_tensor(
                            out=WOv[:, kcw, :], in0=WOv[:, kcw, :], in1=GBCh, op=ALU.mult),
                            reads=["GBCh", ("WOUT", kcw // 2)], writes=[("WOUT", kcw // 2)])
                assert not pending
                av(n - 1)
                widx = widx_box[0]

                bank_ctr[0] = 0
                LNGB = cv("LNGB", A0 + 4096, D, F32)
                LNBB = cv("LNBB", A0 + 8192, D, F32)
                RB = [cv(("RB", i), A0 + 12288 + i * 4096, D, F32) for i in range(3)]
                P.dma("sp", lambda e, l=l: e.dma_start(out=LNGB, in_=dap(lng_d, l * D, [[0, 128], [1, D]])),
                      writes=["LNGB"])
                P.dma("sp", lambda e, l=l: e.dma_start(out=LNBB, in_=dap(lnb_d, l * D, [[0, 128], [1, D]])),
                      writes=["LNBB"])
                prev_fin = [None]
                for tt in range(NT):
                    banks = (next_bank(), next_bank())
                    assert banks[1] == banks[0] + 1
                    for hf in range(2):
                        def mm(e, tt=tt, hf=hf, bank=banks[hf]):
                            inst = None
                            for kc in range(KC):
                                inst = e.matmul(PSv[:, bank, :], lhsT=YTv[:, kc, tt * 128:(tt + 1) * 128],
                                                rhs=WOv[:, kc, hf * 512:(hf + 1) * 512],
                                                start=(kc == 0), stop=(kc == KC - 1))
                            return inst
                        P.add("pe", mm, reads=YT_all + WOUT_all, writes=[("ps", banks[hf])])
                    rb = RB[tt % 3]
                    k2 = tt % 3
                    b0 = banks[0]
                    P.add("dve", lambda e, rb=rb, tt=tt, b0=b0: e.tensor_tensor(
                        out=rb.rearrange("p (h n) -> p h n", h=2), in0=PSv[:, b0:b0 + 2, :],
                        in1=XSv[:, tt, :].rearrange("p (h n) -> p h n", h=2), op=ALU.add),
                        reads=[("ps", banks[0]), ("ps", banks[1]), ("XS", tt)], writes=[("RB", k2)])
                    BNS2 = STAT[:, 214 + k2 * 12: 214 + (k2 + 1) * 12]
                    MV2 = STAT[:, 192 + k2 * 2: 192 + (k2 + 1) * 2]
                    RS2 = STAT[:, 200 + k2: 201 + k2]
                    NB2 = STAT[:, 204 + k2: 205 + k2]
                    for hf in range(2):
                        P.add("dve", lambda e, rb=rb, hf=hf, BNS2=BNS2: e.bn_stats(
                            out=BNS2[:, hf * 6:(hf + 1) * 6], in_=rb[:, hf * 512:(hf + 1) * 512]),
                            reads=[("RB", k2)], writes=[("BNS2", k2)])
                    P.add("dve", lambda e, BNS2=BNS2, MV2=MV2: e.bn_aggr(out=MV2, in_=BNS2),
                          reads=[("BNS2", k2)], writes=[("MV2", k2)])
                    P.add("act", lambda e, RS2=RS2, MV2=MV2: e.activation(out=RS2, in_=MV2[:, 1:2],
                                                                          func=AF.Ln,
                                                                          bias=float(LN_EPS / DN_ALPHA ** 2)),
                          reads=[("MV2", k2)], writes=[("RS2", k2)])
                    P.add("act", lambda e, RS2=RS2: e.activation(out=RS2, in_=RS2, func=AF.Exp, scale=-0.5),
                          reads=[("RS2", k2)], writes=[("RS2", k2)])
                    P.add("dve", lambda e, NB2=NB2, MV2=MV2, RS2=RS2: e.scalar_tensor_tensor(
                        out=NB2, in0=MV2[:, 0:1], scalar=-1.0, in1=RS2,
                        op0=ALU.mult, op1=ALU.mult), reads=[("MV2", k2), ("RS2", k2)], writes=[("NB2", k2)])
                    P.add("act", lambda e, rb=rb, RS2=RS2, NB2=NB2: e.activation(
                        out=rb, in_=rb, func=AF.Identity, scale=RS2, bias=NB2),
                        reads=[("RB", k2), ("RS2", k2), ("NB2", k2)], writes=[("RB", k2)])
                    P.add("pool", lambda e, rb=rb: e.tensor_tensor(out=rb, in0=rb, in1=LNGB, op=ALU.mult),
                          reads=[("RB", k2), "LNGB"], writes=[("RB", k2)])
                    P.add("pool", lambda e, rb=rb, tt=tt: e.tensor_tensor(
                        out=XSv[:, tt, 0:512], in0=rb[:, 0:512], in1=LNBB[:, 0:512], op=ALU.add),
                        reads=[("RB", k2), "LNBB"], writes=[("XS", tt)])
                    def fin_tile(tt=tt, rb=rb, k2=k2, si=si):
                        P.add("dve", lambda e: e.tensor_tensor(out=XSv[:, tt, 512:1024], in0=rb[:, 512:1024],
                                                               in1=LNBB[:, 512:1024], op=ALU.add),
                              reads=[("RB", k2), "LNBB"], writes=[("XS", tt)])
                        if li == nlayers - 1:
                            od = P.dma("sp", lambda e: e.dma_start(
                                out=dap(out_d, (si * S + tt * 128) * D, [[D, 128], [1, D]]), in_=XSv[:, tt, :]),
                                reads=[("XS", tt)])
                            out_dmas.append(od)
                    if prev_fin[0] is not None:
                        prev_fin[0]()
                    prev_fin[0] = fin_tile
                prev_fin[0]()
                prev_fin[0] = None

        fin = P.add("sp", None)
        fin.deps = set(out_dmas)

        with nc.Block() as block:
            emit(nc, P, block, sems, {"sp": ring_sp, "pool": ring_pool})
    return nc


def _consts():
    j = np.arange(128)[:, None]
    s = np.arange(128)[None, :]
    uneg = np.where(j >= s, -1.0, 0.0)
    lneg = np.where(j < s, -1.0, 0.0)
    mneg = np.where(j >= s, -30000.0, 0.0)
    idb = np.eye(128)
    cb = np.concatenate([uneg, lneg, mneg, idb, np.ones((128, 128))], axis=1).astype(np.float32).astype(ml_dtypes.bfloat16)
    invc = np.zeros((128, 2 * PAD), np.float32)
    invw = np.zeros((128, 2), np.float32)
    for ch in range(2):
        for p in range(128):
            win = WINS[2 * ch + p // 64]
            invw[p, ch] = 1.0 / win
            for t in range(PAD):
                invc[p, ch * PAD + t] = 1.0 / min(t + 1, win)
    return cb, invc, invw


def _prep_shared(w_in, pool_w, pool_scale, sgu_ln_g, sgu_ln_b, sgu_w, sgu_b, w_out, ada_w, ada_b, ln_g, ln_b):
    f = lambda a: np.ascontiguousarray(np.asarray(a, dtype=np.float32))
    w_in, w_out, ada_w = f(w_in), f(w_out), f(ada_w)
    cb, invc, invw = _consts()
    sh = {}
    sh["w_in_b"] = np.ascontiguousarray(
        w_in.reshape(DEPTH, KC, 128, 26, 128).transpose(0, 3, 2, 1, 4)).reshape(DEPTH, 26, 128, 1024)
    sh["w_out_b"] = np.ascontiguousarray(
        w_out.reshape(DEPTH, KC, 128, D).transpose(0, 2, 1, 3)).reshape(DEPTH, 128, 8192)
    sh["ada_w_b"] = np.ascontiguousarray(
        ada_w.reshape(DEPTH, KC, 128, 6, 512).transpose(0, 3, 2, 1, 4)).reshape(DEPTH, 6, 128, 4096)
    sh["ada_b"] = f(ada_b)
    pw = f(pool_w)
    sh["pool_w_b"] = np.ascontiguousarray(
        pw.reshape(DEPTH, 2, 2, 64, 64).transpose(0, 2, 3, 1, 4)).reshape(DEPTH, 128, 128)
    sh["pool_sc"] = np.ascontiguousarray(f(pool_scale).reshape(DEPTH, 2, 128).transpose(0, 2, 1))
    sh["sgu_g"] = f(sgu_ln_g)
    sh["sgu_bln"] = f(sgu_ln_b)
    sh["sgu_wT"] = np.ascontiguousarray(f(sgu_w).transpose(0, 3, 1, 2)).reshape(DEPTH, 128, 512)
    sh["sgu_bias"] = f(sgu_b)
    sh["ln_g"] = f(ln_g)
    sh["ln_b"] = f(ln_b)
    sh["ident"] = np.eye(128, dtype=np.float32)
    sh["cst_bf"] = cb
    sh["invc"] = invc
    sh["invw"] = invw
    return sh


FUSED = True


def kernel(x, c, w_in, pool_w, pool_scale, sgu_ln_g, sgu_ln_b, sgu_w, sgu_b,
           w_out, ada_w, ada_b, ln_g, ln_b):
    x = np.asarray(x, dtype=np.float32)
    c = np.asarray(c, dtype=np.float32)
    sh = _prep_shared(w_in, pool_w, pool_scale, sgu_ln_g, sgu_ln_b, sgu_w, sgu_b, w_out, ada_w, ada_b, ln_g, ln_b)
    cTs = []
    for i in range(NCORES):
        cc = c[i * NSEQ:(i + 1) * NSEQ]
        cTs.append(np.ascontiguousarray(cc.reshape(NSEQ, KC, 128).transpose(2, 1, 0)).reshape(128, KC * NSEQ))

    def run(nc, xin):
        in_maps = []
        for i in range(NCORES):
            m = dict(sh)
            m["x"] = np.ascontiguousarray(xin[i * NSEQ:(i + 1) * NSEQ])
            m["cT"] = cTs[i]
            in_maps.append(m)
        res = run_bass_kernel_spmd(nc, in_maps, core_ids=list(range(NCORES)))
        return np.concatenate([np.asarray(r["out"]) for r in res.results], axis=0)

    if FUSED:
        return run(build_nc(DEPTH, 0), x)
    y = x
    for l in range(DEPTH):
        y = run(build_nc(1, l), y)
    return y
```

```python
import contextlib
import numpy as np
import ml_dtypes
import concourse.bass as bass
import concourse.mybir as mybir
from concourse.bass_utils import run_bass_kernel_spmd

F32 = mybir.dt.float32
BF16 = mybir.dt.bfloat16
AF = mybir.ActivationFunctionType
ALU = mybir.AluOpType

NCORES = 8
NSEQ = 4
S = 2048
D = 1024
NT = 16
KC = 8
DEPTH = 2
LN_EPS = 1e-5
DN_ALPHA = (2 * DEPTH) ** 0.25
WINS = (2, 4, 8, 16)
PAD = 16

ARENA_BYTES = 64 * 1024


class Op:
    __slots__ = ("eng", "fn", "deps", "kind", "signal", "seq", "sem", "val", "idx")


class Prog:
    def __init__(self):
        self.ops = []
        self.res_w = {}
        self.res_r = {}
        self.alias = {}
        self.by_space = {}

    def set_alias(self, name, space, ranges):
        if name in self.alias:
            assert self.alias[name] == (space, ranges), name
            return
        self.alias[name] = (space, ranges)
        self.by_space.setdefault(space, []).append(name)

    def _overlaps(self, name):
        a = self.alias.get(name)
        if a is None:
            return ()
        space, rs = a
        out = []
        for m in self.by_space[space]:
            if m == name:
                continue
            for (lo, hi) in self.alias[m][1]:
                if any(lo < h2 and l2 < hi for (l2, h2) in rs):
                    out.append(m)
                    break
        return out

    def _add(self, eng, fn, reads, writes, kind):
        op = Op()
        op.eng, op.fn, op.kind = eng, fn, kind
        op.signal = False
        op.seq = op.sem = op.val = None
        op.idx = len(self.ops)
        deps = set()
        for r in reads:
            w = self.res_w.get(r)
            if w is not None:
                deps.add(w)
        for r in writes:
            w = self.res_w.get(r)
            if w is not None:
                deps.add(w)
            for rd in self.res_r.get(r, ()):
                deps.add(rd)
        for r in reads:
            self.res_r.setdefault(r, []).append(op)
        for r in writes:
            self.res_w[r] = op
            self.res_r[r] = []
        for r in list(reads) + list(writes):
            for m in self._overlaps(r):
                w = self.res_w.get(m)
                if w is not None:
                    deps.add(w)
                for rd in self.res_r.get(m, ()):
                    deps.add(rd)
        deps.discard(op)
        op.deps = deps
        self.ops.append(op)
        return op

    def add(self, eng, fn, reads=(), writes=()):
        return self._add(eng, fn, reads, writes, "c")

    def dma(self, queue, fn, reads=(), writes=()):
        return self._add(queue, fn, reads, writes, "d")

    def finalize(self):
        for op in self.ops:
            nd = set()
            for d in op.deps:
                if d.kind == "c" and op.kind == "c" and d.eng == "pe" and op.eng == "pe":
                    continue
                nd.add(d)
            op.deps = nd
            for d in nd:
                d.signal = True


def emit(nc, prog, block, sems, dma_sems):
    prog.finalize()
    seqc = {e: 0 for e in sems}
    dcount = {q: 0 for q in dma_sems}
    dprev = {}
    for op in prog.ops:
        if op.kind == "c":
            if op.signal:
                seqc[op.eng] += 1
                op.seq = seqc[op.eng]
        else:
            ring = dma_sems[op.eng]
            i = dcount[op.eng]
            dcount[op.eng] += 1
            op.sem = ring[i % len(ring)]
            op.val = 16 * (i // len(ring) + 1)
    lists = {e: [] for e in sems}
    for op in prog.ops:
        lists[op.eng].append(op)

    def body(engname):
        def f(e):
            waited = {}

            def wait(sem, val):
                k = id(sem)
                if waited.get(k, 0) >= val:
                    return
                waited[k] = val
                e.wait_ge(sem, val)

            for op in lists[engname]:
                for d in sorted(op.deps, key=lambda o: o.idx):
                    if d.kind == "c":
                        wait(sems[d.eng], d.seq)
                    else:
                        wait(d.sem, d.val)
                if op.kind == "d":
                    if op.val > 16:
                        wait(op.sem, op.val - 16)
                    inst = op.fn(e)
                    inst.then_inc(op.sem, 16)
                else:
                    if op.fn is None:
                        continue
                    inst = op.fn(e)
                    if op.signal:
                        inst.then_inc(sems[op.eng], 1)
        return f

    block.sync(body("sp"))
    block.gpsimd(body("pool"))
    block.tensor(body("pe"))
    block.vector(body("dve"))
    block.scalar(body("act"))


def build_nc(nlayers=DEPTH, layer0=0, nseq=NSEQ):
    nc = bass.Bass("TRN2", target_bir_lowering=False)
    dr = lambda name, shape, dt=F32: nc.dram_tensor(name, list(shape), dt, kind="ExternalInput")
    x_d = dr("x", [nseq, S, D])
    cT_d = dr("cT", [128, KC * 4])
    win_d = dr("w_in_b", [DEPTH, 26, 128, 1024])
    wout_d = dr("w_out_b", [DEPTH, 128, 8192])
    ada_d = dr("ada_w_b", [DEPTH, 6, 128, 4096])
    adab_d = dr("ada_b", [DEPTH, 3072])
    pw_d = dr("pool_w_b", [DEPTH, 128, 128])
    psc_d = dr("pool_sc", [DEPTH, 128, 2])
    sgg_d = dr("sgu_g", [DEPTH, 256])
    sgb_d = dr("sgu_bln", [DEPTH, 256])
    swt_d = dr("sgu_wT", [DEPTH, 128, 512])
    sbias_d = dr("sgu_bias", [DEPTH, 4, 128])
    lng_d = dr("ln_g", [DEPTH, 1024])
    lnb_d = dr("ln_b", [DEPTH, 1024])
    ident_d = dr("ident", [128, 128])
    cb_d = dr("cst_bf", [128, 640], BF16)
    invc_d = dr("invc", [128, 2 * PAD])
    invw_d = dr("invw", [128, 2])
    out_d = nc.dram_tensor("out", [nseq, S, D], F32, kind="ExternalOutput")
    modd = nc.dram_tensor("modd", [DEPTH, 4, 3072], F32, kind="Internal")

    def dap(t, offset, pat):
        return bass.AP(t, offset, [list(p) for p in pat])

    P = Prog()

    with contextlib.ExitStack() as es:
        sb = lambda name, shape, dt: es.enter_context(nc.sbuf_tensor(name, shape, dt))
        XS = sb("XS", [128, NT * D], F32)
        HT = sb("HT", [128, KC * S], BF16)
        YT = sb("YT", [128, KC * S], BF16)
        ARb = sb("arena", [128, ARENA_BYTES // 2], BF16)
        IDENT = sb("IDENT", [128, 128], F32)
        CB = sb("CB", [128, 640], BF16)
        ZER = sb("ZER", [128, 512], BF16)
        MODT = sb("MODT", [128, DEPTH * 4 * 16], F32)
        PW = sb("PW", [128, DEPTH * 128], BF16)
        PSC = sb("PSC", [128, DEPTH * 2], F32)
        SWT = sb("SWT", [128, DEPTH * 512], BF16)
        INVC = sb("INVC", [128, 2 * PAD], F32)
        INVW = sb("INVW", [128, 2], F32)
        STAT = sb("STAT", [128, 256], F32)
        PS = es.enter_context(nc.psum_tensor("PS", [128, 4096], F32))
        sem = lambda name: es.enter_context(nc.semaphore(name))
        sems = {"pe": sem("s_pe"), "act": sem("s_act"), "dve": sem("s_dve"),
                "pool": sem("s_pool"), "sp": sem("s_sp")}
        ring_sp = [sem(f"dsp{i}") for i in range(20)]
        ring_pool = [sem(f"dpl{i}") for i in range(8)]
        AR32 = ARb.bitcast(F32)

        def cv(name, off, n, dt):
            size = 4 if dt == F32 else 2
            assert off % size == 0 and off + size * n <= ARENA_BYTES, (name, off, n)
            P.set_alias(name, "arena", [(off, off + size * n)])
            if dt == F32:
                return AR32[:, off // 4: off // 4 + n]
            return ARb[:, off // 2: off // 2 + n]

        XSv = XS[:].rearrange("p (t f) -> p t f", t=NT)
        HTv = HT[:].rearrange("p (k t) -> p k t", k=KC)
        YTv = YT[:].rearrange("p (k t) -> p k t", k=KC)
        PSv = PS[:].rearrange("p (b n) -> p b n", b=8)
        UNEG = CB[:, 0:128]
        LNEG = CB[:, 128:256]
        MNEG = CB[:, 256:384]
        IDB = CB[:, 384:512]
        ONESB = CB[:, 512:640]
        for kc in range(KC):
            for g4 in range(4):
                o = (kc * S + g4 * 512) * 2
                P.set_alias(("HT", kc, g4), "HT", [(o, o + 1024)])
                P.set_alias(("YT", kc, g4), "YT", [(o, o + 1024)])
        for q in range(4):
            P.set_alias(("WOUT", q), "HT", [(q * 4096, (q + 1) * 4096)])
        P.set_alias("DT", "YT", [(7 * S * 2, 8 * S * 2)])
        for t in range(NT):
            P.set_alias(("VLN", t), "YT", [((5 + hf) * S * 2 + t * 256, (5 + hf) * S * 2 + (t + 1) * 256)
                                           for hf in range(2)])
        HT_all = [("HT", kc, g4) for kc in range(KC) for g4 in range(4)]
        WOUT_all = [("WOUT", q) for q in range(4)]
        YT_all = [("YT", chn, g4) for chn in range(8) for g4 in range(4)]

        bank_ctr = [0]

        def next_bank():
            b = bank_ctr[0] % 8
            bank_ctr[0] += 1
            return b

        WR = [cv(("W", i), i * 2048, 1024, BF16) for i in range(4)]
        A0 = 4 * 2048
        wring_ctr = [0]

        def load_wblock(l, blk):
            slot = wring_ctr[0] % 4
            wring_ctr[0] += 1
            src = dap(win_d, ((l * 26 + blk) * 128) * 1024, [[1024, 128], [1, 1024]])
            P.dma("pool", lambda e, slot=slot, src=src: e.dma_start(out=WR[slot], in_=src),
                  writes=[("W", slot)])
            return slot

        P.dma("sp", lambda e: e.dma_start(out=IDENT[:], in_=ident_d.ap()), writes=["IDENT"])
        P.dma("sp", lambda e: e.dma_start(out=CB[:], in_=cb_d.ap()), writes=["CB"])
        P.dma("sp", lambda e: e.dma_start(out=INVC[:], in_=invc_d.ap()), writes=["INVC"])
        P.dma("sp", lambda e: e.dma_start(out=INVW[:], in_=invw_d.ap()), writes=["INVW"])
        P.dma("sp", lambda e: e.dma_start(out=PSC[:].rearrange("p (l c) -> p l c", l=DEPTH),
                                          in_=psc_d.ap().rearrange("l p c -> p l c")),
              writes=["PSC"])
        P.dma("pool", lambda e: e.dma_start(out=PW[:].rearrange("p (l c) -> p l c", l=DEPTH),
                                            in_=pw_d.ap().rearrange("l p c -> p l c")),
              writes=["PW"])
        P.dma("pool", lambda e: e.dma_start(out=SWT[:].rearrange("p (l c) -> p l c", l=DEPTH),
                                            in_=swt_d.ap().rearrange("l p c -> p l c")),
              writes=["SWT"])
        P.add("dve", lambda e: e.memset(ZER[:], 0.0), writes=["ZER"])
        SWTv = SWT[:].rearrange("p (a t) -> p a t", t=128)
        P.add("dve", lambda e: e.memset(SWTv[64:128, :, 0:64], 0.0), reads=["SWT"], writes=["SWT"])

        SC = cv("SC", A0, 32, F32)
        ADAB = cv("ADAB", A0 + 128, 3072, F32)
        MR = [cv(("MR", i), A0 + 128 + 12288 + i * 2048, 512, F32) for i in range(2)]
        ADA_OFF = A0 + 128 + 12288 + 4096
        ADA = [cv(("ADA", i), ADA_OFF + i * 8192, 2048, F32) for i in range(4)]

        P.dma("sp", lambda e: e.dma_start(out=SC, in_=cT_d.ap()), writes=["SC"])
        P.add("act", lambda e: e.activation(out=SC, in_=SC, func=AF.Silu), reads=["SC"], writes=["SC"])
        adactr = 0
        for l in range(DEPTH):
            P.dma("sp", lambda e, l=l: e.dma_start(
                out=ADAB[0:4, :], in_=dap(adab_d, l * 3072, [[0, 4], [1, 3072]])),
                writes=["ADAB"])
            for cbk in range(6):
                bank = next_bank()
                for half in range(2):
                    buf = adactr % 4
                    adactr += 1
                    src = dap(ada_d, ((l * 6 + cbk) * 128) * 4096 + half * 2048,
                              [[4096, 128], [1, 2048]])
                    P.dma("sp", lambda e, buf=buf, src=src: e.dma_start(out=ADA[buf], in_=src),
                          writes=[("ADA", buf)])

                    def mm(e, buf=buf, half=half, bank=bank):
                        inst = None
                        for k4 in range(4):
                            kc = half * 4 + k4
                            inst = e.matmul(PSv[0:4, bank, :], lhsT=SC[:, kc * 4:(kc + 1) * 4],
                                            rhs=ADA[buf][:, k4 * 512:(k4 + 1) * 512],
                                            start=(kc == 0), stop=(kc == 7))
                        return inst
                    P.add("pe", mm, reads=[("ADA", buf), "SC"], writes=[("ps", bank)])
                mr = MR[cbk % 2]
                P.add("dve", lambda e, mr=mr, bank=bank, cbk=cbk: e.tensor_tensor(
                    out=mr[0:4, :], in0=PSv[0:4, bank, :], in1=ADAB[0:4, cbk * 512:(cbk + 1) * 512],
                    op=ALU.add), reads=[("ps", bank), "ADAB"], writes=[("MR", cbk % 2)])
                P.dma("sp", lambda e, mr=mr, l=l, cbk=cbk: e.dma_start(
                    out=dap(modd, l * 4 * 3072 + cbk * 512, [[3072, 4], [1, 512]]), in_=mr[0:4, :]),
                    reads=[("MR", cbk % 2)], writes=[("modd", l, cbk)])
        for l in range(DEPTH):
            for b in range(4):
                P.dma("sp", lambda e, l=l, b=b: e.dma_start(
                    out=MODT[:, (l * 4 + b) * 16:(l * 4 + b + 1) * 16],
                    in_=dap(modd, (l * 4 + b) * 3072, [[1, 128], [128, 16]]),
                    allow_slow_non_contiguous=True),
                    reads=[("modd", l, cbk) for cbk in range(4)], writes=[("MODTld", l, b)])
        MODTv = MODT[:].rearrange("p (a j) -> p a j", j=16)
        P.add("dve", lambda e: e.tensor_scalar(out=MODTv[:, :, 8:16], in0=MODTv[:, :, 8:16],
                                               scalar1=1.0, scalar2=None, op0=ALU.add),
              reads=[("MODTld", l, b) for l in range(DEPTH) for b in range(4)], writes=["MODT"])

        out_dmas = []
        evac_ctr = [0]

        for si in range(nseq):
            for tt in range(NT):
                P.dma("sp", lambda e, si=si, tt=tt: e.dma_start(
                    out=XSv[:, tt, :], in_=dap(x_d, (si * S + tt * 128) * D, [[D, 128], [1, D]])),
                    writes=[("XS", tt)])
            for li in range(nlayers):
                l = layer0 + li
                mcol = (l * 4 + si) * 16
                blk_order = [0, 2, 1, 3, 6, 7, 8, 4, 9, 5]
                for hp in range(4):
                    if hp == 0:
                        blk_order += [22 + hp, 10 + hp, 14 + hp, 18 + hp]
                    else:
                        blk_order += [10 + hp, 14 + hp, 18 + hp, 22 + hp]
                slots = {}
                nload = [0]

                def prefetch(upto, l=l, slots=slots, nload=nload, blk_order=blk_order):
                    while nload[0] < min(upto, len(blk_order)):
                        slots[nload[0]] = load_wblock(l, blk_order[nload[0]])
                        nload[0] += 1
                prefetch(2)

                for g4 in range(4):
                    for kc in range(KC):
                        bank = next_bank()

                        def tr(e, g4=g4, kc=kc, bank=bank):
                            inst = None
                            for j in range(4):
                                inst = e.transpose(out=PSv[:, bank, j * 128:(j + 1) * 128],
                                                   in_=XSv[:, g4 * 4 + j, kc * 128:(kc + 1) * 128],
                                                   identity=IDENT[:])
                            return inst
                        P.add("pe", tr, reads=[("XS", g4 * 4 + j) for j in range(4)] + ["IDENT"],
                              writes=[("ps", bank)])
                        dst = HTv[:, kc, g4 * 512:(g4 + 1) * 512]
                        sc_ap = MODT[:, mcol + 8 + kc: mcol + 9 + kc]
                        sh_ap = MODT[:, mcol + kc: mcol + kc + 1]
                        if evac_ctr[0] % 2 == 0:
                            P.add("dve", lambda e, dst=dst, bank=bank, sc_ap=sc_ap, sh_ap=sh_ap:
                                  e.tensor_scalar(out=dst, in0=PSv[:, bank, :], scalar1=sc_ap,
                                                  scalar2=sh_ap, op0=ALU.mult, op1=ALU.add),
                                  reads=[("ps", bank), "MODT"], writes=[("HT", kc, g4)])
                        else:
                            P.add("act", lambda e, dst=dst, bank=bank, sc_ap=sc_ap, sh_ap=sh_ap:
                                  e.activation(out=dst, in_=PSv[:, bank, :], func=AF.Identity,
                                               scale=sc_ap, bias=sh_ap),
                                  reads=[("ps", bank), "MODT"], writes=[("HT", kc, g4)])
                        evac_ctr[0] += 1

                def proj_fm(widx, evac, slots=slots, prefetch=prefetch):
                    slot = slots[widx]
                    prefetch(widx + 3)
                    for g4 in range(4):
                        bank = next_bank()

                        def mm(e, slot=slot, g4=g4, bank=bank):
                            inst = None
                            for kc in range(KC):
                                inst = e.matmul(PSv[:, bank, :], lhsT=WR[slot][:, kc * 128:(kc + 1) * 128],
                                                rhs=HTv[:, kc, g4 * 512:(g4 + 1) * 512],
                                                start=(kc == 0), stop=(kc == KC - 1))
                            return inst
                        P.add("pe", mm, reads=[("W", slot)] + [("HT", kc, g4) for kc in range(KC)],
                              writes=[("ps", bank)])
                        evac(bank, g4)

                def proj_tm(widx, evac, group=4, slots=slots, prefetch=prefetch):
                    slot = slots[widx]
                    prefetch(widx + 3)
                    for t0 in range(0, NT, group):
                        bank = next_bank()

                        def mm(e, slot=slot, t0=t0, bank=bank):
                            inst = None
                            for j in range(group):
                                tt = t0 + j
                                for kc in range(KC):
                                    inst = e.matmul(PSv[:, bank, j * 128:(j + 1) * 128],
                                                    lhsT=HTv[:, kc, tt * 128:(tt + 1) * 128],
                                                    rhs=WR[slot][:, kc * 128:(kc + 1) * 128],
                                                    start=(kc == 0), stop=(kc == KC - 1))
                            return inst
                        P.add("pe", mm, reads=[("W", slot)] + HT_all, writes=[("ps", bank)])
                        evac(bank, t0, group)

                W2 = S + PAD
                AT = cv("AT", A0, W2, F32)
                B1 = cv("B1", A0 + 4 * W2, W2, F32)
                B2 = cv("B2", A0 + 8 * W2, W2, F32)
                SGA = cv("SGA", A0 + 12 * W2, S, F32)
                DT = YTv[:, 7, :]
                for bufname, buf in (("AT", AT), ("B1", B1), ("B2", B2)):
                    P.add("pool", lambda e, buf=buf: e.memset(buf[:, 0:PAD], 0.0), writes=[bufname])
                widx = 0
                sh = lambda buf, k: buf[:, PAD - k: PAD - k + S]
                for ch in range(2):
                    def ev_a(bank, g4):
                        P.add("act", lambda e, bank=bank, g4=g4: e.activation(
                            out=AT[:, PAD + g4 * 512: PAD + (g4 + 1) * 512], in_=PSv[:, bank, :],
                            func=AF.Identity), reads=[("ps", bank)], writes=["AT"])
                    proj_fm(widx, ev_a)
                    widx += 1

                    def ev_g(bank, g4):
                        P.add("act", lambda e, bank=bank, g4=g4: e.activation(
                            out=SGA[:, g4 * 512:(g4 + 1) * 512], in_=PSv[:, bank, :],
                            func=AF.Silu), reads=[("ps", bank)], writes=["SGA"])
                    proj_fm(widx, ev_g)
                    widx += 1
                    P.add("pool", lambda e: e.tensor_tensor(out=B1[:, PAD:], in0=AT[:, PAD:], in1=sh(AT, 1),
                                                            op=ALU.add), reads=["AT"], writes=["B1"])
                    if ch == 0:
                        P.add("pool", lambda e: e.tensor_tensor(out=B2[64:128, PAD:], in0=B1[64:128, PAD:],
                                                                in1=sh(B1, 2)[64:128], op=ALU.add),
                              reads=["B1"], writes=["B2"])
                    else:
                        P.add("pool", lambda e: e.tensor_tensor(out=B2[:, PAD:], in0=B1[:, PAD:], in1=sh(B1, 2),
                                                                op=ALU.add), reads=["B1"], writes=["B2"])
                        P.add("pool", lambda e: e.tensor_tensor(out=B1[:, PAD:], in0=B2[:, PAD:], in1=sh(B2, 4),
                                                                op=ALU.add), reads=["B2"], writes=["B1"])
                        P.add("pool", lambda e: e.tensor_tensor(out=B2[64:128, PAD:], in0=B1[64:128, PAD:],
                                                                in1=sh(B1, 8)[64:128], op=ALU.add),
                              reads=["B1"], writes=["B2"])
                    for (p0, p1, src, sname) in ((0, 64, B1, "B1"), (64, 128, B2, "B2")):
                        P.add("dve", lambda e, p0=p0, p1=p1, src=src, ch=ch: e.scalar_tensor_tensor(
                            out=DT[p0:p1, :], in0=src[p0:p1, PAD:], scalar=INVW[p0:p1, ch:ch + 1],
                            in1=AT[p0:p1, PAD:], op0=ALU.mult, op1=ALU.subtract),
                            reads=[sname, "AT", "INVW"], writes=["DT"])
                        P.add("dve", lambda e, p0=p0, p1=p1, src=src, ch=ch: e.tensor_tensor(
                            out=src[p0:p1, PAD:2 * PAD], in0=src[p0:p1, PAD:2 * PAD],
                            in1=INVC[p0:p1, ch * PAD:(ch + 1) * PAD], op=ALU.mult),
                            reads=["DT", "INVC"], writes=[sname])
                        P.add("dve", lambda e, p0=p0, p1=p1, src=src: e.tensor_tensor(
                            out=DT[p0:p1, 0:PAD], in0=src[p0:p1, PAD:2 * PAD], in1=AT[p0:p1, PAD:2 * PAD],
                            op=ALU.subtract), reads=[sname, "AT"], writes=["DT"])
                    for g4 in range(4):
                        for (p0, p1) in ((0, 64), (64, 128)):
                            bank = next_bank()
                            P.add("pe", lambda e, p0=p0, p1=p1, g4=g4, bank=bank, ch=ch, l=l: e.matmul(
                                PSv[p0:p1, bank, :],
                                lhsT=PW[p0:p1, l * 128 + ch * 64: l * 128 + (ch + 1) * 64],
                                rhs=DT[p0:p1, g4 * 512:(g4 + 1) * 512], start=True, stop=True),
                                reads=["DT", "PW"], writes=[("ps", bank)])
                            P.add("dve", lambda e, p0=p0, p1=p1, g4=g4, bank=bank, ch=ch, l=l:
                                  e.scalar_tensor_tensor(
                                      out=YTv[p0:p1, ch, g4 * 512:(g4 + 1) * 512], in0=PSv[p0:p1, bank, :],
                                      scalar=PSC[p0:p1, l * 2 + ch: l * 2 + ch + 1],
                                      in1=SGA[p0:p1, g4 * 512:(g4 + 1) * 512], op0=ALU.mult, op1=ALU.mult),
                                  reads=[("ps", bank), "PSC", "SGA"], writes=[("YT", ch, g4)])

                VLN = [YTv[:, 5, :].rearrange("p (t f) -> p t f", t=NT),
                       YTv[:, 6, :].rearrange("p (t f) -> p t f", t=NT)]
                UT = cv("UT", A0, S, F32)
                SGB = cv("SGB", A0 + 4 * S, S, F32)
                for gi in range(4):
                    P.set_alias(("UT", gi), "arena", [(A0 + gi * 2048, A0 + (gi + 1) * 2048)])
                    P.set_alias(("SGB", gi), "arena", [(A0 + 4 * S + gi * 2048, A0 + 4 * S + (gi + 1) * 2048)])
                T1 = [cv(("T1", i), A0 + 8 * S + i * 2048, 512, F32) for i in range(2)]
                BSB = cv("BSB", A0 + 8 * S + 8192, 256, F32)
                GCOL = STAT[:, 208:210]
                BCOL = STAT[:, 210:212]
                P.dma("sp", lambda e, l=l: e.dma_start(out=GCOL, in_=dap(sgg_d, l * 256, [[1, 128], [128, 2]]),
                                                      allow_slow_non_contiguous=True), writes=["GCOL"])
                P.dma("sp", lambda e, l=l: e.dma_start(out=BCOL, in_=dap(sgb_d, l * 256, [[1, 128], [128, 2]]),
                                                      allow_slow_non_contiguous=True), writes=["BCOL"])
                for h in range(4):
                    P.dma("sp", lambda e, h=h, l=l: e.dma_start(
                        out=BSB[(h % 2) * 64:(h % 2 + 1) * 64, (h // 2) * 128:(h // 2 + 1) * 128],
                        in_=dap(sbias_d, (l * 4 + h) * 128, [[0, 64], [1, 128]])),
                        writes=["BSB"] if h == 0 else [("BSBx", h)])
                BSB_all = ["BSB"] + [("BSBx", h) for h in range(1, 4)]
                for pr_ in range(2):
                    P.set_alias(("BSB2", pr_), "arena", [(A0 + 8 * S + 8192, A0 + 8 * S + 8192 + 1024)])
                for half in range(2):
                    slot = slots[widx]
                    prefetch(widx + 3)
                    widx += 1
                    for tt in range(NT):
                        bank = tt // 2
                        c0 = (tt % 2) * 256 + half * 128

                        def mm(e, slot=slot, tt=tt, bank=bank, c0=c0):
                            inst = None
                            for kc in range(KC):
                                inst = e.matmul(PSv[:, bank, c0:c0 + 128],
                                                lhsT=HTv[:, kc, tt * 128:(tt + 1) * 128],
                                                rhs=WR[slot][:, kc * 128:(kc + 1) * 128],
                                                start=(kc == 0), stop=(kc == KC - 1))
                            return inst
                        P.add("pe", mm, reads=[("W", slot)] + HT_all, writes=[("ps", bank)])
                bank_ctr[0] = 0
                BNS = STAT[:, 0:96].rearrange("p (t s) -> p t s", t=NT)
                MV = STAT[:, 96:128].rearrange("p (t s) -> p t s", t=NT)
                RSTD = STAT[:, 128:144]
                for tt in range(NT):
                    bank = tt // 2
                    c0 = (tt % 2) * 256
                    P.add("dve", lambda e, tt=tt, bank=bank, c0=c0: e.bn_stats(
                        out=BNS[:, tt, :], in_=PSv[:, bank, c0:c0 + 256]),
                        reads=[("ps", bank)], writes=[("BNS", tt)])
                    P.add("dve", lambda e, tt=tt: e.bn_aggr(out=MV[:, tt, :], in_=BNS[:, tt, :]),
                          reads=[("BNS", tt)], writes=["MV"])
                P.add("act", lambda e: e.activation(out=RSTD, in_=MV[:, :, 1], func=AF.Ln, bias=LN_EPS),
                      reads=["MV"], writes=["RSTD"])
                P.add("act", lambda e: e.activation(out=RSTD, in_=RSTD, func=AF.Exp, scale=-0.5),
                      reads=["RSTD"], writes=["RSTD"])
                for tt in range(NT):
                    bank = tt // 2
                    c0 = (tt % 2) * 256
                    P.add("dve", lambda e, tt=tt, bank=bank, c0=c0: e.tensor_scalar(
                        out=YTv[:, 5:7, tt * 128:(tt + 1) * 128],
                        in0=PSv[:, bank, c0:c0 + 256].rearrange("p (h f) -> p h f", h=2),
                        scalar1=MV[:, tt, 0:1], scalar2=RSTD[:, tt:tt + 1], op0=ALU.subtract, op1=ALU.mult),
                        reads=[("ps", bank), "MV", "RSTD"], writes=[("VLN", tt)])
                SWTl = SWT[:, l * 512:(l + 1) * 512].rearrange("p (h t) -> p h t", h=4)
                for pr in range(2):
                    bank = next_bank()

                    def rwmm(e, bank=bank, pr=pr, SWTl=SWTl):
                        inst = None
                        for hh in range(2):
                            inst = e.matmul(PSv[hh * 64:(hh + 1) * 64, bank, 0:128], lhsT=ONESB[:, 0:64],
                                            rhs=SWTl[:, pr * 2 + hh, :], start=True, stop=True)
                        return inst
                    P.add("pe", rwmm, reads=["SWT", "CB"], writes=[("ps", bank)])
                    P.add("dve", lambda e, bank=bank, pr=pr: e.scalar_tensor_tensor(
                        out=BSB[:, pr * 128:(pr + 1) * 128], in0=PSv[:, bank, 0:128], scalar=BCOL[:, pr:pr + 1],
                        in1=BSB[:, pr * 128:(pr + 1) * 128], op0=ALU.mult, op1=ALU.add),
                        reads=[("ps", bank), "BCOL"] + BSB_all, writes=[("BSB2", pr)])

                    def ev_gb(bank, g4):
                        P.add("act", lambda e, bank=bank, g4=g4: e.activation(
                            out=SGB[:, g4 * 512:(g4 + 1) * 512], in_=PSv[:, bank, :], func=AF.Silu),
                            reads=[("ps", bank)], writes=[("SGB", g4)])
                    proj_fm(widx, ev_gb)
                    widx += 1

                    def ev_u(bank, g4):
                        P.add("dve", lambda e, bank=bank, g4=g4: e.tensor_tensor(
                            out=UT[:, g4 * 512:(g4 + 1) * 512], in0=PSv[:, bank, :],
                            in1=SGB[:, g4 * 512:(g4 + 1) * 512], op=ALU.mult),
                            reads=[("ps", bank), ("SGB", g4)], writes=[("UT", g4)])
                    proj_fm(widx, ev_u)
                    widx += 1
                    for g4 in range(4):
                        bank = next_bank()

                        def mm(e, g4=g4, bank=bank, pr=pr, SWTl=SWTl):
                            inst = None
                            for j in range(4):
                                tt = g4 * 4 + j
                                for hh in range(2):
                                    inst = e.matmul(PSv[hh * 64:(hh + 1) * 64, bank, j * 128:(j + 1) * 128],
                                                    lhsT=VLN[pr][:, tt, hh * 64:(hh + 1) * 64],
                                                    rhs=SWTl[:, pr * 2 + hh, :], start=True, stop=True)
                            return inst
                        P.add("pe", mm, reads=[("VLN", g4 * 4 + j) for j in range(4)] + ["SWT"],
                              writes=[("ps", bank)])
                        t1 = T1[g4 % 2]
                        bs_b = bass.AP(BSB.tensor, BSB.offset + pr * 128,
                                       [[int(v) for v in BSB.ap[0]], [0, 4], [1, 128]])
                        P.add("dve", lambda e, t1=t1, bank=bank, bs_b=bs_b, pr=pr: e.scalar_tensor_tensor(
                            out=t1.rearrange("p (j t) -> p j t", j=4),
                            in0=PSv[:, bank, :].rearrange("p (j t) -> p j t", j=4), scalar=GCOL[:, pr:pr + 1],
                            in1=bs_b, op0=ALU.mult, op1=ALU.add),
                            reads=[("ps", bank), ("BSB2", pr), "GCOL"], writes=[("T1", g4 % 2)])
                        P.add("pool", lambda e, t1=t1, g4=g4, pr=pr: e.tensor_tensor(
                            out=YTv[:, 2 + pr, g4 * 512:(g4 + 1) * 512], in0=t1,
                            in1=UT[:, g4 * 512:(g4 + 1) * 512], op=ALU.mult),
                            reads=[("T1", g4 % 2), ("UT", g4)], writes=[("YT", 2 + pr, g4)])

                QTs = [cv(("QTb", i), A0 + i * 12288, S, BF16) for i in range(2)]
                KTs = [cv(("KTb", i), A0 + i * 12288 + 4096, S, BF16) for i in range(2)]
                VPs = [cv(("VPb", i), A0 + i * 12288 + 8192, S, BF16).rearrange("p (t f) -> p t f", t=NT)
                       for i in range(2)]
                GTs = [cv(("GTb", 0), A0 + 24576, S, BF16), cv(("GTb", 1), A0 + 53248, S, BF16)]
                for gi in range(4):
                    P.set_alias(("GT", 0, gi), "arena", [(A0 + 24576 + gi * 1024, A0 + 24576 + (gi + 1) * 1024)])
                    P.set_alias(("GT", 1, gi), "arena", [(A0 + 53248 + gi * 1024, A0 + 53248 + (gi + 1) * 1024)])
                    for par in range(2):
                        for nm, off in (("QT", 0), ("KT", 4096), ("VP", 8192)):
                            o = A0 + par * 12288 + off + gi * 1024
                            P.set_alias((nm, par, gi), "arena", [(o, o + 1024)])
                EB = [cv(("E", i), A0 + 28672 + i * 4096, 1024, F32).rearrange("p (h n) -> p h n", h=2)
                      for i in range(3)]
                SPB = [cv(("SP", i), A0 + 40960 + i * 2048, 1024, BF16).rearrange("p (h n) -> p h n", h=2)
                       for i in range(2)]
                PB = cv("PB", A0 + 45056, 1024, F32).rearrange("p (h n) -> p h n", h=2)
                AB = [cv(("A", i), A0 + 49152 + i * 2048, 1024, BF16).rearrange("p (h n) -> p h n", h=2)
                      for i in range(2)]
                HT32 = HT.bitcast(F32)
                GBCh = HT32[:, 4096:4096 + D]
                P.set_alias("GBCh", "HT", [(16384, 16384 + 4 * D)])
                WOv = HT[:, 0:KC * D].rearrange("p (k f) -> p k f", k=KC)

                def piece_fm(slot, g4, bank, evac):
                    def run():
                        def mm(e):
                            inst = None
                            for kc in range(KC):
                                inst = e.matmul(PSv[:, bank, :], lhsT=WR[slot][:, kc * 128:(kc + 1) * 128],
                                                rhs=HTv[:, kc, g4 * 512:(g4 + 1) * 512],
                                                start=(kc == 0), stop=(kc == KC - 1))
                            return inst
                        P.add("pe", mm, reads=[("W", slot)] + [("HT", kc, g4) for kc in range(KC)],
                              writes=[("ps", bank)])
                        evac(bank, g4)
                    return run

                def piece_v(slot, g4, j, bank, par):
                    def run():
                        tt = g4 * 4 + j

                        def mm(e):
                            inst = None
                            for kc in range(KC):
                                inst = e.matmul(PSv[:, bank, j * 128:(j + 1) * 128],
                                                lhsT=HTv[:, kc, tt * 128:(tt + 1) * 128],
                                                rhs=WR[slot][:, kc * 128:(kc + 1) * 128],
                                                start=(kc == 0), stop=(kc == KC - 1))
                            return inst
                        P.add("pe", mm, reads=[("W", slot)] + [("HT", kc, g4) for kc in range(KC)],
                              writes=[("ps", bank)])
                        if j == 3:
                            P.add("dve", lambda e: e.tensor_copy(
                                out=VPs[par][:, g4 * 4:g4 * 4 + 4, :],
                                in_=PSv[:, bank, :].rearrange("p (j f) -> p j f", j=4)),
                                reads=[("ps", bank)], writes=[("VP", par, g4)])
                    return run

                def mk_ev_q(par):
                    def ev_q(bank, g4):
                        P.add("dve", lambda e: e.tensor_scalar(
                            out=QTs[par][:, g4 * 512:(g4 + 1) * 512], in0=PSv[:, bank, :], scalar1=0.125,
                            scalar2=None, op0=ALU.mult), reads=[("ps", bank)], writes=[("QT", par, g4)])
                    return ev_q

                def mk_ev_k(par):
                    def ev_k(bank, g4):
                        P.add("dve", lambda e: e.tensor_copy(
                            out=KTs[par][:, g4 * 512:(g4 + 1) * 512], in_=PSv[:, bank, :]),
                            reads=[("ps", bank)], writes=[("KT", par, g4)])
                    return ev_k

                def mk_ev_g(par):
                    def ev_g(bank, g4):
                        P.add("dve", lambda e: e.tensor_copy(
                            out=GTs[par][:, g4 * 512:(g4 + 1) * 512], in_=PSv[:, bank, :]),
                            reads=[("ps", bank)], writes=[("GT", par, g4)])
                    return ev_g

                def silu_gate(par):
                    P.add("act", lambda e: e.activation(out=GTs[par], in_=GTs[par], func=AF.Silu),
                          reads=[("GT", par, gi) for gi in range(4)], writes=[("GT", par, gi) for gi in range(4)])

                xb = [0, 1, 2, 3, 7]
                xctr = [0]

                def xbank():
                    b = xb[xctr[0] % len(xb)]
                    xctr[0] += 1
                    return b

                prefetch(widx + 4)
                for g4 in range(4):
                    piece_fm(slots[widx], g4, xbank(), mk_ev_g(0))()
                for g4 in range(4):
                    piece_fm(slots[widx + 1], g4, xbank(), mk_ev_q(0))()
                    piece_fm(slots[widx + 2], g4, xbank(), mk_ev_k(0))()
                    bv = xbank()
                    for j in range(4):
                        piece_v(slots[widx + 3], g4, j, bv, 0)()
                widx += 4
                steps = []
                for hp in range(4):
                    for c in range(4):
                        for kb in range(4 * c + 3, -1, -1):
                            r = kb - 4 * c
                            steps.append((hp, c, kb, max(r, 0) * 128, r >= 0, kb == 4 * c + 3, kb == 0))
                n = len(steps)
                NS = n // 4
                ZB = [(0, 1), (2, 3)]
                XB = (4, 5)
                OB6 = 6
                pending = []
                pstate = {"npieces": 1, "colacc": 0.0}

                def pair_begin(hp):
                    nonlocal_widx = widx_box[0]
                    par = hp % 2
                    if hp < 3:
                        prefetch(nonlocal_widx + 4)
                        npar = 1 - par
                        for g4 in range(4):
                            pending.append(piece_fm(slots[nonlocal_widx + 3], g4, 7, mk_ev_g(npar)))
                            pending.append(piece_fm(slots[nonlocal_widx], g4, 7, mk_ev_q(npar)))
                            pending.append(piece_fm(slots[nonlocal_widx + 1], g4, 7, mk_ev_k(npar)))
                            for j in range(4):
                                pending.append(piece_v(slots[nonlocal_widx + 2], g4, j, 7, npar))
                        widx_box[0] += 4
                    else:
                        for q in range(4):
                            P.dma("pool", lambda e, q=q, l=l: e.dma_start(
                                out=HT[:, q * 2048:(q + 1) * 2048],
                                in_=dap(wout_d, l * 128 * 8192 + q * 2048, [[8192, 128], [1, 2048]])),
                                writes=[("WOUT", q)])
                        P.dma("sp", lambda e, l=l, si=si: e.dma_start(
                            out=GBCh, in_=dap(modd, (l * 4 + si) * 3072 + 2048, [[0, 128], [1, D]])),
                            reads=[("modd", l, 4), ("modd", l, 5)], writes=["GBCh"])
                    pstate["npieces"] = max(len(pending), 1)
                    pstate["colacc"] = 0.0

                widx_box = [widx]

                def qk(i):
                    hp, c, kb, c0, diag, first, last = steps[i]
                    par = hp % 2
                    zb = ZB[i % 2]
                    QTl, KTl = QTs[par], KTs[par]

                    def f(e):
                        inst = None
                        for h in range(2):
                            inst = e.matmul(PSv[:, zb[h], c0:512],
                                            lhsT=KTl[h * 64:(h + 1) * 64, kb * 128:(kb + 1) * 128],
                                            rhs=QTl[h * 64:(h + 1) * 64, c * 512 + c0:(c + 1) * 512],
                                            start=True, stop=not diag)
                        if diag:
                            for h in range(2):
                                inst = e.matmul(PSv[:, zb[h], c0:c0 + 128], lhsT=IDB, rhs=MNEG,
                                                start=False, stop=True)
                        return inst
                    P.add("pe", f, reads=[("QT", par, c), ("KT", par, kb // 4), "CB"],
                          writes=[("ps", zb[0]), ("ps", zb[1])])

                def exp_ln(i):
                    hp, c, kb, c0, diag, first, last = steps[i]
                    zb = ZB[i % 2]
                    eb, sb_ = EB[i % 3], SPB[i % 2]
                    P.add("act", lambda e: e.activation(out=eb[:, :, c0:512],
                                                        in_=PSv[:, zb[0]:zb[0] + 2, c0:512], func=AF.Exp),
                          reads=[("ps", zb[0]), ("ps", zb[1])], writes=[("E", i % 3)])
                    P.add("act", lambda e: e.activation(out=sb_[:, :, c0:512], in_=eb[:, :, c0:512],
                                                        func=AF.Ln, bias=1.0),
                          reads=[("E", i % 3)], writes=[("SP", i % 2)])

                def umm(i):
                    hp, c, kb, c0, diag, first, last = steps[i]
                    sb_ = SPB[i % 2]

                    def f(e):
                        inst = None
                        if first:
                            for h in range(2):
                                inst = e.matmul(PSv[:, XB[h], :], lhsT=ZER[:, 0:128], rhs=ZER[:, :],
                                                start=True, stop=True)
                        for h in range(2):
                            inst = e.matmul(PSv[:, XB[h], c0:512], lhsT=UNEG, rhs=sb_[:, h, c0:512],
                                            start=False, stop=True, skip_group_check=True)
                        return inst
                    P.add("pe", f, reads=[("SP", i % 2), "CB", "ZER"], writes=[("ps", XB[0]), ("ps", XB[1])])

                def expx(i):
                    hp, c, kb, c0, diag, first, last = steps[i]
                    P.add("act", lambda e: e.activation(out=PB[:, :, c0:512],
                                                        in_=PSv[:, XB[0]:XB[0] + 2, c0:512], func=AF.Exp),
                          reads=[("ps", XB[0]), ("ps", XB[1])], writes=["PB"])

                def lmm(i):
                    hp, c, kb, c0, diag, first, last = steps[i]
                    if last:
                        return
                    sb_ = SPB[i % 2]

                    def f(e):
                        inst = None
                        for h in range(2):
                            inst = e.matmul(PSv[:, XB[h], c0:512], lhsT=LNEG, rhs=sb_[:, h, c0:512],
                                            start=False, stop=True, skip_group_check=True)
                        return inst
                    P.add("pe", f, reads=[("SP", i % 2), "CB"], writes=[("ps", XB[0]), ("ps", XB[1])])

                def mul(i):
                    hp, c, kb, c0, diag, first, last = steps[i]
                    eb, ab = EB[i % 3], AB[i % 2]
                    P.add("dve", lambda e: e.tensor_tensor(out=ab[:, :, c0:512], in0=eb[:, :, c0:512],
                                                           in1=PB[:, :, c0:512], op=ALU.mult),
                          reads=[("E", i % 3), "PB"], writes=[("A", i % 2)])

                def av(i):
                    hp, c, kb, c0, diag, first, last = steps[i]
                    par = hp % 2
                    ab = AB[i % 2]
                    ob = OB6
                    VPl, GTl = VPs[par], GTs[par]

                    def f(e):
                        inst = None
                        if first:
                            inst = e.matmul(PSv[:, ob, :], lhsT=ZER[:, 0:128], rhs=ZER[:, :],
                                            start=True, stop=True)
                        for h in range(2):
                            inst = e.matmul(PSv[h * 64:(h + 1) * 64, ob, c0:512],
                                            lhsT=VPl[:, kb, h * 64:(h + 1) * 64], rhs=ab[:, h, c0:512],
                                            start=False, stop=True, skip_group_check=True)
                        return inst
                    P.add("pe", f, reads=[("A", i % 2), ("VP", par, kb // 4), "ZER"], writes=[("ps", ob)])
                    if last:
                        P.add("dve", lambda e: e.tensor_tensor(
                            out=YTv[:, 4 + hp, c * 512:(c + 1) * 512], in0=PSv[:, ob, :],
                            in1=GTl[:, c * 512:(c + 1) * 512], op=ALU.mult),
                            reads=[("ps", ob), ("GT", par, c)], writes=[("YT", 4 + hp, c)])

                silu_gate(0)
                pair_begin(0)
                qk(0)
                exp_ln(0)
                umm(0)
                qk(1)
                for i in range(n):
                    hpc = steps[i][0]
                    lstep = i - hpc * NS
                    if lstep == 0 and hpc > 0:
                        assert not pending
                        pair_begin(hpc)
                    if i + 1 < n:
                        exp_ln(i + 1)
                    if i + 2 < n:
                        qk(i + 2)
                    pstate["colacc"] += (512 - steps[i][3])
                    thr = 14500.0 / pstate["npieces"]
                    while pending and pstate["colacc"] >= thr:
                        pstate["colacc"] -= thr
                        pending.pop(0)()
                    if lstep == 34:
                        while pending:
                            pending.pop(0)()
                    if lstep == 35 and hpc < 3:
                        silu_gate((hpc + 1) % 2)
                    expx(i)
                    mul(i)
                    lmm(i)
                    if i + 1 < n:
                        umm(i + 1)
                    if i >= 1:
                        av(i - 1)
                    if hpc == 3 and lstep == 19:
                        P.add("dve", lambda e: e.tensor_scalar(out=GBCh, in0=GBCh, scalar1=float(1.0 / DN_ALPHA),
                                                               scalar2=None, op0=ALU.mult),
                              reads=["GBCh"], writes=["GBCh"])
                    if hpc == 3 and 20 <= lstep < 28:
                        kcw = lstep - 20
                        P.add("dve", lambda e, kcw=kcw: e.tensor_tensor(
                            out=WOv[:, kcw, :], in0=WOv[:, kcw, :], in1=GBCh, op=ALU.mult),
                            reads=["GBCh", ("WOUT", kcw // 2)], writes=[("WOUT", kcw // 2)])
                assert not pending
                av(n - 1)
                widx = widx_box[0]

                bank_ctr[0] = 0
                LNGB = cv("LNGB", A0 + 4096, D, F32)
                LNBB = cv("LNBB", A0 + 8192, D, F32)
                RB = [cv(("RB", i), A0 + 12288 + i * 4096, D, F32) for i in range(3)]
                P.dma("sp", lambda e, l=l: e.dma_start(out=LNGB, in_=dap(lng_d, l * D, [[0, 128], [1, D]])),
                      writes=["LNGB"])
                P.dma("sp", lambda e, l=l: e.dma_start(out=LNBB, in_=dap(lnb_d, l * D, [[0, 128], [1, D]])),
                      writes=["LNBB"])
                prev_fin = [None]
                def stage1(tt):
                    banks = (next_bank(), next_bank())
                    assert banks[1] == banks[0] + 1
                    for hf in range(2):
                        def mm(e, tt=tt, hf=hf, bank=banks[hf]):
                            inst = None
                            for kc in range(KC):
                                inst = e.matmul(PSv[:, bank, :], lhsT=YTv[:, kc, tt * 128:(tt + 1) * 128],
                                                rhs=WOv[:, kc, hf * 512:(hf + 1) * 512],
                                                start=(kc == 0), stop=(kc == KC - 1))
                            return inst
                        P.add("pe", mm, reads=YT_all + WOUT_all, writes=[("ps", banks[hf])])
                    rb = RB[tt % 3]
                    k2 = tt % 3
                    b0 = banks[0]
                    P.add("dve", lambda e: e.tensor_tensor(
                        out=rb.rearrange("p (h n) -> p h n", h=2), in0=PSv[:, b0:b0 + 2, :],
                        in1=XSv[:, tt, :].rearrange("p (h n) -> p h n", h=2), op=ALU.add),
                        reads=[("ps", banks[0]), ("ps", banks[1]), ("XS", tt)], writes=[("RB", k2)])
                    BNS2 = STAT[:, 214 + k2 * 12: 214 + (k2 + 1) * 12]
                    MV2 = STAT[:, 192 + k2 * 2: 192 + (k2 + 1) * 2]
                    RS2 = STAT[:, 200 + k2: 201 + k2]
                    for hf in range(2):
                        P.add("dve", lambda e, hf=hf: e.bn_stats(
                            out=BNS2[:, hf * 6:(hf + 1) * 6], in_=rb[:, hf * 512:(hf + 1) * 512]),
                            reads=[("RB", k2)], writes=[("BNS2", k2)])
                    P.add("dve", lambda e: e.bn_aggr(out=MV2, in_=BNS2),
                          reads=[("BNS2", k2)], writes=[("MV2", k2)])
                    P.add("act", lambda e: e.activation(out=RS2, in_=MV2[:, 1:2], func=AF.Ln,
                                                        bias=float(LN_EPS / DN_ALPHA ** 2)),
                          reads=[("MV2", k2)], writes=[("RS2", k2)])
                    P.add("act", lambda e: e.activation(out=RS2, in_=RS2, func=AF.Exp, scale=-0.5),
                          reads=[("RS2", k2)], writes=[("RS2", k2)])

                def stage2(tt):
                    rb = RB[tt % 3]
                    k2 = tt % 3
                    MV2 = STAT[:, 192 + k2 * 2: 192 + (k2 + 1) * 2]
                    RS2 = STAT[:, 200 + k2: 201 + k2]
                    NB2 = STAT[:, 204 + k2: 205 + k2]
                    P.add("dve", lambda e: e.scalar_tensor_tensor(
                        out=NB2, in0=MV2[:, 0:1], scalar=-1.0, in1=RS2,
                        op0=ALU.mult, op1=ALU.mult), reads=[("MV2", k2), ("RS2", k2)], writes=[("NB2", k2)])
                    P.add("act", lambda e: e.activation(
                        out=rb, in_=rb, func=AF.Identity, scale=RS2, bias=NB2),
                        reads=[("RB", k2), ("RS2", k2), ("NB2", k2)], writes=[("RB", k2)])
                    P.add("pool", lambda e: e.tensor_tensor(out=rb, in0=rb, in1=LNGB, op=ALU.mult),
                          reads=[("RB", k2), "LNGB"], writes=[("RB", k2)])

                def stage3(tt, si=si):
                    rb = RB[tt % 3]
                    k2 = tt % 3
                    P.add("dve", lambda e: e.tensor_tensor(out=XSv[:, tt, :], in0=rb, in1=LNBB, op=ALU.add),
                          reads=[("RB", k2), "LNBB"], writes=[("XS", tt)])
                    if li == nlayers - 1:
                        od = P.dma("sp", lambda e: e.dma_start(
                            out=dap(out_d, (si * S + tt * 128) * D, [[D, 128], [1, D]]), in_=XSv[:, tt, :]),
                            reads=[("XS", tt)])
                        out_dmas.append(od)

                for tt in range(NT + 2):
                    if tt < NT:
                        stage1(tt)
                    if 0 <= tt - 1 < NT:
                        stage2(tt - 1)
                    if 0 <= tt - 2 < NT:
                        stage3(tt - 2)

        fin = P.add("sp", None)
        fin.deps = set(out_dmas)

        with nc.Block() as block:
            emit(nc, P, block, sems, {"sp": ring_sp, "pool": ring_pool})
    return nc


def _consts():
    j = np.arange(128)[:, None]
    s = np.arange(128)[None, :]
    uneg = np.where(j >= s, -1.0, 0.0)
    lneg = np.where(j < s, -1.0, 0.0)
    mneg = np.where(j >= s, -30000.0, 0.0)
    idb = np.eye(128)
    cb = np.concatenate([uneg, lneg, mneg, idb, np.ones((128, 128))], axis=1).astype(np.float32).astype(ml_dtypes.bfloat16)
    invc = np.zeros((128, 2 * PAD), np.float32)
    invw = np.zeros((128, 2), np.float32)
    for ch in range(2):
        for p in range(128):
            win = WINS[2 * ch + p // 64]
            invw[p, ch] = 1.0 / win
            for t in range(PAD):
                invc[p, ch * PAD + t] = 1.0 / min(t + 1, win)
    return cb, invc, invw


def _prep_shared(w_in, pool_w, pool_scale, sgu_ln_g, sgu_ln_b, sgu_w, sgu_b, w_out, ada_w, ada_b, ln_g, ln_b):
    f = lambda a: np.ascontiguousarray(np.asarray(a, dtype=np.float32))
    w_in, w_out, ada_w = f(w_in), f(w_out), f(ada_w)
    cb, invc, invw = _consts()
    sh = {}
    sh["w_in_b"] = np.ascontiguousarray(
        w_in.reshape(DEPTH, KC, 128, 26, 128).transpose(0, 3, 2, 1, 4)).reshape(DEPTH, 26, 128, 1024)
    sh["w_out_b"] = np.ascontiguousarray(
        w_out.reshape(DEPTH, KC, 128, D).transpose(0, 2, 1, 3)).reshape(DEPTH, 128, 8192)
    sh["ada_w_b"] = np.ascontiguousarray(
        ada_w.reshape(DEPTH, KC, 128, 6, 512).transpose(0, 3, 2, 1, 4)).reshape(DEPTH, 6, 128, 4096)
    sh["ada_b"] = f(ada_b)
    pw = f(pool_w)
    sh["pool_w_b"] = np.ascontiguousarray(
        pw.reshape(DEPTH, 2, 2, 64, 64).transpose(0, 2, 3, 1, 4)).reshape(DEPTH, 128, 128)
    sh["pool_sc"] = np.ascontiguousarray(f(pool_scale).reshape(DEPTH, 2, 128).transpose(0, 2, 1))
    sh["sgu_g"] = f(sgu_ln_g)
    sh["sgu_bln"] = f(sgu_ln_b)
    sh["sgu_wT"] = np.ascontiguousarray(f(sgu_w).transpose(0, 3, 1, 2)).reshape(DEPTH, 128, 512)
    sh["sgu_bias"] = f(sgu_b)
    sh["ln_g"] = f(ln_g)
    sh["ln_b"] = f(ln_b)
    sh["ident"] = np.eye(128, dtype=np.float32)
    sh["cst_bf"] = cb
    sh["invc"] = invc
    sh["invw"] = invw
    return sh


FUSED = True


def kernel(x, c, w_in, pool_w, pool_scale, sgu_ln_g, sgu_ln_b, sgu_w, sgu_b,
           w_out, ada_w, ada_b, ln_g, ln_b):
    x = np.asarray(x, dtype=np.float32)
    c = np.asarray(c, dtype=np.float32)
    sh = _prep_shared(w_in, pool_w, pool_scale, sgu_ln_g, sgu_ln_b, sgu_w, sgu_b, w_out, ada_w, ada_b, ln_g, ln_b)
    cTs = []
    for i in range(NCORES):
        cc = c[i * NSEQ:(i + 1) * NSEQ]
        cTs.append(np.ascontiguousarray(cc.reshape(NSEQ, KC, 128).transpose(2, 1, 0)).reshape(128, KC * NSEQ))

    def run(nc, xin):
        in_maps = []
        for i in range(NCORES):
            m = dict(sh)
            m["x"] = np.ascontiguousarray(xin[i * NSEQ:(i + 1) * NSEQ])
            m["cT"] = cTs[i]
            in_maps.append(m)
        res = run_bass_kernel_spmd(nc, in_maps, core_ids=list(range(NCORES)))
        return np.concatenate([np.asarray(r["out"]) for r in res.results], axis=0)

    if FUSED:
        return run(build_nc(DEPTH, 0), x)
    y = x
    for l in range(DEPTH):
        y = run(build_nc(1, l), y)
    return y
```

```python
import contextlib
import numpy as np
import ml_dtypes
import concourse.bass as bass
import concourse.mybir as mybir
from concourse.bass_utils import run_bass_kernel_spmd

F32 = mybir.dt.float32
BF16 = mybir.dt.bfloat16
AF = mybir.ActivationFunctionType
ALU = mybir.AluOpType

NCORES = 8
NSEQ = 4
S = 2048
D = 1024
NT = 16
KC = 8
DEPTH = 2
LN_EPS = 1e-5
DN_ALPHA = (2 * DEPTH) ** 0.25
WINS = (2, 4, 8, 16)
PAD = 16

ARENA_BYTES = 64 * 1024


class Op:
    __slots__ = ("eng", "fn", "deps", "kind", "signal", "seq", "sem", "val", "idx")


class Prog:
    def __init__(self):
        self.ops = []
        self.res_w = {}
        self.res_r = {}
        self.alias = {}
        self.by_space = {}

    def set_alias(self, name, space, ranges):
        if name in self.alias:
            assert self.alias[name] == (space, ranges), name
            return
        self.alias[name] = (space, ranges)
        self.by_space.setdefault(space, []).append(name)

    def _overlaps(self, name):
        a = self.alias.get(name)
        if a is None:
            return ()
        space, rs = a
        out = []
        for m in self.by_space[space]:
            if m == name:
                continue
            for (lo, hi) in self.alias[m][1]:
                if any(lo < h2 and l2 < hi for (l2, h2) in rs):
                    out.append(m)
                    break
        return out

    def _add(self, eng, fn, reads, writes, kind):
        op = Op()
        op.eng, op.fn, op.kind = eng, fn, kind
        op.signal = False
        op.seq = op.sem = op.val = None
        op.idx = len(self.ops)
        deps = set()
        for r in reads:
            w = self.res_w.get(r)
            if w is not None:
                deps.add(w)
        for r in writes:
            w = self.res_w.get(r)
            if w is not None:
                deps.add(w)
            for rd in self.res_r.get(r, ()):
                deps.add(rd)
        for r in reads:
            self.res_r.setdefault(r, []).append(op)
        for r in writes:
            self.res_w[r] = op
            self.res_r[r] = []
        for r in list(reads) + list(writes):
            for m in self._overlaps(r):
                w = self.res_w.get(m)
                if w is not None:
                    deps.add(w)
                for rd in self.res_r.get(m, ()):
                    deps.add(rd)
        deps.discard(op)
        op.deps = deps
        self.ops.append(op)
        return op

    def add(self, eng, fn, reads=(), writes=()):
        return self._add(eng, fn, reads, writes, "c")

    def dma(self, queue, fn, reads=(), writes=()):
        return self._add(queue, fn, reads, writes, "d")

    def finalize(self):
        for op in self.ops:
            nd = set()
            for d in op.deps:
                if d.kind == "c" and op.kind == "c" and d.eng == "pe" and op.eng == "pe":
                    continue
                nd.add(d)
            op.deps = nd
            for d in nd:
                d.signal = True


def emit(nc, prog, block, sems, dma_sems):
    prog.finalize()
    seqc = {e: 0 for e in sems}
    dcount = {q: 0 for q in dma_sems}
    dprev = {}
    for op in prog.ops:
        if op.kind == "c":
            if op.signal:
                seqc[op.eng] += 1
                op.seq = seqc[op.eng]
        else:
            ring = dma_sems[op.eng]
            i = dcount[op.eng]
            dcount[op.eng] += 1
            op.sem = ring[i % len(ring)]
            op.val = 16 * (i // len(ring) + 1)
    lists = {e: [] for e in sems}
    for op in prog.ops:
        lists[op.eng].append(op)

    def body(engname):
        def f(e):
            waited = {}

            def wait(sem, val):
                k = id(sem)
                if waited.get(k, 0) >= val:
                    return
                waited[k] = val
                e.wait_ge(sem, val)

            for op in lists[engname]:
                for d in sorted(op.deps, key=lambda o: o.idx):
                    if d.kind == "c":
                        wait(sems[d.eng], d.seq)
                    else:
                        wait(d.sem, d.val)
                if op.kind == "d":
                    if op.val > 16:
                        wait(op.sem, op.val - 16)
                    inst = op.fn(e)
                    inst.then_inc(op.sem, 16)
                else:
                    if op.fn is None:
                        continue
                    inst = op.fn(e)
                    if op.signal:
                        inst.then_inc(sems[op.eng], 1)
        return f

    block.sync(body("sp"))
    block.gpsimd(body("pool"))
    block.tensor(body("pe"))
    block.vector(body("dve"))
    block.scalar(body("act"))


def build_nc(nlayers=DEPTH, layer0=0, nseq=NSEQ):
    nc = bass.Bass("TRN2", target_bir_lowering=False)
    dr = lambda name, shape, dt=F32: nc.dram_tensor(name, list(shape), dt, kind="ExternalInput")
    x_d = dr("x", [nseq, S, D])
    cT_d = dr("cT", [128, KC * 4])
    win_d = dr("w_in_b", [DEPTH, 26, 128, 1024])
    wout_d = dr("w_out_b", [DEPTH, 128, 8192])
    ada_d = dr("ada_w_b", [DEPTH, 6, 128, 4096])
    adab_d = dr("ada_b", [DEPTH, 3072])
    pw_d = dr("pool_w_b", [DEPTH, 128, 128])
    psc_d = dr("pool_sc", [DEPTH, 128, 2])
    sgg_d = dr("sgu_g", [DEPTH, 256])
    sgb_d = dr("sgu_bln", [DEPTH, 256])
    swt_d = dr("sgu_wT", [DEPTH, 128, 512])
    sbias_d = dr("sgu_bias", [DEPTH, 4, 128])
    lng_d = dr("ln_g", [DEPTH, 1024])
    lnb_d = dr("ln_b", [DEPTH, 1024])
    ident_d = dr("ident", [128, 128])
    cb_d = dr("cst_bf", [128, 640], BF16)
    invc_d = dr("invc", [128, 2 * PAD])
    invw_d = dr("invw", [128, 2])
    out_d = nc.dram_tensor("out", [nseq, S, D], F32, kind="ExternalOutput")
    modd = nc.dram_tensor("modd", [DEPTH, 4, 3072], F32, kind="Internal")

    def dap(t, offset, pat):
        return bass.AP(t, offset, [list(p) for p in pat])

    P = Prog()

    with contextlib.ExitStack() as es:
        sb = lambda name, shape, dt: es.enter_context(nc.sbuf_tensor(name, shape, dt))
        XS = sb("XS", [128, NT * D], F32)
        HT = sb("HT", [128, KC * S], BF16)
        YT = sb("YT", [128, KC * S], BF16)
        ARb = sb("arena", [128, ARENA_BYTES // 2], BF16)
        IDENT = sb("IDENT", [128, 128], F32)
        CB = sb("CB", [128, 640], BF16)
        ZER = sb("ZER", [128, 512], BF16)
        MODT = sb("MODT", [128, DEPTH * 4 * 16], F32)
        PW = sb("PW", [128, DEPTH * 128], BF16)
        PSC = sb("PSC", [128, DEPTH * 2], F32)
        SWT = sb("SWT", [128, DEPTH * 512], BF16)
        INVC = sb("INVC", [128, 2 * PAD], F32)
        INVW = sb("INVW", [128, 2], F32)
        STAT = sb("STAT", [128, 256], F32)
        PS = es.enter_context(nc.psum_tensor("PS", [128, 4096], F32))
        sem = lambda name: es.enter_context(nc.semaphore(name))
        sems = {"pe": sem("s_pe"), "act": sem("s_act"), "dve": sem("s_dve"),
                "pool": sem("s_pool"), "sp": sem("s_sp")}
        ring_sp = [sem(f"dsp{i}") for i in range(20)]
        ring_pool = [sem(f"dpl{i}") for i in range(8)]
        AR32 = ARb.bitcast(F32)

        def cv(name, off, n, dt):
            size = 4 if dt == F32 else 2
            assert off % size == 0 and off + size * n <= ARENA_BYTES, (name, off, n)
            P.set_alias(name, "arena", [(off, off + size * n)])
            if dt == F32:
                return AR32[:, off // 4: off // 4 + n]
            return ARb[:, off // 2: off // 2 + n]

        XSv = XS[:].rearrange("p (t f) -> p t f", t=NT)
        HTv = HT[:].rearrange("p (k t) -> p k t", k=KC)
        YTv = YT[:].rearrange("p (k t) -> p k t", k=KC)
        PSv = PS[:].rearrange("p (b n) -> p b n", b=8)
        UNEG = CB[:, 0:128]
        LNEG = CB[:, 128:256]
        MNEG = CB[:, 256:384]
        IDB = CB[:, 384:512]
        ONESB = CB[:, 512:640]
        for kc in range(KC):
            for g4 in range(4):
                o = (kc * S + g4 * 512) * 2
                P.set_alias(("HT", kc, g4), "HT", [(o, o + 1024)])
                P.set_alias(("YT", kc, g4), "YT", [(o, o + 1024)])
        for q in range(4):
            P.set_alias(("WOUT", q), "HT", [(q * 4096, (q + 1) * 4096)])
        P.set_alias("DT", "YT", [(7 * S * 2, 8 * S * 2)])
        for t in range(NT):
            P.set_alias(("VLN", t), "YT", [((5 + hf) * S * 2 + t * 256, (5 + hf) * S * 2 + (t + 1) * 256)
                                           for hf in range(2)])
        HT_all = [("HT", kc, g4) for kc in range(KC) for g4 in range(4)]
        WOUT_all = [("WOUT", q) for q in range(4)]
        YT_all = [("YT", chn, g4) for chn in range(8) for g4 in range(4)]

        bank_ctr = [0]

        def next_bank():
            b = bank_ctr[0] % 8
            bank_ctr[0] += 1
            return b

        WR = [cv(("W", i), i * 2048, 1024, BF16) for i in range(4)]
        A0 = 4 * 2048
        wring_ctr = [0]

        def load_wblock(l, blk):
            slot = wring_ctr[0] % 4
            wring_ctr[0] += 1
            src = dap(win_d, ((l * 26 + blk) * 128) * 1024, [[1024, 128], [1, 1024]])
            P.dma("pool", lambda e, slot=slot, src=src: e.dma_start(out=WR[slot], in_=src),
                  writes=[("W", slot)])
            return slot

        P.dma("sp", lambda e: e.dma_start(out=IDENT[:], in_=ident_d.ap()), writes=["IDENT"])
        P.dma("sp", lambda e: e.dma_start(out=CB[:], in_=cb_d.ap()), writes=["CB"])
        P.dma("sp", lambda e: e.dma_start(out=INVC[:], in_=invc_d.ap()), writes=["INVC"])
        P.dma("sp", lambda e: e.dma_start(out=INVW[:], in_=invw_d.ap()), writes=["INVW"])
        P.dma("sp", lambda e: e.dma_start(out=PSC[:].rearrange("p (l c) -> p l c", l=DEPTH),
                                          in_=psc_d.ap().rearrange("l p c -> p l c")),
              writes=["PSC"])
        P.dma("pool", lambda e: e.dma_start(out=PW[:].rearrange("p (l c) -> p l c", l=DEPTH),
                                            in_=pw_d.ap().rearrange("l p c -> p l c")),
              writes=["PW"])
        P.dma("pool", lambda e: e.dma_start(out=SWT[:].rearrange("p (l c) -> p l c", l=DEPTH),
                                            in_=swt_d.ap().rearrange("l p c -> p l c")),
              writes=["SWT"])
        P.add("dve", lambda e: e.memset(ZER[:], 0.0), writes=["ZER"])
        SWTv = SWT[:].rearrange("p (a t) -> p a t", t=128)
        P.add("dve", lambda e: e.memset(SWTv[64:128, :, 0:64], 0.0), reads=["SWT"], writes=["SWT"])

        SC = cv("SC", A0, 32, F32)
        ADAB = cv("ADAB", A0 + 128, 3072, F32)
        MR = [cv(("MR", i), A0 + 128 + 12288 + i * 2048, 512, F32) for i in range(2)]
        ADA_OFF = A0 + 128 + 12288 + 4096
        ADA = [cv(("ADA", i), ADA_OFF + i * 8192, 2048, F32) for i in range(4)]

        P.dma("sp", lambda e: e.dma_start(out=SC, in_=cT_d.ap()), writes=["SC"])
        P.add("act", lambda e: e.activation(out=SC, in_=SC, func=AF.Silu), reads=["SC"], writes=["SC"])
        adactr = 0
        for l in range(DEPTH):
            P.dma("sp", lambda e, l=l: e.dma_start(
                out=ADAB[0:4, :], in_=dap(adab_d, l * 3072, [[0, 4], [1, 3072]])),
                writes=["ADAB"])
            for cbk in range(6):
                bank = next_bank()
                for half in range(2):
                    buf = adactr % 4
                    adactr += 1
                    src = dap(ada_d, ((l * 6 + cbk) * 128) * 4096 + half * 2048,
                              [[4096, 128], [1, 2048]])
                    P.dma("sp", lambda e, buf=buf, src=src: e.dma_start(out=ADA[buf], in_=src),
                          writes=[("ADA", buf)])

                    def mm(e, buf=buf, half=half, bank=bank):
                        inst = None
                        for k4 in range(4):
                            kc = half * 4 + k4
                            inst = e.matmul(PSv[0:4, bank, :], lhsT=SC[:, kc * 4:(kc + 1) * 4],
                                            rhs=ADA[buf][:, k4 * 512:(k4 + 1) * 512],
                                            start=(kc == 0), stop=(kc == 7))
                        return inst
                    P.add("pe", mm, reads=[("ADA", buf), "SC"], writes=[("ps", bank)])
                mr = MR[cbk % 2]
                P.add("dve", lambda e, mr=mr, bank=bank, cbk=cbk: e.tensor_tensor(
                    out=mr[0:4, :], in0=PSv[0:4, bank, :], in1=ADAB[0:4, cbk * 512:(cbk + 1) * 512],
                    op=ALU.add), reads=[("ps", bank), "ADAB"], writes=[("MR", cbk % 2)])
                P.dma("sp", lambda e, mr=mr, l=l, cbk=cbk: e.dma_start(
                    out=dap(modd, l * 4 * 3072 + cbk * 512, [[3072, 4], [1, 512]]), in_=mr[0:4, :]),
                    reads=[("MR", cbk % 2)], writes=[("modd", l, cbk)])
        for l in range(DEPTH):
            for b in range(4):
                P.dma("sp", lambda e, l=l, b=b: e.dma_start(
                    out=MODT[:, (l * 4 + b) * 16:(l * 4 + b + 1) * 16],
                    in_=dap(modd, (l * 4 + b) * 3072, [[1, 128], [128, 16]]),
                    allow_slow_non_contiguous=True),
                    reads=[("modd", l, cbk) for cbk in range(4)], writes=[("MODTld", l, b)])
        MODTv = MODT[:].rearrange("p (a j) -> p a j", j=16)
        P.add("dve", lambda e: e.tensor_scalar(out=MODTv[:, :, 8:16], in0=MODTv[:, :, 8:16],
                                               scalar1=1.0, scalar2=None, op0=ALU.add),
              reads=[("MODTld", l, b) for l in range(DEPTH) for b in range(4)], writes=["MODT"])

        out_dmas = []
        evac_ctr = [0]

        for si in range(nseq):
            for tt in range(NT):
                P.dma("sp", lambda e, si=si, tt=tt: e.dma_start(
                    out=XSv[:, tt, :], in_=dap(x_d, (si * S + tt * 128) * D, [[D, 128], [1, D]])),
                    writes=[("XS", tt)])
            for li in range(nlayers):
                l = layer0 + li
                mcol = (l * 4 + si) * 16
                blk_order = [0, 2, 1, 3, 6, 7, 8, 4, 9, 5]
                for hp in range(4):
                    if hp == 0:
                        blk_order += [22 + hp, 10 + hp, 14 + hp, 18 + hp]
                    else:
                        blk_order += [10 + hp, 14 + hp, 18 + hp, 22 + hp]
                slots = {}
                nload = [0]

                def prefetch(upto, l=l, slots=slots, nload=nload, blk_order=blk_order):
                    while nload[0] < min(upto, len(blk_order)):
                        slots[nload[0]] = load_wblock(l, blk_order[nload[0]])
                        nload[0] += 1
                prefetch(2)

                for g4 in range(4):
                    for kc in range(KC):
                        bank = next_bank()

                        def tr(e, g4=g4, kc=kc, bank=bank):
                            inst = None
                            for j in range(4):
                                inst = e.transpose(out=PSv[:, bank, j * 128:(j + 1) * 128],
                                                   in_=XSv[:, g4 * 4 + j, kc * 128:(kc + 1) * 128],
                                                   identity=IDENT[:])
                            return inst
                        P.add("pe", tr, reads=[("XS", g4 * 4 + j) for j in range(4)] + ["IDENT"],
                              writes=[("ps", bank)])
                        dst = HTv[:, kc, g4 * 512:(g4 + 1) * 512]
                        sc_ap = MODT[:, mcol + 8 + kc: mcol + 9 + kc]
                        sh_ap = MODT[:, mcol + kc: mcol + kc + 1]
                        if evac_ctr[0] % 2 == 0:
                            P.add("dve", lambda e, dst=dst, bank=bank, sc_ap=sc_ap, sh_ap=sh_ap:
                                  e.tensor_scalar(out=dst, in0=PSv[:, bank, :], scalar1=sc_ap,
                                                  scalar2=sh_ap, op0=ALU.mult, op1=ALU.add),
                                  reads=[("ps", bank), "MODT"], writes=[("HT", kc, g4)])
                        else:
                            P.add("act", lambda e, dst=dst, bank=bank, sc_ap=sc_ap, sh_ap=sh_ap:
                                  e.activation(out=dst, in_=PSv[:, bank, :], func=AF.Identity,
                                               scale=sc_ap, bias=sh_ap),
                                  reads=[("ps", bank), "MODT"], writes=[("HT", kc, g4)])
                        evac_ctr[0] += 1

                def proj_fm(widx, evac, slots=slots, prefetch=prefetch):
                    slot = slots[widx]
                    prefetch(widx + 3)
                    for g4 in range(4):
                        bank = next_bank()

                        def mm(e, slot=slot, g4=g4, bank=bank):
                            inst = None
                            for kc in range(KC):
                                inst = e.matmul(PSv[:, bank, :], lhsT=WR[slot][:, kc * 128:(kc + 1) * 128],
                                                rhs=HTv[:, kc, g4 * 512:(g4 + 1) * 512],
                                                start=(kc == 0), stop=(kc == KC - 1))
                            return inst
                        P.add("pe", mm, reads=[("W", slot)] + [("HT", kc, g4) for kc in range(KC)],
                              writes=[("ps", bank)])
                        evac(bank, g4)

                def proj_tm(widx, evac, group=4, slots=slots, prefetch=prefetch):
                    slot = slots[widx]
                    prefetch(widx + 3)
                    for t0 in range(0, NT, group):
                        bank = next_bank()

                        def mm(e, slot=slot, t0=t0, bank=bank):
                            inst = None
                            for j in range(group):
                                tt = t0 + j
                                for kc in range(KC):
                                    inst = e.matmul(PSv[:, bank, j * 128:(j + 1) * 128],
                                                    lhsT=HTv[:, kc, tt * 128:(tt + 1) * 128],
                                                    rhs=WR[slot][:, kc * 128:(kc + 1) * 128],
                                                    start=(kc == 0), stop=(kc == KC - 1))
                            return inst
                        P.add("pe", mm, reads=[("W", slot)] + HT_all, writes=[("ps", bank)])
                        evac(bank, t0, group)

                W2 = S + PAD
                AT = cv("AT", A0, W2, F32)
                B1 = cv("B1", A0 + 4 * W2, W2, F32)
                B2 = cv("B2", A0 + 8 * W2, W2, F32)
                SGA = cv("SGA", A0 + 12 * W2, S, F32)
                DT = YTv[:, 7, :]
                for bufname, buf in (("AT", AT), ("B1", B1), ("B2", B2)):
                    P.add("pool", lambda e, buf=buf: e.memset(buf[:, 0:PAD], 0.0), writes=[bufname])
                widx = 0
                sh = lambda buf, k: buf[:, PAD - k: PAD - k + S]
                for ch in range(2):
                    def ev_a(bank, g4):
                        P.add("act", lambda e, bank=bank, g4=g4: e.activation(
                            out=AT[:, PAD + g4 * 512: PAD + (g4 + 1) * 512], in_=PSv[:, bank, :],
                            func=AF.Identity), reads=[("ps", bank)], writes=["AT"])
                    proj_fm(widx, ev_a)
                    widx += 1

                    def ev_g(bank, g4):
                        P.add("act", lambda e, bank=bank, g4=g4: e.activation(
                            out=SGA[:, g4 * 512:(g4 + 1) * 512], in_=PSv[:, bank, :],
                            func=AF.Silu), reads=[("ps", bank)], writes=["SGA"])
                    proj_fm(widx, ev_g)
                    widx += 1
                    P.add("pool", lambda e: e.tensor_tensor(out=B1[:, PAD:], in0=AT[:, PAD:], in1=sh(AT, 1),
                                                            op=ALU.add), reads=["AT"], writes=["B1"])
                    if ch == 0:
                        P.add("pool", lambda e: e.tensor_tensor(out=B2[64:128, PAD:], in0=B1[64:128, PAD:],
                                                                in1=sh(B1, 2)[64:128], op=ALU.add),
                              reads=["B1"], writes=["B2"])
                    else:
                        P.add("pool", lambda e: e.tensor_tensor(out=B2[:, PAD:], in0=B1[:, PAD:], in1=sh(B1, 2),
                                                                op=ALU.add), reads=["B1"], writes=["B2"])
                        P.add("pool", lambda e: e.tensor_tensor(out=B1[:, PAD:], in0=B2[:, PAD:], in1=sh(B2, 4),
                                                                op=ALU.add), reads=["B2"], writes=["B1"])
                        P.add("pool", lambda e: e.tensor_tensor(out=B2[64:128, PAD:], in0=B1[64:128, PAD:],
                                                                in1=sh(B1, 8)[64:128], op=ALU.add),
                              reads=["B1"], writes=["B2"])
                    for (p0, p1, src, sname) in ((0, 64, B1, "B1"), (64, 128, B2, "B2")):
                        P.add("dve", lambda e, p0=p0, p1=p1, src=src, ch=ch: e.scalar_tensor_tensor(
                            out=DT[p0:p1, :], in0=src[p0:p1, PAD:], scalar=INVW[p0:p1, ch:ch + 1],
                            in1=AT[p0:p1, PAD:], op0=ALU.mult, op1=ALU.subtract),
                            reads=[sname, "AT", "INVW"], writes=["DT"])
                        P.add("dve", lambda e, p0=p0, p1=p1, src=src, ch=ch: e.tensor_tensor(
                            out=src[p0:p1, PAD:2 * PAD], in0=src[p0:p1, PAD:2 * PAD],
                            in1=INVC[p0:p1, ch * PAD:(ch + 1) * PAD], op=ALU.mult),
                            reads=["DT", "INVC"], writes=[sname])
                        P.add("dve", lambda e, p0=p0, p1=p1, src=src: e.tensor_tensor(
                            out=DT[p0:p1, 0:PAD], in0=src[p0:p1, PAD:2 * PAD], in1=AT[p0:p1, PAD:2 * PAD],
                            op=ALU.subtract), reads=[sname, "AT"], writes=["DT"])
                    for g4 in range(4):
                        for (p0, p1) in ((0, 64), (64, 128)):
                            bank = next_bank()
                            P.add("pe", lambda e, p0=p0, p1=p1, g4=g4, bank=bank, ch=ch, l=l: e.matmul(
                                PSv[p0:p1, bank, :],
                                lhsT=PW[p0:p1, l * 128 + ch * 64: l * 128 + (ch + 1) * 64],
                                rhs=DT[p0:p1, g4 * 512:(g4 + 1) * 512], start=True, stop=True),
                                reads=["DT", "PW"], writes=[("ps", bank)])
                            P.add("dve", lambda e, p0=p0, p1=p1, g4=g4, bank=bank, ch=ch, l=l:
                                  e.scalar_tensor_tensor(
                                      out=YTv[p0:p1, ch, g4 * 512:(g4 + 1) * 512], in0=PSv[p0:p1, bank, :],
                                      scalar=PSC[p0:p1, l * 2 + ch: l * 2 + ch + 1],
                                      in1=SGA[p0:p1, g4 * 512:(g4 + 1) * 512], op0=ALU.mult, op1=ALU.mult),
                                  reads=[("ps", bank), "PSC", "SGA"], writes=[("YT", ch, g4)])

                VLN = [YTv[:, 5, :].rearrange("p (t f) -> p t f", t=NT),
                       YTv[:, 6, :].rearrange("p (t f) -> p t f", t=NT)]
                UT = cv("UT", A0, S, F32)
                SGB = cv("SGB", A0 + 4 * S, S, F32)
                for gi in range(4):
                    P.set_alias(("UT", gi), "arena", [(A0 + gi * 2048, A0 + (gi + 1) * 2048)])
                    P.set_alias(("SGB", gi), "arena", [(A0 + 4 * S + gi * 2048, A0 + 4 * S + (gi + 1) * 2048)])
                T1 = [cv(("T1", i), A0 + 8 * S + i * 2048, 512, F32) for i in range(2)]
                BSB = cv("BSB", A0 + 8 * S + 8192, 256, F32)
                GCOL = STAT[:, 208:210]
                BCOL = STAT[:, 210:212]
                P.dma("sp", lambda e, l=l: e.dma_start(out=GCOL, in_=dap(sgg_d, l * 256, [[1, 128], [128, 2]]),
                                                      allow_slow_non_contiguous=True), writes=["GCOL"])
                P.dma("sp", lambda e, l=l: e.dma_start(out=BCOL, in_=dap(sgb_d, l * 256, [[1, 128], [128, 2]]),
                                                      allow_slow_non_contiguous=True), writes=["BCOL"])
                for h in range(4):
                    P.dma("sp", lambda e, h=h, l=l: e.dma_start(
                        out=BSB[(h % 2) * 64:(h % 2 + 1) * 64, (h // 2) * 128:(h // 2 + 1) * 128],
                        in_=dap(sbias_d, (l * 4 + h) * 128, [[0, 64], [1, 128]])),
                        writes=["BSB"] if h == 0 else [("BSBx", h)])
                BSB_all = ["BSB"] + [("BSBx", h) for h in range(1, 4)]
                for pr_ in range(2):
                    P.set_alias(("BSB2", pr_), "arena", [(A0 + 8 * S + 8192, A0 + 8 * S + 8192 + 1024)])
                for half in range(2):
                    slot = slots[widx]
                    prefetch(widx + 3)
                    widx += 1
                    for tt in range(NT):
                        bank = tt // 2
                        c0 = (tt % 2) * 256 + half * 128

                        def mm(e, slot=slot, tt=tt, bank=bank, c0=c0):
                            inst = None
                            for kc in range(KC):
                                inst = e.matmul(PSv[:, bank, c0:c0 + 128],
                                                lhsT=HTv[:, kc, tt * 128:(tt + 1) * 128],
                                                rhs=WR[slot][:, kc * 128:(kc + 1) * 128],
                                                start=(kc == 0), stop=(kc == KC - 1))
                            return inst
                        P.add("pe", mm, reads=[("W", slot)] + HT_all, writes=[("ps", bank)])
                bank_ctr[0] = 0
                BNS = STAT[:, 0:96].rearrange("p (t s) -> p t s", t=NT)
                MV = STAT[:, 96:128].rearrange("p (t s) -> p t s", t=NT)
                RSTD = STAT[:, 128:144]
                for tt in range(NT):
                    bank = tt // 2
                    c0 = (tt % 2) * 256
                    P.add("dve", lambda e, tt=tt, bank=bank, c0=c0: e.bn_stats(
                        out=BNS[:, tt, :], in_=PSv[:, bank, c0:c0 + 256]),
                        reads=[("ps", bank)], writes=[("BNS", tt)])
                    P.add("dve", lambda e, tt=tt: e.bn_aggr(out=MV[:, tt, :], in_=BNS[:, tt, :]),
                          reads=[("BNS", tt)], writes=["MV"])
                P.add("act", lambda e: e.activation(out=RSTD, in_=MV[:, :, 1], func=AF.Ln, bias=LN_EPS),
                      reads=["MV"], writes=["RSTD"])
                P.add("act", lambda e: e.activation(out=RSTD, in_=RSTD, func=AF.Exp, scale=-0.5),
                      reads=["RSTD"], writes=["RSTD"])
                for tt in range(NT):
                    bank = tt // 2
                    c0 = (tt % 2) * 256
                    P.add("dve", lambda e, tt=tt, bank=bank, c0=c0: e.tensor_scalar(
                        out=YTv[:, 5:7, tt * 128:(tt + 1) * 128],
                        in0=PSv[:, bank, c0:c0 + 256].rearrange("p (h f) -> p h f", h=2),
                        scalar1=MV[:, tt, 0:1], scalar2=RSTD[:, tt:tt + 1], op0=ALU.subtract, op1=ALU.mult),
                        reads=[("ps", bank), "MV", "RSTD"], writes=[("VLN", tt)])
                SWTl = SWT[:, l * 512:(l + 1) * 512].rearrange("p (h t) -> p h t", h=4)
                for pr in range(2):
                    bank = next_bank()

                    def rwmm(e, bank=bank, pr=pr, SWTl=SWTl):
                        inst = None
                        for hh in range(2):
                            inst = e.matmul(PSv[hh * 64:(hh + 1) * 64, bank, 0:128], lhsT=ONESB[:, 0:64],
                                            rhs=SWTl[:, pr * 2 + hh, :], start=True, stop=True)
                        return inst
                    P.add("pe", rwmm, reads=["SWT", "CB"], writes=[("ps", bank)])
                    P.add("dve", lambda e, bank=bank, pr=pr: e.scalar_tensor_tensor(
                        out=BSB[:, pr * 128:(pr + 1) * 128], in0=PSv[:, bank, 0:128], scalar=BCOL[:, pr:pr + 1],
                        in1=BSB[:, pr * 128:(pr + 1) * 128], op0=ALU.mult, op1=ALU.add),
                        reads=[("ps", bank), "BCOL"] + BSB_all, writes=[("BSB2", pr)])

                    def ev_gb(bank, g4):
                        P.add("act", lambda e, bank=bank, g4=g4: e.activation(
                            out=SGB[:, g4 * 512:(g4 + 1) * 512], in_=PSv[:, bank, :], func=AF.Silu),
                            reads=[("ps", bank)], writes=[("SGB", g4)])
                    proj_fm(widx, ev_gb)
                    widx += 1

                    def ev_u(bank, g4):
                        P.add("dve", lambda e, bank=bank, g4=g4: e.tensor_tensor(
                            out=UT[:, g4 * 512:(g4 + 1) * 512], in0=PSv[:, bank, :],
                            in1=SGB[:, g4 * 512:(g4 + 1) * 512], op=ALU.mult),
                            reads=[("ps", bank), ("SGB", g4)], writes=[("UT", g4)])
                    proj_fm(widx, ev_u)
                    widx += 1
                    for g4 in range(4):
                        bank = next_bank()

                        def mm(e, g4=g4, bank=bank, pr=pr, SWTl=SWTl):
                            inst = None
                            for j in range(4):
                                tt = g4 * 4 + j
                                for hh in range(2):
                                    inst = e.matmul(PSv[hh * 64:(hh + 1) * 64, bank, j * 128:(j + 1) * 128],
                                                    lhsT=VLN[pr][:, tt, hh * 64:(hh + 1) * 64],
                                                    rhs=SWTl[:, pr * 2 + hh, :], start=True, stop=True)
                            return inst
                        P.add("pe", mm, reads=[("VLN", g4 * 4 + j) for j in range(4)] + ["SWT"],
                              writes=[("ps", bank)])
                        t1 = T1[g4 % 2]
                        bs_b = bass.AP(BSB.tensor, BSB.offset + pr * 128,
                                       [[int(v) for v in BSB.ap[0]], [0, 4], [1, 128]])
                        P.add("dve", lambda e, t1=t1, bank=bank, bs_b=bs_b, pr=pr: e.scalar_tensor_tensor(
                            out=t1.rearrange("p (j t) -> p j t", j=4),
                            in0=PSv[:, bank, :].rearrange("p (j t) -> p j t", j=4), scalar=GCOL[:, pr:pr + 1],
                            in1=bs_b, op0=ALU.mult, op1=ALU.add),
                            reads=[("ps", bank), ("BSB2", pr), "GCOL"], writes=[("T1", g4 % 2)])
                        P.add("pool", lambda e, t1=t1, g4=g4, pr=pr: e.tensor_tensor(
                            out=YTv[:, 2 + pr, g4 * 512:(g4 + 1) * 512], in0=t1,
                            in1=UT[:, g4 * 512:(g4 + 1) * 512], op=ALU.mult),
                            reads=[("T1", g4 % 2), ("UT", g4)], writes=[("YT", 2 + pr, g4)])

                QTs = [cv(("QTb", i), A0 + i * 12288, S, BF16) for i in range(2)]
                KTs = [cv(("KTb", i), A0 + i * 12288 + 4096, S, BF16) for i in range(2)]
                VPs = [cv(("VPb", i), A0 + i * 12288 + 8192, S, BF16).rearrange("p (t f) -> p t f", t=NT)
                       for i in range(2)]
                GTs = [cv(("GTb", 0), A0 + 24576, S, BF16), cv(("GTb", 1), A0 + 53248, S, BF16)]
                for gi in range(4):
                    P.set_alias(("GT", 0, gi), "arena", [(A0 + 24576 + gi * 1024, A0 + 24576 + (gi + 1) * 1024)])
                    P.set_alias(("GT", 1, gi), "arena", [(A0 + 53248 + gi * 1024, A0 + 53248 + (gi + 1) * 1024)])
                    for par in range(2):
                        for nm, off in (("QT", 0), ("KT", 4096), ("VP", 8192)):
                            o = A0 + par * 12288 + off + gi * 1024
                            P.set_alias((nm, par, gi), "arena", [(o, o + 1024)])
                EB = [cv(("E", i), A0 + 28672 + i * 4096, 1024, F32).rearrange("p (h n) -> p h n", h=2)
                      for i in range(3)]
                SPB = [cv(("SP", i), A0 + 40960 + i * 2048, 1024, BF16).rearrange("p (h n) -> p h n", h=2)
                       for i in range(2)]
                PB = cv("PB", A0 + 45056, 1024, F32).rearrange("p (h n) -> p h n", h=2)
                AB = [cv(("A", i), A0 + 49152 + i * 2048, 1024, BF16).rearrange("p (h n) -> p h n", h=2)
                      for i in range(2)]
                HT32 = HT.bitcast(F32)
                GBCh = HT32[:, 4096:4096 + D]
                P.set_alias("GBCh", "HT", [(16384, 16384 + 4 * D)])
                WOv = HT[:, 0:KC * D].rearrange("p (k f) -> p k f", k=KC)

                def piece_fm(slot, g4, bank, evac):
                    def run():
                        def mm(e):
                            inst = None
                            for kc in range(KC):
                                inst = e.matmul(PSv[:, bank, :], lhsT=WR[slot][:, kc * 128:(kc + 1) * 128],
                                                rhs=HTv[:, kc, g4 * 512:(g4 + 1) * 512],
                                                start=(kc == 0), stop=(kc == KC - 1))
                            return inst
                        P.add("pe", mm, reads=[("W", slot)] + [("HT", kc, g4) for kc in range(KC)],
                              writes=[("ps", bank)])
                        evac(bank, g4)
                    return run

                def piece_v(slot, g4, j, bank, par):
                    def run():
                        tt = g4 * 4 + j

                        def mm(e):
                            inst = None
                            for kc in range(KC):
                                inst = e.matmul(PSv[:, bank, j * 128:(j + 1) * 128],
                                                lhsT=HTv[:, kc, tt * 128:(tt + 1) * 128],
                                                rhs=WR[slot][:, kc * 128:(kc + 1) * 128],
                                                start=(kc == 0), stop=(kc == KC - 1))
                            return inst
                        P.add("pe", mm, reads=[("W", slot)] + [("HT", kc, g4) for kc in range(KC)],
                              writes=[("ps", bank)])
                        if j == 3:
                            P.add("dve", lambda e: e.tensor_copy(
                                out=VPs[par][:, g4 * 4:g4 * 4 + 4, :],
                                in_=PSv[:, bank, :].rearrange("p (j f) -> p j f", j=4)),
                                reads=[("ps", bank)], writes=[("VP", par, g4)])
                    return run

                def mk_ev_q(par):
                    def ev_q(bank, g4):
                        P.add("dve", lambda e: e.tensor_scalar(
                            out=QTs[par][:, g4 * 512:(g4 + 1) * 512], in0=PSv[:, bank, :], scalar1=0.125,
                            scalar2=None, op0=ALU.mult), reads=[("ps", bank)], writes=[("QT", par, g4)])
                    return ev_q

                def mk_ev_k(par):
                    def ev_k(bank, g4):
                        P.add("dve", lambda e: e.tensor_copy(
                            out=KTs[par][:, g4 * 512:(g4 + 1) * 512], in_=PSv[:, bank, :]),
                            reads=[("ps", bank)], writes=[("KT", par, g4)])
                    return ev_k

                def mk_ev_g(par):
                    def ev_g(bank, g4):
                        P.add("dve", lambda e: e.tensor_copy(
                            out=GTs[par][:, g4 * 512:(g4 + 1) * 512], in_=PSv[:, bank, :]),
                            reads=[("ps", bank)], writes=[("GT", par, g4)])
                    return ev_g

                def silu_gate(par):
                    P.add("act", lambda e: e.activation(out=GTs[par], in_=GTs[par], func=AF.Silu),
                          reads=[("GT", par, gi) for gi in range(4)], writes=[("GT", par, gi) for gi in range(4)])

                xb = [0, 1, 2, 3, 7]
                xctr = [0]

                def xbank():
                    b = xb[xctr[0] % len(xb)]
                    xctr[0] += 1
                    return b

                prefetch(widx + 4)
                for g4 in range(4):
                    piece_fm(slots[widx], g4, xbank(), mk_ev_g(0))()
                for g4 in range(4):
                    piece_fm(slots[widx + 1], g4, xbank(), mk_ev_q(0))()
                    piece_fm(slots[widx + 2], g4, xbank(), mk_ev_k(0))()
                    bv = xbank()
                    for j in range(4):
                        piece_v(slots[widx + 3], g4, j, bv, 0)()
                widx += 4
                steps = []
                for hp in range(4):
                    for c in range(4):
                        for kb in range(4 * c + 3, -1, -1):
                            r = kb - 4 * c
                            steps.append((hp, c, kb, max(r, 0) * 128, r >= 0, kb == 4 * c + 3, kb == 0))
                n = len(steps)
                NS = n // 4
                ZB = [(0, 1), (2, 3)]
                XB = (4, 5)
                OB6 = 6
                pending = []
                pstate = {"npieces": 1, "colacc": 0.0}

                def pair_begin(hp):
                    nonlocal_widx = widx_box[0]
                    par = hp % 2
                    if hp < 3:
                        prefetch(nonlocal_widx + 4)
                        npar = 1 - par
                        for g4 in range(4):
                            pending.append(piece_fm(slots[nonlocal_widx + 3], g4, 7, mk_ev_g(npar)))
                            pending.append(piece_fm(slots[nonlocal_widx], g4, 7, mk_ev_q(npar)))
                            pending.append(piece_fm(slots[nonlocal_widx + 1], g4, 7, mk_ev_k(npar)))
                            for j in range(4):
                                pending.append(piece_v(slots[nonlocal_widx + 2], g4, j, 7, npar))
                        widx_box[0] += 4
                    else:
                        for q in range(4):
                            P.dma("pool", lambda e, q=q, l=l: e.dma_start(
                                out=HT[:, q * 2048:(q + 1) * 2048],
                                in_=dap(wout_d, l * 128 * 8192 + q * 2048, [[8192, 128], [1, 2048]])),
                                writes=[("WOUT", q)])
                        P.dma("sp", lambda e, l=l, si=si: e.dma_start(
                            out=GBCh, in_=dap(modd, (l * 4 + si) * 3072 + 2048, [[0, 128], [1, D]])),
                            reads=[("modd", l, 4), ("modd", l, 5)], writes=["GBCh"])
                    pstate["npieces"] = max(len(pending), 1)
                    pstate["colacc"] = 0.0

                widx_box = [widx]

                def qk(i):
                    hp, c, kb, c0, diag, first, last = steps[i]
                    par = hp % 2
                    zb = ZB[i % 2]
                    QTl, KTl = QTs[par], KTs[par]

                    def f(e):
                        inst = None
                        for h in range(2):
                            inst = e.matmul(PSv[:, zb[h], c0:512],
                                            lhsT=KTl[h * 64:(h + 1) * 64, kb * 128:(kb + 1) * 128],
                                            rhs=QTl[h * 64:(h + 1) * 64, c * 512 + c0:(c + 1) * 512],
                                            start=True, stop=not diag)
                        if diag:
                            for h in range(2):
                                inst = e.matmul(PSv[:, zb[h], c0:c0 + 128], lhsT=IDB, rhs=MNEG,
                                                start=False, stop=True)
                        return inst
                    P.add("pe", f, reads=[("QT", par, c), ("KT", par, kb // 4), "CB"],
                          writes=[("ps", zb[0]), ("ps", zb[1])])

                def exp_ln(i):
                    hp, c, kb, c0, diag, first, last = steps[i]
                    zb = ZB[i % 2]
                    eb, sb_ = EB[i % 3], SPB[i % 2]
                    P.add("act", lambda e: e.activation(out=eb[:, :, c0:512],
                                                        in_=PSv[:, zb[0]:zb[0] + 2, c0:512], func=AF.Exp),
                          reads=[("ps", zb[0]), ("ps", zb[1])], writes=[("E", i % 3)])
                    P.add("act", lambda e: e.activation(out=sb_[:, :, c0:512], in_=eb[:, :, c0:512],
                                                        func=AF.Ln, bias=1.0),
                          reads=[("E", i % 3)], writes=[("SP", i % 2)])

                def umm(i):
                    hp, c, kb, c0, diag, first, last = steps[i]
                    sb_ = SPB[i % 2]

                    def f(e):
                        inst = None
                        for h in range(2):
                            inst = e.matmul(PSv[:, XB[h], c0:512], lhsT=UNEG, rhs=sb_[:, h, c0:512],
                                            start=first, stop=True, skip_group_check=True)
                        return inst
                    P.add("pe", f, reads=[("SP", i % 2), "CB", "ZER"], writes=[("ps", XB[0]), ("ps", XB[1])])

                def expx(i):
                    hp, c, kb, c0, diag, first, last = steps[i]
                    P.add("act", lambda e: e.activation(out=PB[:, :, c0:512],
                                                        in_=PSv[:, XB[0]:XB[0] + 2, c0:512], func=AF.Exp),
                          reads=[("ps", XB[0]), ("ps", XB[1])], writes=["PB"])

                def lmm(i):
                    hp, c, kb, c0, diag, first, last = steps[i]
                    if last:
                        return
                    sb_ = SPB[i % 2]

                    def f(e):
                        inst = None
                        for h in range(2):
                            inst = e.matmul(PSv[:, XB[h], c0:512], lhsT=LNEG, rhs=sb_[:, h, c0:512],
                                            start=False, stop=True, skip_group_check=True)
                        return inst
                    P.add("pe", f, reads=[("SP", i % 2), "CB"], writes=[("ps", XB[0]), ("ps", XB[1])])

                def mul(i):
                    hp, c, kb, c0, diag, first, last = steps[i]
                    eb, ab = EB[i % 3], AB[i % 2]
                    P.add("dve", lambda e: e.tensor_tensor(out=ab[:, :, c0:512], in0=eb[:, :, c0:512],
                                                           in1=PB[:, :, c0:512], op=ALU.mult),
                          reads=[("E", i % 3), "PB"], writes=[("A", i % 2)])

                def av(i):
                    hp, c, kb, c0, diag, first, last = steps[i]
                    par = hp % 2
                    ab = AB[i % 2]
                    ob = OB6
                    VPl, GTl = VPs[par], GTs[par]

                    def f(e):
                        inst = None
                        for h in range(2):
                            inst = e.matmul(PSv[h * 64:(h + 1) * 64, ob, c0:512],
                                            lhsT=VPl[:, kb, h * 64:(h + 1) * 64], rhs=ab[:, h, c0:512],
                                            start=first, stop=True, skip_group_check=True)
                        return inst
                    P.add("pe", f, reads=[("A", i % 2), ("VP", par, kb // 4), "ZER"], writes=[("ps", ob)])
                    if last:
                        P.add("dve", lambda e: e.tensor_tensor(
                            out=YTv[:, 4 + hp, c * 512:(c + 1) * 512], in0=PSv[:, ob, :],
                            in1=GTl[:, c * 512:(c + 1) * 512], op=ALU.mult),
                            reads=[("ps", ob), ("GT", par, c)], writes=[("YT", 4 + hp, c)])

                silu_gate(0)
                pair_begin(0)
                qk(0)
                exp_ln(0)
                umm(0)
                qk(1)
                for i in range(n):
                    hpc = steps[i][0]
                    lstep = i - hpc * NS
                    if lstep == 0 and hpc > 0:
                        assert not pending
                        pair_begin(hpc)
                    if i + 1 < n:
                        exp_ln(i + 1)
                    if i + 2 < n:
                        qk(i + 2)
                    pstate["colacc"] += (512 - steps[i][3])
                    thr = 14500.0 / pstate["npieces"]
                    while pending and pstate["colacc"] >= thr:
                        pstate["colacc"] -= thr
                        pending.pop(0)()
                    if lstep == 34:
                        while pending:
                            pending.pop(0)()
                    if lstep == 35 and hpc < 3:
                        silu_gate((hpc + 1) % 2)
                    expx(i)
                    mul(i)
                    lmm(i)
                    if i + 1 < n:
                        umm(i + 1)
                    if i >= 1:
                        av(i - 1)
                    if hpc == 3 and lstep == 19:
                        P.add("dve", lambda e: e.tensor_scalar(out=GBCh, in0=GBCh, scalar1=float(1.0 / DN_ALPHA),
                                                               scalar2=None, op0=ALU.mult),
                              reads=["GBCh"], writes=["GBCh"])
                    if hpc == 3 and 20 <= lstep < 28:
                        kcw = lstep - 20
                        P.add("dve", lambda e, kcw=kcw: e.tensor_tensor(
                            out=WOv[:, kcw, :], in0=WOv[:, kcw, :], in1=GBCh, op=ALU.mult),
                            reads=["GBCh", ("WOUT", kcw // 2)], writes=[("WOUT", kcw // 2)])
                assert not pending
                av(n - 1)
                widx = widx_box[0]

                bank_ctr[0] = 0
                LNGB = cv("LNGB", A0 + 4096, D, F32)
                LNBB = cv("LNBB", A0 + 8192, D, F32)
                RB = [cv(("RB", i), A0 + 12288 + i * 4096, D, F32) for i in range(3)]
                P.dma("sp", lambda e, l=l: e.dma_start(out=LNGB, in_=dap(lng_d, l * D, [[0, 128], [1, D]])),
                      writes=["LNGB"])
                P.dma("sp", lambda e, l=l: e.dma_start(out=LNBB, in_=dap(lnb_d, l * D, [[0, 128], [1, D]])),
                      writes=["LNBB"])
                prev_fin = [None]
                for tt in range(NT):
                    banks = (next_bank(), next_bank())
                    assert banks[1] == banks[0] + 1
                    for hf in range(2):
                        def mm(e, tt=tt, hf=hf, bank=banks[hf]):
                            inst = None
                            for kc in range(KC):
                                inst = e.matmul(PSv[:, bank, :], lhsT=YTv[:, kc, tt * 128:(tt + 1) * 128],
                                                rhs=WOv[:, kc, hf * 512:(hf + 1) * 512],
                                                start=(kc == 0), stop=(kc == KC - 1))
                            return inst
                        P.add("pe", mm, reads=YT_all + WOUT_all, writes=[("ps", banks[hf])])
                    rb = RB[tt % 3]
                    k2 = tt % 3
                    b0 = banks[0]
                    P.add("dve", lambda e, rb=rb, tt=tt, b0=b0: e.tensor_tensor(
                        out=rb.rearrange("p (h n) -> p h n", h=2), in0=PSv[:, b0:b0 + 2, :],
                        in1=XSv[:, tt, :].rearrange("p (h n) -> p h n", h=2), op=ALU.add),
                        reads=[("ps", banks[0]), ("ps", banks[1]), ("XS", tt)], writes=[("RB", k2)])
                    BNS2 = STAT[:, 214 + k2 * 12: 214 + (k2 + 1) * 12]
                    MV2 = STAT[:, 192 + k2 * 2: 192 + (k2 + 1) * 2]
                    RS2 = STAT[:, 200 + k2: 201 + k2]
                    NB2 = STAT[:, 204 + k2: 205 + k2]
                    for hf in range(2):
                        P.add("dve", lambda e, rb=rb, hf=hf, BNS2=BNS2: e.bn_stats(
                            out=BNS2[:, hf * 6:(hf + 1) * 6], in_=rb[:, hf * 512:(hf + 1) * 512]),
                            reads=[("RB", k2)], writes=[("BNS2", k2)])
                    P.add("dve", lambda e, BNS2=BNS2, MV2=MV2: e.bn_aggr(out=MV2, in_=BNS2),
                          reads=[("BNS2", k2)], writes=[("MV2", k2)])
                    P.add("act", lambda e, RS2=RS2, MV2=MV2: e.activation(out=RS2, in_=MV2[:, 1:2],
                                                                          func=AF.Ln,
                                                                          bias=float(LN_EPS / DN_ALPHA ** 2)),
                          reads=[("MV2", k2)], writes=[("RS2", k2)])
                    P.add("act", lambda e, RS2=RS2: e.activation(out=RS2, in_=RS2, func=AF.Exp, scale=-0.5),
                          reads=[("RS2", k2)], writes=[("RS2", k2)])
                    P.add("dve", lambda e, NB2=NB2, MV2=MV2, RS2=RS2: e.scalar_tensor_tensor(
                        out=NB2, in0=MV2[:, 0:1], scalar=-1.0, in1=RS2,
                        op0=ALU.mult, op1=ALU.mult), reads=[("MV2", k2), ("RS2", k2)], writes=[("NB2", k2)])
                    P.add("act", lambda e, rb=rb, RS2=RS2, NB2=NB2: e.activation(
                        out=rb, in_=rb, func=AF.Identity, scale=RS2, bias=NB2),
                        reads=[("RB", k2), ("RS2", k2), ("NB2", k2)], writes=[("RB", k2)])
                    P.add("pool", lambda e, rb=rb: e.tensor_tensor(out=rb, in0=rb, in1=LNGB, op=ALU.mult),
                          reads=[("RB", k2), "LNGB"], writes=[("RB", k2)])
                    def fin_tile(tt=tt, rb=rb, k2=k2, si=si):
                        P.add("dve", lambda e: e.tensor_tensor(out=XSv[:, tt, :], in0=rb, in1=LNBB, op=ALU.add),
                              reads=[("RB", k2), "LNBB"], writes=[("XS", tt)])
                        if li == nlayers - 1:
                            od = P.dma("sp", lambda e: e.dma_start(
                                out=dap(out_d, (si * S + tt * 128) * D, [[D, 128], [1, D]]), in_=XSv[:, tt, :]),
                                reads=[("XS", tt)])
                            out_dmas.append(od)
                    if prev_fin[0] is not None:
                        prev_fin[0]()
                    prev_fin[0] = fin_tile
                prev_fin[0]()
                prev_fin[0] = None

        fin = P.add("sp", None)
        fin.deps = set(out_dmas)

        with nc.Block() as block:
            emit(nc, P, block, sems, {"sp": ring_sp, "pool": ring_pool})
    return nc


def _consts():
    j = np.arange(128)[:, None]
    s = np.arange(128)[None, :]
    uneg = np.where(j >= s, -1.0, 0.0)
    lneg = np.where(j < s, -1.0, 0.0)
    mneg = np.where(j >= s, -30000.0, 0.0)
    idb = np.eye(128)
    cb = np.concatenate([uneg, lneg, mneg, idb, np.ones((128, 128))], axis=1).astype(np.float32).astype(ml_dtypes.bfloat16)
    invc = np.zeros((128, 2 * PAD), np.float32)
    invw = np.zeros((128, 2), np.float32)
    for ch in range(2):
        for p in range(128):
            win = WINS[2 * ch + p // 64]
            invw[p, ch] = 1.0 / win
            for t in range(PAD):
                invc[p, ch * PAD + t] = 1.0 / min(t + 1, win)
    return cb, invc, invw


def _prep_shared(w_in, pool_w, pool_scale, sgu_ln_g, sgu_ln_b, sgu_w, sgu_b, w_out, ada_w, ada_b, ln_g, ln_b):
    f = lambda a: np.ascontiguousarray(np.asarray(a, dtype=np.float32))
    w_in, w_out, ada_w = f(w_in), f(w_out), f(ada_w)
    cb, invc, invw = _consts()
    sh = {}
    sh["w_in_b"] = np.ascontiguousarray(
        w_in.reshape(DEPTH, KC, 128, 26, 128).transpose(0, 3, 2, 1, 4)).reshape(DEPTH, 26, 128, 1024)
    sh["w_out_b"] = np.ascontiguousarray(
        w_out.reshape(DEPTH, KC, 128, D).transpose(0, 2, 1, 3)).reshape(DEPTH, 128, 8192)
    sh["ada_w_b"] = np.ascontiguousarray(
        ada_w.reshape(DEPTH, KC, 128, 6, 512).transpose(0, 3, 2, 1, 4)).reshape(DEPTH, 6, 128, 4096)
    sh["ada_b"] = f(ada_b)
    pw = f(pool_w)
    sh["pool_w_b"] = np.ascontiguousarray(
        pw.reshape(DEPTH, 2, 2, 64, 64).transpose(0, 2, 3, 1, 4)).reshape(DEPTH, 128, 128)
    sh["pool_sc"] = np.ascontiguousarray(f(pool_scale).reshape(DEPTH, 2, 128).transpose(0, 2, 1))
    sh["sgu_g"] = f(sgu_ln_g)
    sh["sgu_bln"] = f(sgu_ln_b)
    sh["sgu_wT"] = np.ascontiguousarray(f(sgu_w).transpose(0, 3, 1, 2)).reshape(DEPTH, 128, 512)
    sh["sgu_bias"] = f(sgu_b)
    sh["ln_g"] = f(ln_g)
    sh["ln_b"] = f(ln_b)
    sh["ident"] = np.eye(128, dtype=np.float32)
    sh["cst_bf"] = cb
    sh["invc"] = invc
    sh["invw"] = invw
    return sh


FUSED = True


def kernel(x, c, w_in, pool_w, pool_scale, sgu_ln_g, sgu_ln_b, sgu_w, sgu_b,
           w_out, ada_w, ada_b, ln_g, ln_b):
    x = np.asarray(x, dtype=np.float32)
    c = np.asarray(c, dtype=np.float32)
    sh = _prep_shared(w_in, pool_w, pool_scale, sgu_ln_g, sgu_ln_b, sgu_w, sgu_b, w_out, ada_w, ada_b, ln_g, ln_b)
    cTs = []
    for i in range(NCORES):
        cc = c[i * NSEQ:(i + 1) * NSEQ]
        cTs.append(np.ascontiguousarray(cc.reshape(NSEQ, KC, 128).transpose(2, 1, 0)).reshape(128, KC * NSEQ))

    def run(nc, xin):
        in_maps = []
        for i in range(NCORES):
            m = dict(sh)
            m["x"] = np.ascontiguousarray(xin[i * NSEQ:(i + 1) * NSEQ])
            m["cT"] = cTs[i]
            in_maps.append(m)
        res = run_bass_kernel_spmd(nc, in_maps, core_ids=list(range(NCORES)))
        return np.concatenate([np.asarray(r["out"]) for r in res.results], axis=0)

    if FUSED:
        return run(build_nc(DEPTH, 0), x)
    y = x
    for l in range(DEPTH):
        y = run(build_nc(1, l), y)
    return y
```

```python
import contextlib
import numpy as np
import ml_dtypes
import concourse.bass as bass
import concourse.mybir as mybir
from concourse.bass_utils import run_bass_kernel_spmd

F32 = mybir.dt.float32
BF16 = mybir.dt.bfloat16
AF = mybir.ActivationFunctionType
ALU = mybir.AluOpType

NCORES = 8
NSEQ = 4
S = 2048
D = 1024
NT = 16
KC = 8
DEPTH = 2
LN_EPS = 1e-5
DN_ALPHA = (2 * DEPTH) ** 0.25
WINS = (2, 4, 8, 16)
PAD = 16

ARENA_BYTES = 64 * 1024


class Op:
    __slots__ = ("eng", "fn", "deps", "kind", "signal", "seq", "sem", "val", "idx")


class Prog:
    def __init__(self):
        self.ops = []
        self.res_w = {}
        self.res_r = {}
        self.alias = {}
        self.by_space = {}

    def set_alias(self, name, space, ranges):
        if name in self.alias:
            assert self.alias[name] == (space, ranges), name
            return
        self.alias[name] = (space, ranges)
        self.by_space.setdefault(space, []).append(name)

    def _overlaps(self, name):
        a = self.alias.get(name)
        if a is None:
            return ()
        space, rs = a
        out = []
        for m in self.by_space[space]:
            if m == name:
                continue
            for (lo, hi) in self.alias[m][1]:
                if any(lo < h2 and l2 < hi for (l2, h2) in rs):
                    out.append(m)
                    break
        return out

    def _add(self, eng, fn, reads, writes, kind):
        op = Op()
        op.eng, op.fn, op.kind = eng, fn, kind
        op.signal = False
        op.seq = op.sem = op.val = None
        op.idx = len(self.ops)
        deps = set()
        for r in reads:
            w = self.res_w.get(r)
            if w is not None:
                deps.add(w)
        for r in writes:
            w = self.res_w.get(r)
            if w is not None:
                deps.add(w)
            for rd in self.res_r.get(r, ()):
                deps.add(rd)
        for r in reads:
            self.res_r.setdefault(r, []).append(op)
        for r in writes:
            self.res_w[r] = op
            self.res_r[r] = []
        for r in list(reads) + list(writes):
            for m in self._overlaps(r):
                w = self.res_w.get(m)
                if w is not None:
                    deps.add(w)
                for rd in self.res_r.get(m, ()):
                    deps.add(rd)
        deps.discard(op)
        op.deps = deps
        self.ops.append(op)
        return op

    def add(self, eng, fn, reads=(), writes=()):
        return self._add(eng, fn, reads, writes, "c")

    def dma(self, queue, fn, reads=(), writes=()):
        return self._add(queue, fn, reads, writes, "d")

    def finalize(self):
        for op in self.ops:
            nd = set()
            for d in op.deps:
                if d.kind == "c" and op.kind == "c" and d.eng == "pe" and op.eng == "pe":
                    continue
                nd.add(d)
            op.deps = nd
            for d in nd:
                d.signal = True


def emit(nc, prog, block, sems, dma_sems):
    prog.finalize()
    seqc = {e: 0 for e in sems}
    dcount = {q: 0 for q in dma_sems}
    dprev = {}
    for op in prog.ops:
        if op.kind == "c":
            if op.signal:
                seqc[op.eng] += 1
                op.seq = seqc[op.eng]
        else:
            ring = dma_sems[op.eng]
            i = dcount[op.eng]
            dcount[op.eng] += 1
            op.sem = ring[i % len(ring)]
            op.val = 16 * (i // len(ring) + 1)
    lists = {e: [] for e in sems}
    for op in prog.ops:
        lists[op.eng].append(op)

    def body(engname):
        def f(e):
            waited = {}

            def wait(sem, val):
                k = id(sem)
                if waited.get(k, 0) >= val:
                    return
                waited[k] = val
                e.wait_ge(sem, val)

            for op in lists[engname]:
                for d in sorted(op.deps, key=lambda o: o.idx):
                    if d.kind == "c":
                        wait(sems[d.eng], d.seq)
                    else:
                        wait(d.sem, d.val)
                if op.kind == "d":
                    if op.val > 16:
                        wait(op.sem, op.val - 16)
                    inst = op.fn(e)
                    inst.then_inc(op.sem, 16)
                else:
                    if op.fn is None:
                        continue
                    inst = op.fn(e)
                    if op.signal:
                        inst.then_inc(sems[op.eng], 1)
        return f

    block.sync(body("sp"))
    block.gpsimd(body("pool"))
    block.tensor(body("pe"))
    block.vector(body("dve"))
    block.scalar(body("act"))


def build_nc(nlayers=DEPTH, layer0=0, nseq=NSEQ):
    nc = bass.Bass("TRN2", target_bir_lowering=False)
    dr = lambda name, shape, dt=F32: nc.dram_tensor(name, list(shape), dt, kind="ExternalInput")
    x_d = dr("x", [nseq, S, D])
    cT_d = dr("cT", [128, KC * 4])
    win_d = dr("w_in_b", [DEPTH, 26, 128, 1024])
    wout_d = dr("w_out_b", [DEPTH, 128, 8192])
    ada_d = dr("ada_w_b", [DEPTH, 6, 128, 4096])
    adab_d = dr("ada_b", [DEPTH, 3072])
    pw_d = dr("pool_w_b", [DEPTH, 128, 128])
    psc_d = dr("pool_sc", [DEPTH, 128, 2])
    sgg_d = dr("sgu_g", [DEPTH, 256])
    sgb_d = dr("sgu_bln", [DEPTH, 256])
    swt_d = dr("sgu_wT", [DEPTH, 128, 512])
    sbias_d = dr("sgu_bias", [DEPTH, 4, 128])
    lng_d = dr("ln_g", [DEPTH, 1024])
    lnb_d = dr("ln_b", [DEPTH, 1024])
    ident_d = dr("ident", [128, 128])
    cb_d = dr("cst_bf", [128, 640], BF16)
    invc_d = dr("invc", [128, 2 * PAD])
    invw_d = dr("invw", [128, 2])
    out_d = nc.dram_tensor("out", [nseq, S, D], F32, kind="ExternalOutput")
    modd = nc.dram_tensor("modd", [DEPTH, 4, 3072], F32, kind="Internal")

    def dap(t, offset, pat):
        return bass.AP(t, offset, [list(p) for p in pat])

    P = Prog()

    with contextlib.ExitStack() as es:
        sb = lambda name, shape, dt: es.enter_context(nc.sbuf_tensor(name, shape, dt))
        XS = sb("XS", [128, NT * D], F32)
        HT = sb("HT", [128, KC * S], BF16)
        YT = sb("YT", [128, KC * S], BF16)
        ARb = sb("arena", [128, ARENA_BYTES // 2], BF16)
        IDENT = sb("IDENT", [128, 128], F32)
        CB = sb("CB", [128, 640], BF16)
        ZER = sb("ZER", [128, 512], BF16)
        MODT = sb("MODT", [128, DEPTH * 4 * 16], F32)
        PW = sb("PW", [128, DEPTH * 128], BF16)
        PSC = sb("PSC", [128, DEPTH * 2], F32)
        SWT = sb("SWT", [128, DEPTH * 512], BF16)
        INVC = sb("INVC", [128, 2 * PAD], F32)
        INVW = sb("INVW", [128, 2], F32)
        STAT = sb("STAT", [128, 256], F32)
        PS = es.enter_context(nc.psum_tensor("PS", [128, 4096], F32))
        sem = lambda name: es.enter_context(nc.semaphore(name))
        sems = {"pe": sem("s_pe"), "act": sem("s_act"), "dve": sem("s_dve"),
                "pool": sem("s_pool"), "sp": sem("s_sp")}
        ring_sp = [sem(f"dsp{i}") for i in range(20)]
        ring_pool = [sem(f"dpl{i}") for i in range(8)]
        AR32 = ARb.bitcast(F32)

        def cv(name, off, n, dt):
            size = 4 if dt == F32 else 2
            assert off % size == 0 and off + size * n <= ARENA_BYTES, (name, off, n)
            P.set_alias(name, "arena", [(off, off + size * n)])
            if dt == F32:
                return AR32[:, off // 4: off // 4 + n]
            return ARb[:, off // 2: off // 2 + n]

        XSv = XS[:].rearrange("p (t f) -> p t f", t=NT)
        HTv = HT[:].rearrange("p (k t) -> p k t", k=KC)
        YTv = YT[:].rearrange("p (k t) -> p k t", k=KC)
        PSv = PS[:].rearrange("p (b n) -> p b n", b=8)
        UNEG = CB[:, 0:128]
        LNEG = CB[:, 128:256]
        MNEG = CB[:, 256:384]
        IDB = CB[:, 384:512]
        ONESB = CB[:, 512:640]
        for kc in range(KC):
            for g4 in range(4):
                o = (kc * S + g4 * 512) * 2
                P.set_alias(("HT", kc, g4), "HT", [(o, o + 1024)])
                P.set_alias(("YT", kc, g4), "YT", [(o, o + 1024)])
        for q in range(4):
            P.set_alias(("WOUT", q), "HT", [(q * 4096, (q + 1) * 4096)])
        P.set_alias("DT", "YT", [(7 * S * 2, 8 * S * 2)])
        for t in range(NT):
            P.set_alias(("VLN", t), "YT", [((5 + hf) * S * 2 + t * 256, (5 + hf) * S * 2 + (t + 1) * 256)
                                           for hf in range(2)])
        HT_all = [("HT", kc, g4) for kc in range(KC) for g4 in range(4)]
        WOUT_all = [("WOUT", q) for q in range(4)]
        YT_all = [("YT", chn, g4) for chn in range(8) for g4 in range(4)]

        bank_ctr = [0]

        def next_bank():
            b = bank_ctr[0] % 8
            bank_ctr[0] += 1
            return b

        WR = [cv(("W", i), i * 2048, 1024, BF16) for i in range(4)]
        A0 = 4 * 2048
        wring_ctr = [0]

        def load_wblock(l, blk):
            slot = wring_ctr[0] % 4
            wring_ctr[0] += 1
            src = dap(win_d, ((l * 26 + blk) * 128) * 1024, [[1024, 128], [1, 1024]])
            P.dma("pool", lambda e, slot=slot, src=src: e.dma_start(out=WR[slot], in_=src),
                  writes=[("W", slot)])
            return slot

        P.dma("sp", lambda e: e.dma_start(out=IDENT[:], in_=ident_d.ap()), writes=["IDENT"])
        P.dma("sp", lambda e: e.dma_start(out=CB[:], in_=cb_d.ap()), writes=["CB"])
        P.dma("sp", lambda e: e.dma_start(out=INVC[:], in_=invc_d.ap()), writes=["INVC"])
        P.dma("sp", lambda e: e.dma_start(out=INVW[:], in_=invw_d.ap()), writes=["INVW"])
        P.dma("sp", lambda e: e.dma_start(out=PSC[:].rearrange("p (l c) -> p l c", l=DEPTH),
                                          in_=psc_d.ap().rearrange("l p c -> p l c")),
              writes=["PSC"])
        P.dma("pool", lambda e: e.dma_start(out=PW[:].rearrange("p (l c) -> p l c", l=DEPTH),
                                            in_=pw_d.ap().rearrange("l p c -> p l c")),
              writes=["PW"])
        P.dma("pool", lambda e: e.dma_start(out=SWT[:].rearrange("p (l c) -> p l c", l=DEPTH),
                                            in_=swt_d.ap().rearrange("l p c -> p l c")),
              writes=["SWT"])
        P.add("dve", lambda e: e.memset(ZER[:], 0.0), writes=["ZER"])
        SWTv = SWT[:].rearrange("p (a t) -> p a t", t=128)
        P.add("dve", lambda e: e.memset(SWTv[64:128, :, 0:64], 0.0), reads=["SWT"], writes=["SWT"])

        SC = cv("SC", A0, 32, F32)
        ADAB = cv("ADAB", A0 + 128, 3072, F32)
        MR = [cv(("MR", i), A0 + 128 + 12288 + i * 2048, 512, F32) for i in range(2)]
        ADA_OFF = A0 + 128 + 12288 + 4096
        ADA = [cv(("ADA", i), ADA_OFF + i * 8192, 2048, F32) for i in range(4)]

        P.dma("sp", lambda e: e.dma_start(out=SC, in_=cT_d.ap()), writes=["SC"])
        P.add("act", lambda e: e.activation(out=SC, in_=SC, func=AF.Silu), reads=["SC"], writes=["SC"])
        adactr = 0
        for l in range(DEPTH):
            P.dma("sp", lambda e, l=l: e.dma_start(
                out=ADAB[0:4, :], in_=dap(adab_d, l * 3072, [[0, 4], [1, 3072]])),
                writes=["ADAB"])
            for cbk in range(6):
                bank = next_bank()
                for half in range(2):
                    buf = adactr % 4
                    adactr += 1
                    src = dap(ada_d, ((l * 6 + cbk) * 128) * 4096 + half * 2048,
                              [[4096, 128], [1, 2048]])
                    P.dma("sp", lambda e, buf=buf, src=src: e.dma_start(out=ADA[buf], in_=src),
                          writes=[("ADA", buf)])

                    def mm(e, buf=buf, half=half, bank=bank):
                        inst = None
                        for k4 in range(4):
                            kc = half * 4 + k4
                            inst = e.matmul(PSv[0:4, bank, :], lhsT=SC[:, kc * 4:(kc + 1) * 4],
                                            rhs=ADA[buf][:, k4 * 512:(k4 + 1) * 512],
                                            start=(kc == 0), stop=(kc == 7))
                        return inst
                    P.add("pe", mm, reads=[("ADA", buf), "SC"], writes=[("ps", bank)])
                mr = MR[cbk % 2]
                P.add("dve", lambda e, mr=mr, bank=bank, cbk=cbk: e.tensor_tensor(
                    out=mr[0:4, :], in0=PSv[0:4, bank, :], in1=ADAB[0:4, cbk * 512:(cbk + 1) * 512],
                    op=ALU.add), reads=[("ps", bank), "ADAB"], writes=[("MR", cbk % 2)])
                P.dma("sp", lambda e, mr=mr, l=l, cbk=cbk: e.dma_start(
                    out=dap(modd, l * 4 * 3072 + cbk * 512, [[3072, 4], [1, 512]]), in_=mr[0:4, :]),
                    reads=[("MR", cbk % 2)], writes=[("modd", l, cbk)])
        for l in range(DEPTH):
            for b in range(4):
                P.dma("sp", lambda e, l=l, b=b: e.dma_start(
                    out=MODT[:, (l * 4 + b) * 16:(l * 4 + b + 1) * 16],
                    in_=dap(modd, (l * 4 + b) * 3072, [[1, 128], [128, 16]]),
                    allow_slow_non_contiguous=True),
                    reads=[("modd", l, cbk) for cbk in range(4)], writes=[("MODTld", l, b)])
        MODTv = MODT[:].rearrange("p (a j) -> p a j", j=16)
        P.add("dve", lambda e: e.tensor_scalar(out=MODTv[:, :, 8:16], in0=MODTv[:, :, 8:16],
                                               scalar1=1.0, scalar2=None, op0=ALU.add),
              reads=[("MODTld", l, b) for l in range(DEPTH) for b in range(4)], writes=["MODT"])

        out_dmas = []
        evac_ctr = [0]

        for si in range(nseq):
            for tt in range(NT):
                P.dma("sp", lambda e, si=si, tt=tt: e.dma_start(
                    out=XSv[:, tt, :], in_=dap(x_d, (si * S + tt * 128) * D, [[D, 128], [1, D]])),
                    writes=[("XS", tt)])
            for li in range(nlayers):
                l = layer0 + li
                mcol = (l * 4 + si) * 16
                blk_order = [0, 2, 1, 3, 6, 7, 8, 4, 9, 5]
                for hp in range(4):
                    if hp == 0:
                        blk_order += [22 + hp, 10 + hp, 14 + hp, 18 + hp]
                    else:
                        blk_order += [10 + hp, 14 + hp, 18 + hp, 22 + hp]
                slots = {}
                nload = [0]

                def prefetch(upto, l=l, slots=slots, nload=nload, blk_order=blk_order):
                    while nload[0] < min(upto, len(blk_order)):
                        slots[nload[0]] = load_wblock(l, blk_order[nload[0]])
                        nload[0] += 1
                prefetch(2)

                for g4 in range(4):
                    for kc in range(KC):
                        bank = next_bank()

                        def tr(e, g4=g4, kc=kc, bank=bank):
                            inst = None
                            for j in range(4):
                                inst = e.transpose(out=PSv[:, bank, j * 128:(j + 1) * 128],
                                                   in_=XSv[:, g4 * 4 + j, kc * 128:(kc + 1) * 128],
                                                   identity=IDENT[:])
                            return inst
                        P.add("pe", tr, reads=[("XS", g4 * 4 + j) for j in range(4)] + ["IDENT"],
                              writes=[("ps", bank)])
                        dst = HTv[:, kc, g4 * 512:(g4 + 1) * 512]
                        sc_ap = MODT[:, mcol + 8 + kc: mcol + 9 + kc]
                        sh_ap = MODT[:, mcol + kc: mcol + kc + 1]
                        if evac_ctr[0] % 2 == 0:
                            P.add("dve", lambda e, dst=dst, bank=bank, sc_ap=sc_ap, sh_ap=sh_ap:
                                  e.tensor_scalar(out=dst, in0=PSv[:, bank, :], scalar1=sc_ap,
                                                  scalar2=sh_ap, op0=ALU.mult, op1=ALU.add),
                                  reads=[("ps", bank), "MODT"], writes=[("HT", kc, g4)])
                        else:
                            P.add("act", lambda e, dst=dst, bank=bank, sc_ap=sc_ap, sh_ap=sh_ap:
                                  e.activation(out=dst, in_=PSv[:, bank, :], func=AF.Identity,
                                               scale=sc_ap, bias=sh_ap),
                                  reads=[("ps", bank), "MODT"], writes=[("HT", kc, g4)])
                        evac_ctr[0] += 1

                def proj_fm(widx, evac, slots=slots, prefetch=prefetch):
                    slot = slots[widx]
                    prefetch(widx + 3)
                    for g4 in range(4):
                        bank = next_bank()

                        def mm(e, slot=slot, g4=g4, bank=bank):
                            inst = None
                            for kc in range(KC):
                                inst = e.matmul(PSv[:, bank, :], lhsT=WR[slot][:, kc * 128:(kc + 1) * 128],
                                                rhs=HTv[:, kc, g4 * 512:(g4 + 1) * 512],
                                                start=(kc == 0), stop=(kc == KC - 1))
                            return inst
                        P.add("pe", mm, reads=[("W", slot)] + [("HT", kc, g4) for kc in range(KC)],
                              writes=[("ps", bank)])
                        evac(bank, g4)

                def proj_tm(widx, evac, group=4, slots=slots, prefetch=prefetch):
                    slot = slots[widx]
                    prefetch(widx + 3)
                    for t0 in range(0, NT, group):
                        bank = next_bank()

                        def mm(e, slot=slot, t0=t0, bank=bank):
                            inst = None
                            for j in range(group):
                                tt = t0 + j
                                for kc in range(KC):
                                    inst = e.matmul(PSv[:, bank, j * 128:(j + 1) * 128],
                                                    lhsT=HTv[:, kc, tt * 128:(tt + 1) * 128],
                                                    rhs=WR[slot][:, kc * 128:(kc + 1) * 128],
                                                    start=(kc == 0), stop=(kc == KC - 1))
                            return inst
                        P.add("pe", mm, reads=[("W", slot)] + HT_all, writes=[("ps", bank)])
                        evac(bank, t0, group)

                W2 = S + PAD
                ATs = [cv(("ATb", i), A0 + i * 4 * W2, W2, F32) for i in range(2)]
                B1 = cv("B1", A0 + 8 * W2, W2, F32)
                B2 = cv("B2", A0 + 12 * W2, W2, F32)
                SGAs = [cv(("SGAb", i), A0 + 16 * W2 + i * 4 * S, S, F32) for i in range(2)]
                DT = YTv[:, 7, :]
                for bufname, buf in ((("ATb", 0), ATs[0]), (("ATb", 1), ATs[1]), ("B1", B1), ("B2", B2)):
                    P.add("pool", lambda e, buf=buf: e.memset(buf[:, 0:PAD], 0.0), writes=[bufname])
                widx = 0
                sh = lambda buf, k: buf[:, PAD - k: PAD - k + S]
                for ch in range(2):
                    def ev_a(bank, g4, ch=ch):
                        P.add("act", lambda e, bank=bank, g4=g4: e.activation(
                            out=ATs[ch][:, PAD + g4 * 512: PAD + (g4 + 1) * 512], in_=PSv[:, bank, :],
                            func=AF.Identity), reads=[("ps", bank)], writes=[("ATb", ch)])
                    proj_fm(widx, ev_a)
                    widx += 1

                    def ev_g(bank, g4, ch=ch):
                        P.add("act", lambda e, bank=bank, g4=g4: e.activation(
                            out=SGAs[ch][:, g4 * 512:(g4 + 1) * 512], in_=PSv[:, bank, :],
                            func=AF.Silu), reads=[("ps", bank)], writes=[("SGAb", ch)])
                    proj_fm(widx, ev_g)
                    widx += 1
                for ch in range(2):
                    AT = ATs[ch]
                    SGA = SGAs[ch]
                    atn = ("ATb", ch)
                    sgn = ("SGAb", ch)
                    P.add("pool", lambda e, AT=AT: e.tensor_tensor(out=B1[:, PAD:], in0=AT[:, PAD:], in1=sh(AT, 1),
                                                                  op=ALU.add), reads=[atn], writes=["B1"])
                    if ch == 0:
                        P.add("pool", lambda e: e.tensor_tensor(out=B2[64:128, PAD:], in0=B1[64:128, PAD:],
                                                                in1=sh(B1, 2)[64:128], op=ALU.add),
                              reads=["B1"], writes=["B2"])
                    else:
                        P.add("pool", lambda e: e.tensor_tensor(out=B2[:, PAD:], in0=B1[:, PAD:], in1=sh(B1, 2),
                                                                op=ALU.add), reads=["B1"], writes=["B2"])
                        P.add("pool", lambda e: e.tensor_tensor(out=B1[:, PAD:], in0=B2[:, PAD:], in1=sh(B2, 4),
                                                                op=ALU.add), reads=["B2"], writes=["B1"])
                        P.add("pool", lambda e: e.tensor_tensor(out=B2[64:128, PAD:], in0=B1[64:128, PAD:],
                                                                in1=sh(B1, 8)[64:128], op=ALU.add),
                              reads=["B1"], writes=["B2"])
                    for (p0, p1, src_, sname) in ((0, 64, B1, "B1"), (64, 128, B2, "B2")):
                        P.add("dve", lambda e, p0=p0, p1=p1, src_=src_, ch=ch, AT=AT: e.scalar_tensor_tensor(
                            out=DT[p0:p1, :], in0=src_[p0:p1, PAD:], scalar=INVW[p0:p1, ch:ch + 1],
                            in1=AT[p0:p1, PAD:], op0=ALU.mult, op1=ALU.subtract),
                            reads=[sname, atn, "INVW"], writes=["DT"])
                        P.add("dve", lambda e, p0=p0, p1=p1, src_=src_, ch=ch: e.tensor_tensor(
                            out=src_[p0:p1, PAD:2 * PAD], in0=src_[p0:p1, PAD:2 * PAD],
                            in1=INVC[p0:p1, ch * PAD:(ch + 1) * PAD], op=ALU.mult),
                            reads=["DT", "INVC"], writes=[sname])
                        P.add("dve", lambda e, p0=p0, p1=p1, src_=src_, AT=AT: e.tensor_tensor(
                            out=DT[p0:p1, 0:PAD], in0=src_[p0:p1, PAD:2 * PAD], in1=AT[p0:p1, PAD:2 * PAD],
                            op=ALU.subtract), reads=[sname, atn], writes=["DT"])
                    for g4 in range(4):
                        for (p0, p1) in ((0, 64), (64, 128)):
                            bank = next_bank()
                            P.add("pe", lambda e, p0=p0, p1=p1, g4=g4, bank=bank, ch=ch, l=l: e.matmul(
                                PSv[p0:p1, bank, :],
                                lhsT=PW[p0:p1, l * 128 + ch * 64: l * 128 + (ch + 1) * 64],
                                rhs=DT[p0:p1, g4 * 512:(g4 + 1) * 512], start=True, stop=True),
                                reads=["DT", "PW"], writes=[("ps", bank)])
                            P.add("dve", lambda e, p0=p0, p1=p1, g4=g4, bank=bank, ch=ch, l=l, SGA=SGA:
                                  e.scalar_tensor_tensor(
                                      out=YTv[p0:p1, ch, g4 * 512:(g4 + 1) * 512], in0=PSv[p0:p1, bank, :],
                                      scalar=PSC[p0:p1, l * 2 + ch: l * 2 + ch + 1],
                                      in1=SGA[p0:p1, g4 * 512:(g4 + 1) * 512], op0=ALU.mult, op1=ALU.mult),
                                  reads=[("ps", bank), "PSC", sgn], writes=[("YT", ch, g4)])

                VLN = [YTv[:, 5, :].rearrange("p (t f) -> p t f", t=NT),
                       YTv[:, 6, :].rearrange("p (t f) -> p t f", t=NT)]
                UT = cv("UT", A0, S, F32)
                SGB = cv("SGB", A0 + 4 * S, S, F32)
                for gi in range(4):
                    P.set_alias(("UT", gi), "arena", [(A0 + gi * 2048, A0 + (gi + 1) * 2048)])
                    P.set_alias(("SGB", gi), "arena", [(A0 + 4 * S + gi * 2048, A0 + 4 * S + (gi + 1) * 2048)])
                T1 = [cv(("T1", i), A0 + 8 * S + i * 2048, 512, F32) for i in range(2)]
                BSB = cv("BSB", A0 + 8 * S + 8192, 256, F32)
                GCOL = STAT[:, 208:210]
                BCOL = STAT[:, 210:212]
                P.dma("sp", lambda e, l=l: e.dma_start(out=GCOL, in_=dap(sgg_d, l * 256, [[1, 128], [128, 2]]),
                                                      allow_slow_non_contiguous=True), writes=["GCOL"])
                P.dma("sp", lambda e, l=l: e.dma_start(out=BCOL, in_=dap(sgb_d, l * 256, [[1, 128], [128, 2]]),
                                                      allow_slow_non_contiguous=True), writes=["BCOL"])
                for h in range(4):
                    P.dma("sp", lambda e, h=h, l=l: e.dma_start(
                        out=BSB[(h % 2) * 64:(h % 2 + 1) * 64, (h // 2) * 128:(h // 2 + 1) * 128],
                        in_=dap(sbias_d, (l * 4 + h) * 128, [[0, 64], [1, 128]])),
                        writes=["BSB"] if h == 0 else [("BSBx", h)])
                BSB_all = ["BSB"] + [("BSBx", h) for h in range(1, 4)]
                for pr_ in range(2):
                    P.set_alias(("BSB2", pr_), "arena", [(A0 + 8 * S + 8192, A0 + 8 * S + 8192 + 1024)])
                for half in range(2):
                    slot = slots[widx]
                    prefetch(widx + 3)
                    widx += 1
                    for tt in range(NT):
                        bank = tt // 2
                        c0 = (tt % 2) * 256 + half * 128

                        def mm(e, slot=slot, tt=tt, bank=bank, c0=c0):
                            inst = None
                            for kc in range(KC):
                                inst = e.matmul(PSv[:, bank, c0:c0 + 128],
                                                lhsT=HTv[:, kc, tt * 128:(tt + 1) * 128],
                                                rhs=WR[slot][:, kc * 128:(kc + 1) * 128],
                                                start=(kc == 0), stop=(kc == KC - 1))
                            return inst
                        P.add("pe", mm, reads=[("W", slot)] + HT_all, writes=[("ps", bank)])
                bank_ctr[0] = 0
                BNS = STAT[:, 0:96].rearrange("p (t s) -> p t s", t=NT)
                MV = STAT[:, 96:128].rearrange("p (t s) -> p t s", t=NT)
                RSTD = STAT[:, 128:144]
                for tt in range(NT):
                    bank = tt // 2
                    c0 = (tt % 2) * 256
                    P.add("dve", lambda e, tt=tt, bank=bank, c0=c0: e.bn_stats(
                        out=BNS[:, tt, :], in_=PSv[:, bank, c0:c0 + 256]),
                        reads=[("ps", bank)], writes=[("BNS", tt)])
                    P.add("dve", lambda e, tt=tt: e.bn_aggr(out=MV[:, tt, :], in_=BNS[:, tt, :]),
                          reads=[("BNS", tt)], writes=["MV"])
                P.add("act", lambda e: e.activation(out=RSTD, in_=MV[:, :, 1], func=AF.Ln, bias=LN_EPS),
                      reads=["MV"], writes=["RSTD"])
                P.add("act", lambda e: e.activation(out=RSTD, in_=RSTD, func=AF.Exp, scale=-0.5),
                      reads=["RSTD"], writes=["RSTD"])
                for tt in range(NT):
                    bank = tt // 2
                    c0 = (tt % 2) * 256
                    P.add("dve", lambda e, tt=tt, bank=bank, c0=c0: e.tensor_scalar(
                        out=YTv[:, 5:7, tt * 128:(tt + 1) * 128],
                        in0=PSv[:, bank, c0:c0 + 256].rearrange("p (h f) -> p h f", h=2),
                        scalar1=MV[:, tt, 0:1], scalar2=RSTD[:, tt:tt + 1], op0=ALU.subtract, op1=ALU.mult),
                        reads=[("ps", bank), "MV", "RSTD"], writes=[("VLN", tt)])
                SWTl = SWT[:, l * 512:(l + 1) * 512].rearrange("p (h t) -> p h t", h=4)
                for pr in range(2):
                    bank = next_bank()

                    def rwmm(e, bank=bank, pr=pr, SWTl=SWTl):
                        inst = None
                        for hh in range(2):
                            inst = e.matmul(PSv[hh * 64:(hh + 1) * 64, bank, 0:128], lhsT=ONESB[:, 0:64],
                                            rhs=SWTl[:, pr * 2 + hh, :], start=True, stop=True)
                        return inst
                    P.add("pe", rwmm, reads=["SWT", "CB"], writes=[("ps", bank)])
                    P.add("dve", lambda e, bank=bank, pr=pr: e.scalar_tensor_tensor(
                        out=BSB[:, pr * 128:(pr + 1) * 128], in0=PSv[:, bank, 0:128], scalar=BCOL[:, pr:pr + 1],
                        in1=BSB[:, pr * 128:(pr + 1) * 128], op0=ALU.mult, op1=ALU.add),
                        reads=[("ps", bank), "BCOL"] + BSB_all, writes=[("BSB2", pr)])

                    def ev_gb(bank, g4):
                        P.add("act", lambda e, bank=bank, g4=g4: e.activation(
                            out=SGB[:, g4 * 512:(g4 + 1) * 512], in_=PSv[:, bank, :], func=AF.Silu),
                            reads=[("ps", bank)], writes=[("SGB", g4)])
                    proj_fm(widx, ev_gb)
                    widx += 1

                    def ev_u(bank, g4):
                        P.add("dve", lambda e, bank=bank, g4=g4: e.tensor_tensor(
                            out=UT[:, g4 * 512:(g4 + 1) * 512], in0=PSv[:, bank, :],
                            in1=SGB[:, g4 * 512:(g4 + 1) * 512], op=ALU.mult),
                            reads=[("ps", bank), ("SGB", g4)], writes=[("UT", g4)])
                    proj_fm(widx, ev_u)
                    widx += 1
                    for g4 in range(4):
                        bank = next_bank()

                        def mm(e, g4=g4, bank=bank, pr=pr, SWTl=SWTl):
                            inst = None
                            for j in range(4):
                                tt = g4 * 4 + j
                                for hh in range(2):
                                    inst = e.matmul(PSv[hh * 64:(hh + 1) * 64, bank, j * 128:(j + 1) * 128],
                                                    lhsT=VLN[pr][:, tt, hh * 64:(hh + 1) * 64],
                                                    rhs=SWTl[:, pr * 2 + hh, :], start=True, stop=True)
                            return inst
                        P.add("pe", mm, reads=[("VLN", g4 * 4 + j) for j in range(4)] + ["SWT"],
                              writes=[("ps", bank)])
                        t1 = T1[g4 % 2]
                        bs_b = bass.AP(BSB.tensor, BSB.offset + pr * 128,
                                       [[int(v) for v in BSB.ap[0]], [0, 4], [1, 128]])
                        P.add("dve", lambda e, t1=t1, bank=bank, bs_b=bs_b, pr=pr: e.scalar_tensor_tensor(
                            out=t1.rearrange("p (j t) -> p j t", j=4),
                            in0=PSv[:, bank, :].rearrange("p (j t) -> p j t", j=4), scalar=GCOL[:, pr:pr + 1],
                            in1=bs_b, op0=ALU.mult, op1=ALU.add),
                            reads=[("ps", bank), ("BSB2", pr), "GCOL"], writes=[("T1", g4 % 2)])
                        P.add("pool", lambda e, t1=t1, g4=g4, pr=pr: e.tensor_tensor(
                            out=YTv[:, 2 + pr, g4 * 512:(g4 + 1) * 512], in0=t1,
                            in1=UT[:, g4 * 512:(g4 + 1) * 512], op=ALU.mult),
                            reads=[("T1", g4 % 2), ("UT", g4)], writes=[("YT", 2 + pr, g4)])

                QTs = [cv(("QTb", i), A0 + i * 12288, S, BF16) for i in range(2)]
                KTs = [cv(("KTb", i), A0 + i * 12288 + 4096, S, BF16) for i in range(2)]
                VPs = [cv(("VPb", i), A0 + i * 12288 + 8192, S, BF16).rearrange("p (t f) -> p t f", t=NT)
                       for i in range(2)]
                GTs = [cv(("GTb", 0), A0 + 24576, S, BF16), cv(("GTb", 1), A0 + 53248, S, BF16)]
                for gi in range(4):
                    P.set_alias(("GT", 0, gi), "arena", [(A0 + 24576 + gi * 1024, A0 + 24576 + (gi + 1) * 1024)])
                    P.set_alias(("GT", 1, gi), "arena", [(A0 + 53248 + gi * 1024, A0 + 53248 + (gi + 1) * 1024)])
                    for par in range(2):
                        for nm, off in (("QT", 0), ("KT", 4096), ("VP", 8192)):
                            o = A0 + par * 12288 + off + gi * 1024
                            P.set_alias((nm, par, gi), "arena", [(o, o + 1024)])
                EB = [cv(("E", i), A0 + 28672 + i * 4096, 1024, F32).rearrange("p (h n) -> p h n", h=2)
                      for i in range(3)]
                SPB = [cv(("SP", i), A0 + 40960 + i * 2048, 1024, BF16).rearrange("p (h n) -> p h n", h=2)
                       for i in range(2)]
                PB = cv("PB", A0 + 45056, 1024, F32).rearrange("p (h n) -> p h n", h=2)
                AB = [cv(("A", i), A0 + 49152 + i * 2048, 1024, BF16).rearrange("p (h n) -> p h n", h=2)
                      for i in range(2)]
                HT32 = HT.bitcast(F32)
                GBCh = HT32[:, 4096:4096 + D]
                P.set_alias("GBCh", "HT", [(16384, 16384 + 4 * D)])
                WOv = HT[:, 0:KC * D].rearrange("p (k f) -> p k f", k=KC)

                def piece_fm(slot, g4, bank, evac):
                    def run():
                        def mm(e):
                            inst = None
                            for kc in range(KC):
                                inst = e.matmul(PSv[:, bank, :], lhsT=WR[slot][:, kc * 128:(kc + 1) * 128],
                                                rhs=HTv[:, kc, g4 * 512:(g4 + 1) * 512],
                                                start=(kc == 0), stop=(kc == KC - 1))
                            return inst
                        P.add("pe", mm, reads=[("W", slot)] + [("HT", kc, g4) for kc in range(KC)],
                              writes=[("ps", bank)])
                        evac(bank, g4)
                    return run

                def piece_v(slot, g4, j, bank, par):
                    def run():
                        tt = g4 * 4 + j

                        def mm(e):
                            inst = None
                            for kc in range(KC):
                                inst = e.matmul(PSv[:, bank, j * 128:(j + 1) * 128],
                                                lhsT=HTv[:, kc, tt * 128:(tt + 1) * 128],
                                                rhs=WR[slot][:, kc * 128:(kc + 1) * 128],
                                                start=(kc == 0), stop=(kc == KC - 1))
                            return inst
                        P.add("pe", mm, reads=[("W", slot)] + [("HT", kc, g4) for kc in range(KC)],
                              writes=[("ps", bank)])
                        if j == 3:
                            P.add("dve", lambda e: e.tensor_copy(
                                out=VPs[par][:, g4 * 4:g4 * 4 + 4, :],
                                in_=PSv[:, bank, :].rearrange("p (j f) -> p j f", j=4)),
                                reads=[("ps", bank)], writes=[("VP", par, g4)])
                    return run

                def mk_ev_q(par):
                    def ev_q(bank, g4):
                        P.add("dve", lambda e: e.tensor_scalar(
                            out=QTs[par][:, g4 * 512:(g4 + 1) * 512], in0=PSv[:, bank, :], scalar1=0.125,
                            scalar2=None, op0=ALU.mult), reads=[("ps", bank)], writes=[("QT", par, g4)])
                    return ev_q

                def mk_ev_k(par):
                    def ev_k(bank, g4):
                        P.add("dve", lambda e: e.tensor_copy(
                            out=KTs[par][:, g4 * 512:(g4 + 1) * 512], in_=PSv[:, bank, :]),
                            reads=[("ps", bank)], writes=[("KT", par, g4)])
                    return ev_k

                def mk_ev_g(par):
                    def ev_g(bank, g4):
                        P.add("dve", lambda e: e.tensor_copy(
                            out=GTs[par][:, g4 * 512:(g4 + 1) * 512], in_=PSv[:, bank, :]),
                            reads=[("ps", bank)], writes=[("GT", par, g4)])
                    return ev_g

                def silu_gate(par):
                    P.add("act", lambda e: e.activation(out=GTs[par], in_=GTs[par], func=AF.Silu),
                          reads=[("GT", par, gi) for gi in range(4)], writes=[("GT", par, gi) for gi in range(4)])

                xb = [0, 1, 2, 3, 7]
                xctr = [0]

                def xbank():
                    b = xb[xctr[0] % len(xb)]
                    xctr[0] += 1
                    return b

                prefetch(widx + 4)
                for g4 in range(4):
                    piece_fm(slots[widx], g4, xbank(), mk_ev_g(0))()
                for g4 in range(4):
                    piece_fm(slots[widx + 1], g4, xbank(), mk_ev_q(0))()
                    piece_fm(slots[widx + 2], g4, xbank(), mk_ev_k(0))()
                    bv = xbank()
                    for j in range(4):
                        piece_v(slots[widx + 3], g4, j, bv, 0)()
                widx += 4
                steps = []
                for hp in range(4):
                    for c in range(4):
                        for kb in range(4 * c + 3, -1, -1):
                            r = kb - 4 * c
                            steps.append((hp, c, kb, max(r, 0) * 128, r >= 0, kb == 4 * c + 3, kb == 0))
                n = len(steps)
                NS = n // 4
                ZB = [(0, 1), (2, 3)]
                XB = (4, 5)
                OB6 = 6
                pending = []
                pstate = {"npieces": 1, "colacc": 0.0}

                def pair_begin(hp):
                    nonlocal_widx = widx_box[0]
                    par = hp % 2
                    if hp < 3:
                        prefetch(nonlocal_widx + 4)
                        npar = 1 - par
                        for g4 in range(4):
                            pending.append(piece_fm(slots[nonlocal_widx + 3], g4, 7, mk_ev_g(npar)))
                            pending.append(piece_fm(slots[nonlocal_widx], g4, 7, mk_ev_q(npar)))
                            pending.append(piece_fm(slots[nonlocal_widx + 1], g4, 7, mk_ev_k(npar)))
                            for j in range(4):
                                pending.append(piece_v(slots[nonlocal_widx + 2], g4, j, 7, npar))
                        widx_box[0] += 4
                    else:
                        for q in range(4):
                            P.dma("pool", lambda e, q=q, l=l: e.dma_start(
                                out=HT[:, q * 2048:(q + 1) * 2048],
                                in_=dap(wout_d, l * 128 * 8192 + q * 2048, [[8192, 128], [1, 2048]])),
                                writes=[("WOUT", q)])
                        P.dma("sp", lambda e, l=l, si=si: e.dma_start(
                            out=GBCh, in_=dap(modd, (l * 4 + si) * 3072 + 2048, [[0, 128], [1, D]])),
                            reads=[("modd", l, 4), ("modd", l, 5)], writes=["GBCh"])
                    pstate["npieces"] = max(len(pending), 1)
                    pstate["colacc"] = 0.0

                widx_box = [widx]

                def qk(i):
                    hp, c, kb, c0, diag, first, last = steps[i]
                    par = hp % 2
                    zb = ZB[i % 2]
                    QTl, KTl = QTs[par], KTs[par]

                    def f(e):
                        inst = None
                        for h in range(2):
                            inst = e.matmul(PSv[:, zb[h], c0:512],
                                            lhsT=KTl[h * 64:(h + 1) * 64, kb * 128:(kb + 1) * 128],
                                            rhs=QTl[h * 64:(h + 1) * 64, c * 512 + c0:(c + 1) * 512],
                                            start=True, stop=not diag)
                        if diag:
                            for h in range(2):
                                inst = e.matmul(PSv[:, zb[h], c0:c0 + 128], lhsT=IDB, rhs=MNEG,
                                                start=False, stop=True)
                        return inst
                    P.add("pe", f, reads=[("QT", par, c), ("KT", par, kb // 4), "CB"],
                          writes=[("ps", zb[0]), ("ps", zb[1])])

                def exp_ln(i):
                    hp, c, kb, c0, diag, first, last = steps[i]
                    zb = ZB[i % 2]
                    eb, sb_ = EB[i % 3], SPB[i % 2]
                    P.add("act", lambda e: e.activation(out=eb[:, :, c0:512],
                                                        in_=PSv[:, zb[0]:zb[0] + 2, c0:512], func=AF.Exp),
                          reads=[("ps", zb[0]), ("ps", zb[1])], writes=[("E", i % 3)])
                    P.add("act", lambda e: e.activation(out=sb_[:, :, c0:512], in_=eb[:, :, c0:512],
                                                        func=AF.Ln, bias=1.0),
                          reads=[("E", i % 3)], writes=[("SP", i % 2)])

                def umm(i):
                    hp, c, kb, c0, diag, first, last = steps[i]
                    sb_ = SPB[i % 2]

                    def f(e):
                        inst = None
                        for h in range(2):
                            inst = e.matmul(PSv[:, XB[h], c0:512], lhsT=UNEG, rhs=sb_[:, h, c0:512],
                                            start=first, stop=True, skip_group_check=True)
                        return inst
                    P.add("pe", f, reads=[("SP", i % 2), "CB", "ZER"], writes=[("ps", XB[0]), ("ps", XB[1])])

                def expx(i):
                    hp, c, kb, c0, diag, first, last = steps[i]
                    P.add("act", lambda e: e.activation(out=PB[:, :, c0:512],
                                                        in_=PSv[:, XB[0]:XB[0] + 2, c0:512], func=AF.Exp),
                          reads=[("ps", XB[0]), ("ps", XB[1])], writes=["PB"])

                def lmm(i):
                    hp, c, kb, c0, diag, first, last = steps[i]
                    if last:
                        return
                    sb_ = SPB[i % 2]

                    def f(e):
                        inst = None
                        for h in range(2):
                            inst = e.matmul(PSv[:, XB[h], c0:512], lhsT=LNEG, rhs=sb_[:, h, c0:512],
                                            start=False, stop=True, skip_group_check=True)
                        return inst
                    P.add("pe", f, reads=[("SP", i % 2), "CB"], writes=[("ps", XB[0]), ("ps", XB[1])])

                def mul(i):
                    hp, c, kb, c0, diag, first, last = steps[i]
                    eb, ab = EB[i % 3], AB[i % 2]
                    P.add("dve", lambda e: e.tensor_tensor(out=ab[:, :, c0:512], in0=eb[:, :, c0:512],
                                                           in1=PB[:, :, c0:512], op=ALU.mult),
                          reads=[("E", i % 3), "PB"], writes=[("A", i % 2)])

                def av(i):
                    hp, c, kb, c0, diag, first, last = steps[i]
                    par = hp % 2
                    ab = AB[i % 2]
                    ob = OB6
                    VPl, GTl = VPs[par], GTs[par]

                    def f(e):
                        inst = None
                        for h in range(2):
                            inst = e.matmul(PSv[h * 64:(h + 1) * 64, ob, c0:512],
                                            lhsT=VPl[:, kb, h * 64:(h + 1) * 64], rhs=ab[:, h, c0:512],
                                            start=first, stop=True, skip_group_check=True)
                        return inst
                    P.add("pe", f, reads=[("A", i % 2), ("VP", par, kb // 4), "ZER"], writes=[("ps", ob)])
                    if last:
                        P.add("dve", lambda e: e.tensor_tensor(
                            out=YTv[:, 4 + hp, c * 512:(c + 1) * 512], in0=PSv[:, ob, :],
                            in1=GTl[:, c * 512:(c + 1) * 512], op=ALU.mult),
                            reads=[("ps", ob), ("GT", par, c)], writes=[("YT", 4 + hp, c)])

                silu_gate(0)
                pair_begin(0)
                qk(0)
                exp_ln(0)
                umm(0)
                qk(1)
                for i in range(n):
                    hpc = steps[i][0]
                    lstep = i - hpc * NS
                    if lstep == 0 and hpc > 0:
                        assert not pending
                        pair_begin(hpc)
                    if i + 1 < n:
                        exp_ln(i + 1)
                    if i + 2 < n:
                        qk(i + 2)
                    pstate["colacc"] += (512 - steps[i][3])
                    thr = 14500.0 / pstate["npieces"]
                    while pending and pstate["colacc"] >= thr:
                        pstate["colacc"] -= thr
                        pending.pop(0)()
                    if lstep == 34:
                        while pending:
                            pending.pop(0)()
                    if lstep == 35 and hpc < 3:
                        silu_gate((hpc + 1) % 2)
                    expx(i)
                    mul(i)
                    lmm(i)
                    if i + 1 < n:
                        umm(i + 1)
                    if i >= 1:
                        av(i - 1)
                    if hpc == 3 and lstep == 19:
                        P.add("dve", lambda e: e.tensor_scalar(out=GBCh, in0=GBCh, scalar1=float(1.0 / DN_ALPHA),
                                                               scalar2=None, op0=ALU.mult),
                              reads=["GBCh"], writes=["GBCh"])
                    if hpc == 3 and 20 <= lstep < 28:
                        kcw = lstep - 20
                        P.add("dve", lambda e, kcw=kcw: e.tensor_tensor(
                            out=WOv[:, kcw, :], in0=WOv[:, kcw, :], in1=GBCh, op=ALU.mult),
                            reads=["GBCh", ("WOUT", kcw // 2)], writes=[("WOUT", kcw // 2)])
                assert not pending
                av(n - 1)
                widx = widx_box[0]

                bank_ctr[0] = 0
                LNGB = cv("LNGB", A0 + 4096, D, F32)
                LNBB = cv("LNBB", A0 + 8192, D, F32)
                RB = [cv(("RB", i), A0 + 12288 + i * 4096, D, F32) for i in range(3)]
                P.dma("sp", lambda e, l=l: e.dma_start(out=LNGB, in_=dap(lng_d, l * D, [[0, 128], [1, D]])),
                      writes=["LNGB"])
                P.dma("sp", lambda e, l=l: e.dma_start(out=LNBB, in_=dap(lnb_d, l * D, [[0, 128], [1, D]])),
                      writes=["LNBB"])
                prev_fin = [None]
                for tt in range(NT):
                    banks = (next_bank(), next_bank())
                    assert banks[1] == banks[0] + 1
                    for hf in range(2):
                        def mm(e, tt=tt, hf=hf, bank=banks[hf]):
                            inst = None
                            for kc in range(KC):
                                inst = e.matmul(PSv[:, bank, :], lhsT=YTv[:, kc, tt * 128:(tt + 1) * 128],
                                                rhs=WOv[:, kc, hf * 512:(hf + 1) * 512],
                                                start=(kc == 0), stop=(kc == KC - 1))
                            return inst
                        P.add("pe", mm, reads=YT_all + WOUT_all, writes=[("ps", banks[hf])])
                    rb = RB[tt % 3]
                    k2 = tt % 3
                    b0 = banks[0]
                    P.add("dve", lambda e, rb=rb, tt=tt, b0=b0: e.tensor_tensor(
                        out=rb.rearrange("p (h n) -> p h n", h=2), in0=PSv[:, b0:b0 + 2, :],
                        in1=XSv[:, tt, :].rearrange("p (h n) -> p h n", h=2), op=ALU.add),
                        reads=[("ps", banks[0]), ("ps", banks[1]), ("XS", tt)], writes=[("RB", k2)])
                    BNS2 = STAT[:, 214 + k2 * 12: 214 + (k2 + 1) * 12]
                    MV2 = STAT[:, 192 + k2 * 2: 192 + (k2 + 1) * 2]
                    RS2 = STAT[:, 200 + k2: 201 + k2]
                    NB2 = STAT[:, 204 + k2: 205 + k2]
                    for hf in range(2):
                        P.add("dve", lambda e, rb=rb, hf=hf, BNS2=BNS2: e.bn_stats(
                            out=BNS2[:, hf * 6:(hf + 1) * 6], in_=rb[:, hf * 512:(hf + 1) * 512]),
                            reads=[("RB", k2)], writes=[("BNS2", k2)])
                    P.add("dve", lambda e, BNS2=BNS2, MV2=MV2: e.bn_aggr(out=MV2, in_=BNS2),
                          reads=[("BNS2", k2)], writes=[("MV2", k2)])
                    P.add("act", lambda e, RS2=RS2, MV2=MV2: e.activation(out=RS2, in_=MV2[:, 1:2],
                                                                          func=AF.Ln,
                                                                          bias=float(LN_EPS / DN_ALPHA ** 2)),
                          reads=[("MV2", k2)], writes=[("RS2", k2)])
                    P.add("act", lambda e, RS2=RS2: e.activation(out=RS2, in_=RS2, func=AF.Exp, scale=-0.5),
                          reads=[("RS2", k2)], writes=[("RS2", k2)])
                    P.add("dve", lambda e, NB2=NB2, MV2=MV2, RS2=RS2: e.scalar_tensor_tensor(
                        out=NB2, in0=MV2[:, 0:1], scalar=-1.0, in1=RS2,
                        op0=ALU.mult, op1=ALU.mult), reads=[("MV2", k2), ("RS2", k2)], writes=[("NB2", k2)])
                    P.add("act", lambda e, rb=rb, RS2=RS2, NB2=NB2: e.activation(
                        out=rb, in_=rb, func=AF.Identity, scale=RS2, bias=NB2),
                        reads=[("RB", k2), ("RS2", k2), ("NB2", k2)], writes=[("RB", k2)])
                    P.add("pool", lambda e, rb=rb: e.tensor_tensor(out=rb, in0=rb, in1=LNGB, op=ALU.mult),
                          reads=[("RB", k2), "LNGB"], writes=[("RB", k2)])
                    def fin_tile(tt=tt, rb=rb, k2=k2, si=si):
                        P.add("dve", lambda e: e.tensor_tensor(out=XSv[:, tt, :], in0=rb, in1=LNBB, op=ALU.add),
                              reads=[("RB", k2), "LNBB"], writes=[("XS", tt)])
                        if li == nlayers - 1:
                            od = P.dma("sp", lambda e: e.dma_start(
                                out=dap(out_d, (si * S + tt * 128) * D, [[D, 128], [1, D]]), in_=XSv[:, tt, :]),
                                reads=[("XS", tt)])
                            out_dmas.append(od)
                    if prev_fin[0] is not None:
                        prev_fin[0]()
                    prev_fin[0] = fin_tile
                prev_fin[0]()
                prev_fin[0] = None

        fin = P.add("sp", None)
        fin.deps = set(out_dmas)

        with nc.Block() as block:
            emit(nc, P, block, sems, {"sp": ring_sp, "pool": ring_pool})
    return nc


def _consts():
    j = np.arange(128)[:, None]
    s = np.arange(128)[None, :]
    uneg = np.where(j >= s, -1.0, 0.0)
    lneg = np.where(j < s, -1.0, 0.0)
    mneg = np.where(j >= s, -30000.0, 0.0)
    idb = np.eye(128)
    cb = np.concatenate([uneg, lneg, mneg, idb, np.ones((128, 128))], axis=1).astype(np.float32).astype(ml_dtypes.bfloat16)
    invc = np.zeros((128, 2 * PAD), np.float32)
    invw = np.zeros((128, 2), np.float32)
    for ch in range(2):
        for p in range(128):
            win = WINS[2 * ch + p // 64]
            invw[p, ch] = 1.0 / win
            for t in range(PAD):
                invc[p, ch * PAD + t] = 1.0 / min(t + 1, win)
    return cb, invc, invw


def _prep_shared(w_in, pool_w, pool_scale, sgu_ln_g, sgu_ln_b, sgu_w, sgu_b, w_out, ada_w, ada_b, ln_g, ln_b):
    f = lambda a: np.ascontiguousarray(np.asarray(a, dtype=np.float32))
    w_in, w_out, ada_w = f(w_in), f(w_out), f(ada_w)
    cb, invc, invw = _consts()
    sh = {}
    sh["w_in_b"] = np.ascontiguousarray(
        w_in.reshape(DEPTH, KC, 128, 26, 128).transpose(0, 3, 2, 1, 4)).reshape(DEPTH, 26, 128, 1024)
    sh["w_out_b"] = np.ascontiguousarray(
        w_out.reshape(DEPTH, KC, 128, D).transpose(0, 2, 1, 3)).reshape(DEPTH, 128, 8192)
    sh["ada_w_b"] = np.ascontiguousarray(
        ada_w.reshape(DEPTH, KC, 128, 6, 512).transpose(0, 3, 2, 1, 4)).reshape(DEPTH, 6, 128, 4096)
    sh["ada_b"] = f(ada_b)
    pw = f(pool_w)
    sh["pool_w_b"] = np.ascontiguousarray(
        pw.reshape(DEPTH, 2, 2, 64, 64).transpose(0, 2, 3, 1, 4)).reshape(DEPTH, 128, 128)
    sh["pool_sc"] = np.ascontiguousarray(f(pool_scale).reshape(DEPTH, 2, 128).transpose(0, 2, 1))
    sh["sgu_g"] = f(sgu_ln_g)
    sh["sgu_bln"] = f(sgu_ln_b)
    sh["sgu_wT"] = np.ascontiguousarray(f(sgu_w).transpose(0, 3, 1, 2)).reshape(DEPTH, 128, 512)
    sh["sgu_bias"] = f(sgu_b)
    sh["ln_g"] = f(ln_g)
    sh["ln_b"] = f(ln_b)
    sh["ident"] = np.eye(128, dtype=np.float32)
    sh["cst_bf"] = cb
    sh["invc"] = invc
    sh["invw"] = invw
    return sh


FUSED = True


def kernel(x, c, w_in, pool_w, pool_scale, sgu_ln_g, sgu_ln_b, sgu_w, sgu_b,
           w_out, ada_w, ada_b, ln_g, ln_b):
    x = np.asarray(x, dtype=np.float32)
    c = np.asarray(c, dtype=np.float32)
    sh = _prep_shared(w_in, pool_w, pool_scale, sgu_ln_g, sgu_ln_b, sgu_w, sgu_b, w_out, ada_w, ada_b, ln_g, ln_b)
    cTs = []
    for i in range(NCORES):
        cc = c[i * NSEQ:(i + 1) * NSEQ]
        cTs.append(np.ascontiguousarray(cc.reshape(NSEQ, KC, 128).transpose(2, 1, 0)).reshape(128, KC * NSEQ))

    def run(nc, xin):
        in_maps = []
        for i in range(NCORES):
            m = dict(sh)
            m["x"] = np.ascontiguousarray(xin[i * NSEQ:(i + 1) * NSEQ])
            m["cT"] = cTs[i]
            in_maps.append(m)
        res = run_bass_kernel_spmd(nc, in_maps, core_ids=list(range(NCORES)))
        return np.concatenate([np.asarray(r["out"]) for r in res.results], axis=0)

    if FUSED:
        return run(build_nc(DEPTH, 0), x)
    y = x
    for l in range(DEPTH):
        y = run(build_nc(1, l), y)
    return y
```
